# Optimizing a Trainium2 kernel written in Bass

```python
import jax, jax.numpy as jnp
from jax import lax
import numpy as np

D_MODEL = 2048
BATCH = 4
SEQ = 4096
DEPTH = 2

GRID_W = 64
CTX_LEN = 256
N_MIXERS = 2
N_MLA_LAYERS = (DEPTH + 1) // 2
N_NA_LAYERS = DEPTH // 2
MLA_HEADS = 16
MLA_Q_RANK = 512
MLA_KV_RANK = 512
MLA_NOPE = 128
MLA_ROPE = 64
MLA_V = 128
NA_HEADS = 16
NA_HEAD_DIM = D_MODEL // NA_HEADS
NA_WIN_R = 8
NA_WIN_C = 16
D_FF = -(-(8 * D_MODEL) // (3 * 256)) * 256
ROPE_THETA = 10000.0
EPS = 1e-6
Q_BLOCK = 128

kernel_name = "hybrid_mla_natten_dit_block"


def rmsnorm(x, g):
    x32 = x.astype(jnp.float32)
    y = x32 * lax.rsqrt(jnp.mean(x32 * x32, axis=-1, keepdims=True) + EPS)
    return (y * g.astype(jnp.float32)).astype(x.dtype)


def modulate(h, shift, scale):
    return h * (1 + scale) + shift


def swiglu(h, w_gate, w_up, w_down):
    return (jax.nn.silu(h @ w_gate) * (h @ w_up)) @ w_down


def attention(q, k, v, scale):
    s = jnp.einsum('bqhd,bkhd->bhqk', q, k) * scale
    p = jax.nn.softmax(s.astype(jnp.float32), axis=-1).astype(v.dtype)
    return jnp.einsum('bhqk,bkhd->bqhd', p, v)


def blocked_attention(q, k, v, scale):
    B, S, H, Dq = q.shape
    nb = S // Q_BLOCK
    qb = q.reshape(B, nb, Q_BLOCK, H, Dq).transpose(1, 0, 2, 3, 4)
    ob = lax.map(lambda qi: attention(qi, k, v, scale), qb)
    return ob.transpose(1, 0, 2, 3, 4).reshape(B, S, H, v.shape[-1])


def axial_rope_tables(S, dtype):
    t = jnp.arange(S)
    row = (t // GRID_W).astype(jnp.float32)
    col = (t % GRID_W).astype(jnp.float32)
    n_freq = MLA_ROPE // 4
    inv_freq = ROPE_THETA ** (-jnp.arange(n_freq, dtype=jnp.float32) / n_freq)
    ang_r = row[:, None] * inv_freq
    ang_c = col[:, None] * inv_freq
    return (jnp.cos(ang_r)[:, None, :].astype(dtype), jnp.sin(ang_r)[:, None, :].astype(dtype),
            jnp.cos(ang_c)[:, None, :].astype(dtype), jnp.sin(ang_c)[:, None, :].astype(dtype))


def rope_1d(x, cos, sin):
    x1, x2 = jnp.split(x, 2, axis=-1)
    return jnp.concatenate([x1 * cos - x2 * sin, x2 * cos + x1 * sin], axis=-1)


def rope_2d(x, rope):
    cos_r, sin_r, cos_c, sin_c = rope
    xr, xc = jnp.split(x, 2, axis=-1)
    return jnp.concatenate([rope_1d(xr, cos_r, sin_r), rope_1d(xc, cos_c, sin_c)], axis=-1)


def _mla_q(t, w_dq, q_norm, w_uq):
    B, L, _ = t.shape
    cq = rmsnorm(t @ w_dq, q_norm)
    q = (cq @ w_uq).reshape(B, L, MLA_HEADS, MLA_NOPE + MLA_ROPE)
    return q[..., :MLA_NOPE], q[..., MLA_NOPE:]


def _mla_kv(t, w_dkv, kv_norm, w_ukv):
    B, L, _ = t.shape
    kv_a = t @ w_dkv
    c_kv = rmsnorm(kv_a[..., :MLA_KV_RANK], kv_norm)
    k_pe = kv_a[..., MLA_KV_RANK:][:, :, None, :]
    kv = (c_kv @ w_ukv).reshape(B, L, MLA_HEADS, MLA_NOPE + MLA_V)
    return kv[..., :MLA_NOPE], k_pe, kv[..., MLA_NOPE:]


def _assemble(nope, pe):
    pe = jnp.broadcast_to(pe, nope.shape[:-1] + (pe.shape[-1],))
    return jnp.concatenate([nope, pe], axis=-1)


def mla_mixer(h, hc, w_dq, q_norm, w_uq, w_dkv, kv_norm, w_ukv, w_o, rope, with_ctx_out):
    B, S, _ = h.shape
    Lc = hc.shape[1]
    scale = (MLA_NOPE + MLA_ROPE) ** -0.5
    q_nope, q_pe = _mla_q(h, w_dq, q_norm, w_uq)
    k_nope, k_pe, v = _mla_kv(h, w_dkv, kv_norm, w_ukv)
    q = _assemble(q_nope, rope_2d(q_pe, rope))
    k = _assemble(k_nope, rope_2d(k_pe, rope))
    kc_nope, kc_pe, vc = _mla_kv(hc, w_dkv, kv_norm, w_ukv)
    kc = _assemble(kc_nope, kc_pe)
    keys = jnp.concatenate([kc, k], axis=1)
    vals = jnp.concatenate([vc, v], axis=1)
    y = blocked_attention(q, keys, vals, scale).reshape(B, S, MLA_HEADS * MLA_V) @ w_o
    yc = None
    if with_ctx_out:
        qc_nope, qc_pe = _mla_q(hc, w_dq, q_norm, w_uq)
        qc = _assemble(qc_nope, qc_pe)
        yc = attention(qc, kc, vc, scale).reshape(B, Lc, MLA_HEADS * MLA_V) @ w_o
    return y, yc


def na_mixer(h, hc, w_qkv, rel_bias, w_o, with_ctx_out):
    B, S, _ = h.shape
    Lc = hc.shape[1]
    rows = S // GRID_W
    kr = min(NA_WIN_R, rows)
    HD = NA_HEADS * NA_HEAD_DIM
    scale = NA_HEAD_DIM ** -0.5
    qkv = (h @ w_qkv).reshape(B, rows, GRID_W, 3, NA_HEADS, NA_HEAD_DIM)
    q_g, k_g, v_g = qkv[..., 0, :, :], qkv[..., 1, :, :], qkv[..., 2, :, :]
    kvc = (hc @ w_qkv[:, HD:]).reshape(B, Lc, 2, NA_HEADS, NA_HEAD_DIM)
    kc, vc = kvc[:, :, 0], kvc[:, :, 1]
    col = jnp.arange(GRID_W)
    col_start = jnp.clip(col - NA_WIN_C // 2, 0, GRID_W - NA_WIN_C)
    col_idx = col_start[:, None] + jnp.arange(NA_WIN_C)[None, :]
    dc = col_idx - col[:, None] + (NA_WIN_C - 1)
    bias_c = rel_bias[:, :, dc]
    n_loc = kr * NA_WIN_C

    def row_block(r):
        rs = jnp.clip(r - kr // 2, 0, rows - kr)
        q_r = lax.dynamic_index_in_dim(q_g, r, axis=1, keepdims=False)
        k_rows = lax.dynamic_slice_in_dim(k_g, rs, kr, axis=1)
        v_rows = lax.dynamic_slice_in_dim(v_g, rs, kr, axis=1)
        k_win = k_rows[:, :, col_idx]
        v_win = v_rows[:, :, col_idx]
        dr = rs + jnp.arange(kr) - r + (NA_WIN_R - 1)
        bias = bias_c[:, dr].transpose(0, 2, 1, 3)
        s_loc = jnp.einsum('bqhd,biqjhd->bhqij', q_r, k_win) * scale + bias[None]
        s_loc = s_loc.reshape(B, NA_HEADS, GRID_W, n_loc)
        s_ctx = jnp.einsum('bqhd,bkhd->bhqk', q_r, kc) * scale
        p = jax.nn.softmax(jnp.concatenate([s_loc, s_ctx], axis=-1).astype(jnp.float32), axis=-1).astype(v_g.dtype)
        p_loc = p[..., :n_loc].reshape(B, NA_HEADS, GRID_W, kr, NA_WIN_C)
        return (jnp.einsum('bhqij,biqjhd->bqhd', p_loc, v_win)
                + jnp.einsum('bhqk,bkhd->bqhd', p[..., n_loc:], vc))

    o = lax.map(row_block, jnp.arange(rows))
    y = o.transpose(1, 0, 2, 3, 4).reshape(B, S, HD) @ w_o
    yc = None
    if with_ctx_out:
        qc = (hc @ w_qkv[:, :HD]).reshape(B, Lc, NA_HEADS, NA_HEAD_DIM)
        yc = attention(qc, kc, vc, scale).reshape(B, Lc, HD) @ w_o
    return y, yc


def setup_inputs(seed: int = 0) -> dict:
    key = jax.random.key(seed)
    ks = jax.random.split(key, 24)
    f32 = jnp.float32
    D = D_MODEL

    def w(k, shape, fan_in, g=1.0):
        return jax.random.normal(k, shape, f32) * (g * fan_in ** -0.5)

    def gain(k, shape):
        return 1.0 + 0.05 * jax.random.normal(k, shape, f32)

    return {
        "x": jax.random.normal(ks[0], (BATCH, SEQ, D), f32),
        "c": jax.random.normal(ks[1], (BATCH, D), f32),
        "ctx": jax.random.normal(ks[2], (BATCH, CTX_LEN, D), f32),
        "c_ctx": jax.random.normal(ks[3], (D,), f32),
        "ada_w": w(ks[4], (DEPTH, D, 6 * D), D, 0.3),
        "ada_b": 0.02 * jax.random.normal(ks[5], (DEPTH, 6 * D), f32),
        "norm_mix": gain(ks[6], (DEPTH, D)),
        "norm_ffn": gain(ks[7], (DEPTH, D)),
        "norm_final": gain(ks[8], (D,)),
        "mla_w_dq": w(ks[9], (N_MLA_LAYERS, D, MLA_Q_RANK), D),
        "mla_q_norm": gain(ks[10], (N_MLA_LAYERS, MLA_Q_RANK)),
        "mla_w_uq": w(ks[11], (N_MLA_LAYERS, MLA_Q_RANK, MLA_HEADS * (MLA_NOPE + MLA_ROPE)), MLA_Q_RANK),
        "mla_w_dkv": w(ks[12], (N_MLA_LAYERS, D, MLA_KV_RANK + MLA_ROPE), D),
        "mla_kv_norm": gain(ks[13], (N_MLA_LAYERS, MLA_KV_RANK)),
        "mla_w_ukv": w(ks[14], (N_MLA_LAYERS, MLA_KV_RANK, MLA_HEADS * (MLA_NOPE + MLA_V)), MLA_KV_RANK),
        "mla_w_o": w(ks[15], (N_MLA_LAYERS, MLA_HEADS * MLA_V, D), MLA_HEADS * MLA_V),
        "na_w_qkv": w(ks[16], (N_NA_LAYERS, D, 3 * NA_HEADS * NA_HEAD_DIM), D),
        "na_rel_bias": 0.1 * jax.random.normal(ks[17], (N_NA_LAYERS, NA_HEADS, 2 * NA_WIN_R - 1, 2 * NA_WIN_C - 1), f32),
        "na_w_o": w(ks[18], (N_NA_LAYERS, NA_HEADS * NA_HEAD_DIM, D), NA_HEADS * NA_HEAD_DIM),
        "ffn_w_gate": w(ks[19], (DEPTH, D, D_FF), D),
        "ffn_w_up": w(ks[20], (DEPTH, D, D_FF), D),
        "ffn_w_down": w(ks[21], (DEPTH, D_FF, D), D_FF),
    }


def reference(x, c, ctx, c_ctx, ada_w, ada_b, norm_mix, norm_ffn, norm_final,
              mla_w_dq, mla_q_norm, mla_w_uq, mla_w_dkv, mla_kv_norm, mla_w_ukv, mla_w_o,
              na_w_qkv, na_rel_bias, na_w_o,
              ffn_w_gate, ffn_w_up, ffn_w_down):
    S = x.shape[1]
    rope = axial_rope_tables(S, x.dtype)
    xc = ctx
    for i in range(DEPTH):
        last = i == DEPTH - 1
        j = i // N_MIXERS
        m = jax.nn.silu(c) @ ada_w[i] + ada_b[i]
        mc = jax.nn.silu(c_ctx) @ ada_w[i] + ada_b[i]
        sh1, sc1, g1, sh2, sc2, g2 = jnp.split(m[:, None, :], 6, axis=-1)
        csh1, csc1, cg1, csh2, csc2, cg2 = jnp.split(mc, 6, axis=-1)
        h = modulate(rmsnorm(x, norm_mix[i]), sh1, sc1)
        hc = modulate(rmsnorm(xc, norm_mix[i]), csh1, csc1)
        if i % N_MIXERS == 0:
            y, yc = mla_mixer(h, hc, mla_w_dq[j], mla_q_norm[j], mla_w_uq[j], mla_w_dkv[j],
                              mla_kv_norm[j], mla_w_ukv[j], mla_w_o[j], rope, not last)
        else:
            y, yc = na_mixer(h, hc, na_w_qkv[j], na_rel_bias[j], na_w_o[j], not last)
        x = x + g1 * y
        x = x + g2 * swiglu(modulate(rmsnorm(x, norm_ffn[i]), sh2, sc2),
                            ffn_w_gate[i], ffn_w_up[i], ffn_w_down[i])
        if not last:
            xc = xc + cg1 * yc
            xc = xc + cg2 * swiglu(modulate(rmsnorm(xc, norm_ffn[i]), csh2, csc2),
                                   ffn_w_gate[i], ffn_w_up[i], ffn_w_down[i])
    return rmsnorm(x, norm_final)
```

```python
import numpy as np
from contextlib import ExitStack
import concourse.bass as bass
import concourse.mybir as mybir
from concourse.bass_utils import run_bass_kernel_spmd

F32 = mybir.dt.float32
BF16 = mybir.dt.bfloat16
AF = mybir.ActivationFunctionType
ALU = mybir.AluOpType

D = 2048
KC = 16
FF = 5632
FC = 44
NQ = 2560
NK = 4352
NOWN = 2048
EPS = 1e-6
SC_MLA = 192 ** -0.5
SC_NA = 128 ** -0.5
NEG = -30000.0
NDS = 24


class Res:
    __slots__ = ("w", "r")

    def __init__(self):
        self.w = None
        self.r = {}


class Prog:
    def __init__(self, nc, es):
        self.nc = nc
        self.E = {"pe": nc.tensor, "act": nc.scalar, "dve": nc.vector, "pool": nc.gpsimd, "sp": nc.sync}
        self.semobj = {}
        self.cnt = {}
        for e in ("pe", "act", "dve", "pool"):
            self.semobj["c_" + e] = es.enter_context(nc.semaphore("c_" + e))
            self.cnt[e] = 0
        self.dq = {}
        for q in ("sp", "pool"):
            keys = []
            for i in range(NDS):
                k = "d_%s%d" % (q, i)
                self.semobj[k] = es.enter_context(nc.semaphore(k))
                keys.append(k)
            self.dq[q] = keys
        self.dcnt = {}
        self.dsrc = {}
        self.drr = {"sp": 0, "pool": 0}
        self.seen = {e: {} for e in self.E}
        self.nins = 0

    def _wait(self, eng, tok):
        key, val, src = tok
        if src == eng and eng == "pe":
            return
        if self.seen[eng].get(key, 0) >= val:
            return
        self.E[eng].wait_ge(self.semobj[key], val)
        self.seen[eng][key] = val
        self.nins += 1

    def _deps(self, eng, R, W):
        for r in R:
            if r.w is not None:
                self._wait(eng, r.w)
        for w in W:
            if w.w is not None:
                self._wait(eng, w.w)
            for t in w.r.values():
                self._wait(eng, t)

    def _commit(self, tok, R, W):
        for r in R:
            old = r.r.get(tok[0])
            if old is None or old[1] < tok[1]:
                r.r[tok[0]] = tok
        for w in W:
            w.w = tok
            w.r = {}

    def op(self, eng, fn, R=(), W=()):
        self._deps(eng, R, W)
        ins = fn(self.E[eng])
        self.cnt[eng] += 1
        ins.then_inc(self.semobj["c_" + eng], 1)
        self._commit(("c_" + eng, self.cnt[eng], eng), R, W)
        self.nins += 1

    def dma(self, q, out, in_, R=(), W=()):
        keys = self.dq[q]
        i = self.drr[q]
        self.drr[q] = (i + 1) % len(keys)
        key = keys[i]
        c = self.dcnt.get(key, 0)
        if c > 0:
            self._wait(q, (key, 16 * c, q))
        self._deps(q, R, W)
        ins = self.E[q].dma_start(out=out, in_=in_)
        self.dcnt[key] = c + 1
        ins.then_inc(self.semobj[key], 16)
        self._commit((key, 16 * (c + 1), q), R, W)
        self.nins += 1

    def barrier(self):
        toks = [("c_" + e, self.cnt[e], e) for e in self.cnt if self.cnt[e] > 0]
        for q in self.dq:
            for k in self.dq[q]:
                if self.dcnt.get(k, 0) > 0:
                    toks.append((k, 16 * self.dcnt[k], q))
        for eng in self.E:
            for t in toks:
                self._wait(eng, t)


def build_program():
    nc = bass.Bass("TRN2", target_bir_lowering=False)

    def din(name, shape):
        return nc.dram_tensor(name, list(shape), F32, kind="ExternalInput").ap()

    def dscr(name, shape, dt):
        return nc.dram_tensor(name, list(shape), dt, kind="Internal").ap()

    xk = din("xk", [NK, D])
    cvec = din("cvec", [2, D])
    ropek = din("ropek", [NK, 128])
    ropeqc = din("ropeqc", [64, NQ])
    ropeqs = din("ropeqs", [64, NQ])
    identd = din("ident", [128, 128])
    ada_w = din("ada_w", [2, D, 6 * D])
    ada_b = din("ada_b", [2, 6 * D])
    norm_mix = din("norm_mix", [2, D])
    norm_ffn = din("norm_ffn", [2, D])
    norm_final = din("norm_final", [1, D])
    w_dq = din("mla_w_dq", [D, 512])
    q_norm = din("mla_q_norm", [1, 512])
    w_uq = din("mla_w_uq", [512, 3072])
    w_dkv = din("mla_w_dkv", [D, 576])
    kv_norm = din("mla_kv_norm", [1, 512])
    w_ukv = din("mla_w_ukv", [512, 4096])
    w_o0 = din("mla_w_o", [D, D])
    w_qkv = din("na_w_qkv", [D, 3 * D])
    natab = din("natab", [3, 16, 768, 256])
    w_o1 = din("na_w_o", [D, D])
    w_gate = din("ffn_w_gate", [2, D, FF])
    w_up = din("ffn_w_up", [2, D, FF])
    w_down = din("ffn_w_down", [2, FF, D])
    yout = nc.dram_tensor("y", [NOWN, D], F32, kind="ExternalOutput").ap()

    mod = dscr("mod", [2, 2, 6 * D], F32)
    OT = dscr("OT", [NQ, D], BF16)
    x1 = dscr("x1", [NQ, D], F32)
    x2 = dscr("x2", [NQ, D], F32)
    q1T = dscr("q1T", [D, NOWN], BF16)
    k1T = dscr("k1T", [D, NQ], BF16)
    v1 = dscr("v1", [16, 128, NQ // 128, 144], BF16)
    x3 = dscr("x3", [NOWN, D], F32)
    wgb = dscr("wgb", [2, D, FF], BF16)
    wub = dscr("wub", [2, D, FF], BF16)
    wdb = dscr("wdb", [2, FF, D], BF16)

    top = ExitStack()
    P = Prog(nc, top)

    uid = [0]

    def T(es, name, shape, dt):
        uid[0] += 1
        return es.enter_context(nc.sbuf_tensor("%s_%d" % (name, uid[0]), list(shape), dt))

    def PS(es, name, shape, dt):
        uid[0] += 1
        return es.enter_context(nc.psum_tensor("%s_%d" % (name, uid[0]), list(shape), dt))

    identb = T(top, "identb", [128, 128], BF16)
    onesb = T(top, "onesb", [128, 128], BF16)
    RC = Res()
    P.dma("pool", identb[:], identd, W=[RC])
    P.op("dve", lambda e: e.memset(onesb[:], 1.0), W=[RC])

    modR = [Res(), Res()]
    OTR = Res()
    x1R = Res()
    x2R = Res()
    q1R = Res()
    k1R = Res()
    v1R = Res()
    x3R = Res()
    wcastR = [[Res() for _ in range(3)] for _ in range(2)]

    def precast_chunk(l, i):
        P.dma("pool", wgb[l, i * 128:(i + 1) * 128, :], w_gate[l, i * 128:(i + 1) * 128, :], W=[Res()])
        P.dma("pool", wub[l, i * 128:(i + 1) * 128, :], w_up[l, i * 128:(i + 1) * 128, :], W=[Res()])
        P.dma("pool", wdb[l, i * 352:(i + 1) * 352, :], w_down[l, i * 352:(i + 1) * 352, :], W=[Res()])

    def rms_stats(src, n, st, stR, c, srcR, junk):
        P.op("act", lambda e: e.activation(out=junk, in_=src, func=AF.Square, accum_out=st[:, c:c + 1]),
             R=[srcR], W=[stR])
        P.op("act", lambda e: e.activation(out=st[:, c + 1:c + 2], in_=st[:, c:c + 1], func=AF.Sqrt,
                                           scale=1.0 / n, bias=EPS), R=[stR], W=[stR])
        P.op("dve", lambda e: e.reciprocal(out=st[:, c + 2:c + 3], in_=st[:, c + 1:c + 2]), R=[stR], W=[stR])

    def load_bc(dst, dstR, src_row):
        P.dma("sp", dst, src_row.partition_broadcast(128), W=[dstR])

    def modrow(l, s, i):
        return mod[l, s:s + 1, i * D:(i + 1) * D]

    def make_A(A, AR, gw, gwR):
        P.op("dve", lambda e: e.scalar_tensor_tensor(out=A, in0=A, scalar=1.0, in1=gw, op0=ALU.add, op1=ALU.mult),
             R=[gwR], W=[AR])

    sT = T(top, "sT", [128, 32], BF16)
    sTR = Res()
    adak = [T(top, "adak%d" % i, [2, 512], F32) for i in range(2)]
    msk = [T(top, "msk%d" % i, [2, 512], F32) for i in range(2)]
    adakR = [Res(), Res()]
    mskR = [Res(), Res()]
    adan = [0]

    def ada_block(l, nb, wb, wbR, pm, pmR):
        s = adan[0] % 2
        adan[0] += 1
        P.dma("pool", wb[s][:], ada_w[l, :, nb * 512:(nb + 1) * 512].rearrange("(k p) c -> p k c", p=128),
              W=[wbR[s]])
        P.dma("sp", adak[s][:], ada_b[l:l + 1, nb * 512:(nb + 1) * 512].partition_broadcast(2), W=[adakR[s]])
        for k in range(KC):
            P.op("pe", lambda e, k=k: e.matmul(pm[0:2, :], sT[:, 2 * k:2 * k + 2], wb[s][:, k, :],
                                               start=(k == 0), stop=(k == KC - 1)), R=[sTR, wbR[s]], W=[pmR])
        P.op("dve", lambda e: e.tensor_tensor(out=msk[s][:], in0=pm[0:2, :], in1=adak[s][:], op=ALU.add),
             R=[pmR, adakR[s]], W=[mskR[s]])
        P.dma("sp", mod[l, :, nb * 512:(nb + 1) * 512], msk[s][:], R=[mskR[s]], W=[Res()])

    es0 = ExitStack()
    ckvT = T(es0, "ckvT", [128, 4, NK], BF16)
    kpeT = T(es0, "kpeT", [128, NK], BF16)
    cqT = T(es0, "cqT", [128, 4, NQ], BF16)
    ckvR, kpeR, cqR = Res(), Res(), Res()
    P.op("dve", lambda e: e.memset(kpeT[:], 0.0), W=[kpeR])
    esW = ExitStack()
    wdkv = T(esW, "wdkv", [128, KC, 576], BF16)
    wdq = T(esW, "wdq", [128, KC, 512], BF16)
    wdkvR, wdqR = Res(), Res()

    with ExitStack() as es:
        cv = T(es, "cv", [2, D], F32)
        cs = T(es, "cs", [2, D], BF16)
        wb = [T(es, "adaw%d" % i, [128, KC, 512], BF16) for i in range(2)]
        pm = [PS(es, "pm%d" % i, [128, 512], F32) for i in range(2)]
        pt = PS(es, "pt", [128, 32], BF16)
        cvR, csR, ptR = Res(), Res(), Res()
        wbR = [Res(), Res()]
        pmR = [Res(), Res()]
        P.dma("sp", cv[:], cvec, W=[cvR])
        P.op("act", lambda e: e.activation(out=cs[:], in_=cv[:], func=AF.Silu), R=[cvR], W=[csR])
        for k in range(KC):
            P.op("pe", lambda e, k=k: e.transpose(pt[:, 2 * k:2 * k + 2], cs[0:2, k * 128:(k + 1) * 128],
                                                   identb[0:2, 0:2]), R=[csR, RC], W=[ptR])
        P.op("dve", lambda e: e.tensor_copy(out=sT[:], in_=pt[:]), R=[ptR], W=[sTR])
        for nb in range(8):
            ada_block(0, nb, wb, wbR, pm[nb % 2], pmR[nb % 2])
        P.dma("pool", wdkv[:], w_dkv.rearrange("(k p) c -> p k c", p=128), W=[wdkvR])
        P.dma("pool", wdq[:], w_dq.rearrange("(k p) c -> p k c", p=128), W=[wdqR])
        P.barrier()
    ada_todo = [(0, nb) for nb in range(8, 24)] + [(1, nb) for nb in range(24)]


    nrm_n = [0]
    NSPL = 896

    def norm_s1(xt, xR, A, AR, B, BR, tmp, tmpR, hbs, hbRs, junk, st, stR):
        i = nrm_n[0] % 2
        nrm_n[0] += 1
        hb = hbs[i]
        rms_stats(xt, D, st, stR, 0, xR, junk)
        P.op("dve", lambda e: e.scalar_tensor_tensor(out=tmp, in0=xt, scalar=st[:, 2:3], in1=A,
                                                     op0=ALU.mult, op1=ALU.mult), R=[xR, stR, AR], W=[tmpR])
        P.op("pool", lambda e: e.tensor_tensor(out=hb[:, 0:NSPL], in0=tmp[:, 0:NSPL], in1=B[:, 0:NSPL], op=ALU.add),
             R=[tmpR, BR], W=[hbRs[i][0]])
        P.op("dve", lambda e: e.tensor_tensor(out=hb[:, NSPL:D], in0=tmp[:, NSPL:D], in1=B[:, NSPL:D], op=ALU.add),
             R=[tmpR, BR], W=[hbRs[i][1]])
        return i

    def norm_s2(i, hbs, hbRs, ptr, ptrR, hT_lo, hT_hi, hTR):
        hb = hbs[i]
        for k in range(KC):
            pi = k // 8
            P.op("pe", lambda e, k=k, pi=pi: e.transpose(ptr[pi][:, (k % 8) * 128:(k % 8 + 1) * 128],
                                                          hb[:, k * 128:(k + 1) * 128], identb[:]),
                 R=[hbRs[i][0], hbRs[i][1], RC], W=[ptrR[pi]])
        P.op("act", lambda e: e.activation(out=hT_lo, in_=ptr[0][:].rearrange("p (k t) -> p k t", k=8), func=AF.Copy),
             R=[ptrR[0]], W=[hTR[0]])
        P.op("act", lambda e: e.activation(out=hT_hi, in_=ptr[1][:].rearrange("p (k t) -> p k t", k=8), func=AF.Copy),
             R=[ptrR[1]], W=[hTR[1]])

    with ExitStack() as es:
        xin = [T(es, "xin%d" % i, [128, D], F32) for i in range(3)]
        junk = T(es, "junk", [128, D], BF16)
        tmp = T(es, "tmp", [128, D], F32)
        hb = [T(es, "hb%d" % i, [128, D], BF16) for i in range(2)]
        hT = [T(es, "hT%d" % i, [128, KC, 128], BF16) for i in range(2)]
        bcA = [T(es, "bcA%d" % i, [128, D], F32) for i in range(2)]
        bcB = [T(es, "bcB%d" % i, [128, D], F32) for i in range(2)]
        kvn = T(es, "kvn", [128, 512], F32)
        qn = T(es, "qn", [128, 512], F32)
        rk = [T(es, "rk%d" % i, [128, 128], F32) for i in range(2)]
        st = [T(es, "st%d" % i, [128, 16], F32) for i in range(2)]
        ckb = T(es, "ckb", [128, 512], BF16)
        cqb = T(es, "cqb", [128, 512], BF16)
        kpb = T(es, "kpb", [128, 64], BF16)
        t1 = T(es, "t1", [128, 64], F32)
        t2 = T(es, "t2", [128, 64], F32)
        ptr = [PS(es, "ptr%d" % i, [128, 1024], BF16) for i in range(2)]
        pkv1 = PS(es, "pkv1", [128, 512], F32)
        pkv2 = PS(es, "pkv2", [128, 512], F32)
        pq = PS(es, "pq", [128, 512], F32)
        ptc = PS(es, "ptc", [128, 1024], BF16)
        ptc2 = PS(es, "ptc2", [128, 1024], BF16)
        xinR = [Res() for _ in range(3)]
        tmpR, hbR = Res(), [[Res(), Res()], [Res(), Res()]]
        hTR = [[Res(), Res()], [Res(), Res()]]
        bcAR = [Res(), Res()]
        bcBR = [Res(), Res()]
        kvnR, qnR = Res(), Res()
        rkR = [Res(), Res()]
        stR = [Res(), Res()]
        ckbR, cqbR, kpbR, t1R, t2R = Res(), Res(), Res(), Res(), Res()
        ptrR = [Res(), Res()]
        pkv1R, pkv2R, pqR, ptcR, ptc2R = Res(), Res(), Res(), Res(), Res()

        load_bc(kvn[:], kvnR, kv_norm)
        load_bc(qn[:], qnR, q_norm)
        load_bc(tmp[:], tmpR, norm_mix[0:1, :])
        for s in range(2):
            load_bc(bcA[s][:], bcAR[s], modrow(0, s, 1))
            load_bc(bcB[s][:], bcBR[s], modrow(0, s, 0))
            make_A(bcA[s][:], bcAR[s], tmp[:], tmpR)

        NB_A = NK // 128
        rk3 = [rk[0], rk[1], T(es, "rk2", [128, 128], F32)]
        rk3R = [rkR[0], rkR[1], Res()]
        stB = [T(es, "stB%d" % i, [128, 16], F32) for i in range(2)]
        stBR = [Res(), Res()]

        def A_S1(tb):
            s3 = tb % 3
            sc = 1 if tb < 2 else 0
            P.dma("sp", xin[s3][:], xk[tb * 128:(tb + 1) * 128, :], W=[xinR[s3]])
            P.dma("sp", rk3[s3][:], ropek[tb * 128:(tb + 1) * 128, :], W=[rk3R[s3]])
            return norm_s1(xin[s3][:], xinR[s3], bcA[sc][:], bcAR[sc], bcB[sc][:], bcBR[sc], tmp[:], tmpR, hb, hbR,
                           junk[:], st[tb % 2], stR[tb % 2])

        def A_S2a(tb, hi):
            s2 = tb % 2
            norm_s2(hi, hb, hbR, ptr, ptrR, hT[s2][:, 0:8, :], hT[s2][:, 8:16, :], hTR[s2])

        def A_S2b(tb):
            s2 = tb % 2
            for k in range(KC):
                P.op("pe", lambda e, k=k: e.matmul(pkv1[:, :], hT[s2][:, k, :], wdkv[:, k, 0:512],
                                                   start=(k == 0), stop=(k == KC - 1)),
                     R=[hTR[s2][0], hTR[s2][1], wdkvR], W=[pkv1R])
            for k in range(KC):
                P.op("pe", lambda e, k=k: e.matmul(pkv2[:, 0:64], hT[s2][:, k, :], wdkv[:, k, 512:576],
                                                   start=(k == 0), stop=(k == KC - 1)),
                     R=[hTR[s2][0], hTR[s2][1], wdkvR], W=[pkv2R])
            if tb < NQ // 128:
                for k in range(KC):
                    P.op("pe", lambda e, k=k: e.matmul(pq[:, :], hT[s2][:, k, :], wdq[:, k, :],
                                                       start=(k == 0), stop=(k == KC - 1)),
                         R=[hTR[s2][0], hTR[s2][1], wdqR], W=[pqR])

        def A_S3(tb):
            s2 = tb % 2
            s3 = tb % 3
            sB = stB[s2]
            sBR = stBR[s2]
            rkt = rk3[s3]
            rktR = rk3R[s3]
            rms_stats(pkv1[:, :], 512, sB, sBR, 3, pkv1R, junk[:, 0:512])
            P.op("dve", lambda e: e.scalar_tensor_tensor(out=ckb[:], in0=pkv1[:, :], scalar=sB[:, 5:6],
                                                         in1=kvn[:], op0=ALU.mult, op1=ALU.mult),
                 R=[pkv1R, sBR, kvnR], W=[ckbR])
            for c in range(4):
                P.op("pe", lambda e, c=c: e.transpose(ptc[:, c * 128:(c + 1) * 128], ckb[:, c * 128:(c + 1) * 128],
                                                      identb[:]), R=[ckbR, RC], W=[ptcR])
            kp4 = pkv2[:, 0:64].rearrange("p (g h f) -> p g h f", g=2, h=2)
            sn4 = rkt[:, 64:128].rearrange("p (g h f) -> p g h f", g=2, h=2)
            t24 = t2[:].rearrange("p (g h f) -> p g h f", g=2, h=2)
            P.op("dve", lambda e: e.tensor_tensor(out=t1[:], in0=pkv2[:, 0:64], in1=rkt[:, 0:64], op=ALU.mult),
                 R=[pkv2R, rktR], W=[t1R])
            P.op("dve", lambda e: e.tensor_tensor(out=t24[:, :, 0, :], in0=kp4[:, :, 1, :], in1=sn4[:, :, 0, :],
                                                  op=ALU.mult), R=[pkv2R, rktR], W=[t2R])
            P.op("dve", lambda e: e.tensor_tensor(out=t24[:, :, 1, :], in0=kp4[:, :, 0, :], in1=sn4[:, :, 1, :],
                                                  op=ALU.mult), R=[pkv2R, rktR], W=[t2R])
            P.op("dve", lambda e: e.tensor_tensor(out=kpb[:], in0=t1[:], in1=t2[:], op=ALU.add),
                 R=[t1R, t2R], W=[kpbR])
            P.op("pe", lambda e: e.transpose(ptc[0:64, 512:640], kpb[:, :], identb[:]), R=[kpbR, RC], W=[ptcR])
            P.op("act", lambda e: e.activation(out=ckvT[:, :, tb * 128:(tb + 1) * 128],
                                               in_=ptc[:, 0:512].rearrange("p (c t) -> p c t", c=4),
                                               func=AF.Copy), R=[ptcR], W=[ckvR])
            P.op("act", lambda e: e.activation(out=kpeT[0:64, tb * 128:(tb + 1) * 128], in_=ptc[0:64, 512:640],
                                               func=AF.Copy), R=[ptcR], W=[kpeR])
            if tb < NQ // 128:
                rms_stats(pq[:, :], 512, sB, sBR, 6, pqR, junk[:, 512:1024])
                P.op("dve", lambda e: e.scalar_tensor_tensor(out=cqb[:], in0=pq[:, :], scalar=sB[:, 8:9],
                                                             in1=qn[:], op0=ALU.mult, op1=ALU.mult),
                     R=[pqR, sBR, qnR], W=[cqbR])
                for c in range(4):
                    P.op("pe", lambda e, c=c: e.transpose(ptc2[:, c * 128:(c + 1) * 128],
                                                          cqb[:, c * 128:(c + 1) * 128], identb[:]),
                         R=[cqbR, RC], W=[ptc2R])
                P.op("dve", lambda e: e.tensor_copy(out=cqT[:, :, tb * 128:(tb + 1) * 128],
                                                    in_=ptc2[:, 0:512].rearrange("p (c t) -> p c t", c=4)),
                     R=[ptc2R], W=[cqR])

        hslot = {0: A_S1(0)}
        for tb in range(NB_A):
            if tb + 1 < NB_A:
                hslot[tb + 1] = A_S1(tb + 1)
            A_S2a(tb, hslot[tb])
            if tb > 0:
                A_S3(tb - 1)
            A_S2b(tb)
        A_S3(NB_A - 1)
        P.barrier()

    esW.close()

    with ExitStack() as es:
        KT = T(es, "KT", [128, NK], BF16)
        V = T(es, "V", [128, NK // 128, 129], BF16)
        qT = T(es, "qT", [128, NQ], BF16)
        qpeT = T(es, "qpeT", [128, NQ], BF16)
        rqc = T(es, "rqc", [64, NQ], F32)
        rqs = T(es, "rqs", [64, NQ], F32)
        wq = [T(es, "wq%d" % i, [128, 4, 192], BF16) for i in range(2)]
        wqs = [T(es, "wqs%d" % i, [128, 4, 64], BF16) for i in range(2)]
        wkv = [T(es, "wkv%d" % i, [128, 4, 256], BF16) for i in range(2)]
        Pt = [T(es, "Pt%d" % i, [128, 512], BF16) for i in range(4)]
        rec = [T(es, "rec%d" % i, [128, 4], F32) for i in range(2)]
        On = [T(es, "On%d" % i, [128, 4, 128], BF16) for i in range(2)]
        r1 = T(es, "r1", [64, 512], F32)
        r2 = T(es, "r2", [64, 512], F32)
        bank = [PS(es, "bk%d" % i, [128, 512], F32) for i in range(8)]
        bankR = [Res() for _ in range(8)]
        KTR, VR, qTR, qpeR, rqR = Res(), Res(), Res(), Res(), Res()
        wqR = [Res(), Res()]
        wqsR = [Res(), Res()]
        wkvR = [Res(), Res()]
        PtR = [Res() for _ in range(4)]
        recR = [Res(), Res()]
        OnR = [Res(), Res()]
        r1R, r2R = Res(), Res()
        Sb = [0, 1, 2, 3]
        accb = [(4, 5), (6, 7)]
        pA, pB, pC = 3, 0, 1

        P.op("dve", lambda e: e.memset(qpeT[:], 0.0), W=[qpeR])
        P.op("dve", lambda e: e.memset(V[:, :, 128:129], 1.0), W=[VR])
        P.dma("sp", rqc[:], ropeqc, W=[rqR])
        P.dma("sp", rqs[:], ropeqs, W=[rqR])

        def load_head_w(h):
            s = h % 2
            P.dma("pool", wq[s][:], w_uq[:, h * 192:(h + 1) * 192].rearrange("(k p) c -> p k c", p=128), W=[wqR[s]])
            P.dma("pool", wkv[s][:], w_ukv[:, h * 256:(h + 1) * 256].rearrange("(k p) c -> p k c", p=128), W=[wkvR[s]])

        load_head_w(0)
        evn = [0]
        adawb = [T(es, "adawB%d" % i, [128, KC, 512], BF16) for i in range(2)]
        adawbR = [Res(), Res()]

        def evac(out, in_, R, W):
            evn[0] += 1
            if evn[0] % 2 == 0:
                P.op("act", lambda e: e.activation(out=out, in_=in_, func=AF.Copy), R=R, W=W)
            else:
                P.op("dve", lambda e: e.tensor_copy(out=out, in_=in_), R=R, W=W)

        gstep = [0]
        tcount = [0]
        for h in range(16):
            s = h % 2
            if h + 1 < 16:
                load_head_w(h + 1)
            for (al, anb) in ada_todo[h * 40 // 16:(h + 1) * 40 // 16]:
                ada_block(al, anb, adawb, adawbR, bank[pA], bankR[pA])
            precast_chunk(0, h)
            precast_chunk(1, h)
            for (d0, s0) in ((0, 144), (16, 128), (32, 176), (48, 160)):
                P.op("act", lambda e, d0=d0, s0=s0, s=s: e.activation(out=wqs[s][:, :, d0:d0 + 16],
                                                                      in_=wq[s][:, :, s0:s0 + 16], func=AF.Copy),
                     R=[wqR[s]], W=[wqsR[s]])
            for nt in range(9):
                w = 512 if nt < 8 else 256
                b = pA if nt % 2 == 0 else pB
                for k in range(4):
                    P.op("pe", lambda e, k=k, b=b, w=w, nt=nt, s=s: e.matmul(bank[b][:, 0:w], wkv[s][:, k, 0:128],
                                                                             ckvT[:, k, nt * 512:nt * 512 + w],
                                                                             start=(k == 0), stop=(k == 3)),
                         R=[wkvR[s], ckvR], W=[bankR[b]])
                evac(KT[:, nt * 512:nt * 512 + w], bank[b][:, 0:w], [bankR[b]], [KTR])
            for vg in range(9):
                n = 4 if vg < 8 else 2
                b = pA if vg % 2 == 1 else pB
                for ci in range(n):
                    kc = vg * 4 + ci
                    for k in range(4):
                        P.op("pe", lambda e, k=k, b=b, ci=ci, kc=kc, s=s: e.matmul(
                            bank[b][:, ci * 128:(ci + 1) * 128], ckvT[:, k, kc * 128:(kc + 1) * 128],
                            wkv[s][:, k, 128:256], start=(k == 0), stop=(k == 3)),
                            R=[wkvR[s], ckvR], W=[bankR[b]])
                evac(V[:, vg * 4:vg * 4 + n, 0:128], bank[b][:, 0:n * 128].rearrange("p (c d) -> p c d", c=n),
                     [bankR[b]], [VR])
            for qt in range(5):
                c0 = qt * 512
                for k in range(4):
                    P.op("pe", lambda e, k=k, c0=c0, s=s: e.matmul(bank[pA][:, :], wq[s][:, k, 0:128],
                                                                   cqT[:, k, c0:c0 + 512], start=(k == 0), stop=(k == 3)),
                         R=[wqR[s], cqR], W=[bankR[pA]])
                evac(qT[:, c0:c0 + 512], bank[pA][:, :], [bankR[pA]], [qTR])
                for k in range(4):
                    P.op("pe", lambda e, k=k, c0=c0, s=s: e.matmul(bank[pB][0:64, :], wq[s][:, k, 128:192],
                                                                   cqT[:, k, c0:c0 + 512], start=(k == 0), stop=(k == 3)),
                         R=[wqR[s], cqR], W=[bankR[pB]])
                for k in range(4):
                    P.op("pe", lambda e, k=k, c0=c0, s=s: e.matmul(bank[pC][0:64, :], wqs[s][:, k, :],
                                                                   cqT[:, k, c0:c0 + 512], start=(k == 0), stop=(k == 3)),
                         R=[wqsR[s], cqR], W=[bankR[pC]])
                P.op("dve", lambda e, c0=c0: e.tensor_tensor(out=r1[:], in0=bank[pB][0:64, :], in1=rqc[:, c0:c0 + 512],
                                                             op=ALU.mult), R=[bankR[pB], rqR], W=[r1R])
                P.op("dve", lambda e, c0=c0: e.tensor_tensor(out=r2[:], in0=bank[pC][0:64, :], in1=rqs[:, c0:c0 + 512],
                                                             op=ALU.mult), R=[bankR[pC], rqR], W=[r2R])
                P.op("pool", lambda e, c0=c0: e.tensor_tensor(out=qpeT[0:64, c0:c0 + 512], in0=r1[:], in1=r2[:], op=ALU.add),
                     R=[r1R, r2R], W=[qpeR])
            qtiles = [(0, 256, [0, 1])] + [(256 + i * 512, 512, list(range(34))) for i in range(4)] + \
                     [(2304, 256, list(range(34)))]
            for (q0, w, kcs) in qtiles:
                ti = tcount[0]
                tcount[0] += 1
                aO, aS = accb[ti % 2]
                n = len(kcs)
                base = gstep[0]
                gstep[0] += n

                def QK(i, q0=q0, w=w, kcs=kcs, base=base):
                    kc = kcs[i]
                    sb = Sb[(base + i) % 4]
                    P.op("pe", lambda e: e.matmul(bank[sb][:, 0:w], KT[:, kc * 128:(kc + 1) * 128], qT[:, q0:q0 + w],
                                                  start=True, stop=False), R=[KTR, qTR], W=[bankR[sb]])
                    P.op("pe", lambda e: e.matmul(bank[sb][:, 0:w], kpeT[:, kc * 128:(kc + 1) * 128],
                                                  qpeT[:, q0:q0 + w], start=False, stop=True),
                         R=[kpeR, qpeR], W=[bankR[sb]])

                for i0 in range(min(3, n)):
                    QK(i0)
                for i in range(n):
                    kc = kcs[i]
                    sb = Sb[(base + i) % 4]
                    ps = (base + i) % 4
                    P.op("act", lambda e, sb=sb, ps=ps, w=w: e.activation(out=Pt[ps][:, 0:w], in_=bank[sb][:, 0:w],
                                                                         func=AF.Exp, scale=SC_MLA),
                         R=[bankR[sb]], W=[PtR[ps]])
                    for qb in range(w // 128):
                        bk = 4 + qb
                        off = 0
                        P.op("pe", lambda e, kc=kc, ps=ps, i=i, n=n, bk=bk, off=off, qb=qb: e.matmul(
                            bank[bk][:, off:off + 129], Pt[ps][:, qb * 128:(qb + 1) * 128], V[:, kc, :],
                            start=(i == 0), stop=(i == n - 1)), R=[VR, PtR[ps]], W=[bankR[bk]])
                    if i + 3 < n:
                        QK(i + 3)
                sl = ti % 2
                nqb = w // 128
                for qb in range(nqb):
                    bk = 4 + qb
                    off = 0
                    P.op("dve", lambda e, sl=sl, bk=bk, off=off, qb=qb: e.reciprocal(
                        out=rec[sl][:, qb:qb + 1], in_=bank[bk][:, off + 128:off + 129]), R=[bankR[bk]], W=[recR[sl]])
                    P.op("dve", lambda e, sl=sl, bk=bk, off=off, qb=qb: e.tensor_scalar_mul(
                        out=On[sl][:, qb, :], in0=bank[bk][:, off:off + 128], scalar1=rec[sl][:, qb:qb + 1]),
                        R=[bankR[bk], recR[sl]], W=[OnR[sl]])
                P.dma("sp", OT[q0:q0 + w, h * 128:(h + 1) * 128].rearrange("(qb p) d -> p qb d", p=128),
                      On[sl][:, 0:nqb, :], R=[OnR[sl]], W=[Res()])
        P.barrier()
    es0.close()

    def mixer_out(l, w_o, nblk, ot_src, x_src, x_dst, x_dstR, x_srcR, has_ctx):
        with ExitStack() as es:
            wo = T(es, "wo", [128, KC, D], BF16)
            xb = [T(es, "xb%d" % i, [128, D], F32) for i in range(3)]
            xo = [T(es, "xo%d" % i, [128, D], F32) for i in range(2)]
            tp = [T(es, "tp%d" % i, [128, 512], F32) for i in range(2)]
            G = [T(es, "G%d" % i, [128, D], F32) for i in range(2)]
            py = [PS(es, "py%d" % i, [128, 512], F32) for i in range(4)]
            woR = [Res() for _ in range(4)]
            xbR = [Res() for _ in range(3)]
            xoR = [[Res() for _ in range(4)] for _ in range(2)]
            tpR = [Res(), Res()]
            GR = [Res(), Res()]
            pyR = [Res() for _ in range(4)]
            if ot_src is None:
                ob = [T(es, "ob%d" % i, [128, D], BF16) for i in range(3)]
                ot = [T(es, "ot%d" % i, [128, KC, 128], BF16) for i in range(2)]
                ptr = [PS(es, "mptr%d" % i, [128, 1024], BF16) for i in range(2)]
                obR = [Res() for _ in range(3)]
                otR = [[Res(), Res()], [Res(), Res()]]
                ptrR = [Res(), Res()]
            load_bc(G[0][:], GR[0], modrow(l, 0, 2))
            if has_ctx:
                load_bc(G[1][:], GR[1], modrow(l, 1, 2))
            for nb in range(4):
                P.dma("pool", wo[:, :, nb * 512:(nb + 1) * 512],
                      w_o[:, nb * 512:(nb + 1) * 512].rearrange("(k p) c -> p k c", p=128), W=[woR[nb]])

            def loads(blk):
                if ot_src is None:
                    P.dma("sp", ob[blk % 3][:], OT[blk * 128:(blk + 1) * 128, :], R=[OTR], W=[obR[blk % 3]])
                P.dma("sp", xb[blk % 3][:], x_src(blk), R=[x_srcR] if x_srcR is not None else [], W=[xbR[blk % 3]])

            def transp(blk):
                o_ = ob[blk % 3]
                s2 = blk % 2
                for k in range(KC):
                    pi = k // 8
                    P.op("pe", lambda e, k=k, pi=pi: e.transpose(ptr[pi][:, (k % 8) * 128:(k % 8 + 1) * 128],
                                                                  o_[:, k * 128:(k + 1) * 128], identb[:]),
                         R=[obR[blk % 3], RC], W=[ptrR[pi]])
                P.op("act", lambda e: e.activation(out=ot[s2][:, 0:8, :], in_=ptr[0][:].rearrange("p (k t) -> p k t", k=8),
                                                   func=AF.Copy), R=[ptrR[0]], W=[otR[s2][0]])
                P.op("dve", lambda e: e.tensor_copy(out=ot[s2][:, 8:16, :], in_=ptr[1][:].rearrange("p (k t) -> p k t", k=8)),
                     R=[ptrR[1]], W=[otR[s2][1]])

            loads(0)
            if nblk > 1:
                loads(1)
            if ot_src is None:
                transp(0)
            it = 0
            for blk in range(nblk):
                if blk + 2 < nblk:
                    loads(blk + 2)
                if ot_src is None and blk + 1 < nblk:
                    transp(blk + 1)
                s3 = blk % 3
                s2 = blk % 2
                sc = 1 if (has_ctx and blk < 2) else 0
                for nb in range(4):
                    pb = it % 4
                    t2 = it % 2
                    it += 1
                    for k in range(KC):
                        if ot_src is None:
                            lhs = ot[s2][:, k, :]
                            lR = otR[s2]
                        else:
                            lhs = ot_src[0][:, k, blk * 128:(blk + 1) * 128]
                            lR = [ot_src[1]]
                        P.op("pe", lambda e, k=k, lhs=lhs: e.matmul(py[pb][:, :], lhs, wo[:, k, nb * 512:(nb + 1) * 512],
                                                                    start=(k == 0), stop=(k == KC - 1)),
                             R=lR + [woR[nb]], W=[pyR[pb]])
                    P.op("dve", lambda e: e.tensor_tensor(out=tp[t2][:], in0=py[pb][:, :],
                                                          in1=G[sc][:, nb * 512:(nb + 1) * 512], op=ALU.mult),
                         R=[pyR[pb], GR[sc]], W=[tpR[t2]])
                    P.op("pool", lambda e: e.tensor_tensor(out=xo[s2][:, nb * 512:(nb + 1) * 512], in0=tp[t2][:],
                                                           in1=xb[s3][:, nb * 512:(nb + 1) * 512], op=ALU.add),
                         R=[tpR[t2], xbR[s3]], W=[xoR[s2][nb]])
                P.dma("sp", x_dst(blk), xo[s2][:], R=xoR[s2], W=[Res()])
            P.barrier()

    mixer_out(0, w_o0, NQ // 128, None,
              lambda blk: xk[blk * 128:(blk + 1) * 128, :],
              lambda blk: x1[blk * 128:(blk + 1) * 128, :], x1R, None, True)

    def ffn(l, ntile, x_src, x_srcR, x_dst, x_dstR, has_ctx, final):
        with ExitStack() as es:
            h2T = T(es, "h2T", [128, KC, 512], BF16)
            actT = T(es, "actT", [128, FC, 512], BF16)
            wg = [T(es, "wg%d" % i, [128, KC, 256], BF16) for i in range(2)]
            wu = [T(es, "wu%d" % i, [128, KC, 256], BF16) for i in range(2)]
            wd = [T(es, "wd%d" % i, [128, 4, 512], BF16) for i in range(3)]
            xr = [T(es, "xr%d" % i, [128, D], F32) for i in range(4)]
            bcA = T(es, "fA", [128, D], F32)
            bcB = T(es, "fB", [128, D], F32)
            bcG = T(es, "fG", [128, D], F32)
            bcGc = T(es, "fGc", [128, D], F32) if (has_ctx or final) else None
            tmp = T(es, "ftmp", [128, D], F32)
            hb = [T(es, "fhb%d" % i, [128, D], BF16) for i in range(2)]
            junk = T(es, "fjunk", [128, D], BF16)
            sg = [T(es, "sg%d" % i, [128, 512], F32) for i in range(2)]
            tq = [T(es, "tq%d" % i, [128, 512], F32) for i in range(2)]
            st = [T(es, "fst%d" % i, [128, 16], F32) for i in range(2)]
            pd = [PS(es, "pd%d" % i, [128, 512], F32) for i in range(4)]
            ptr = [PS(es, "fptr%d" % i, [128, 1024], BF16) for i in range(2)]
            h2R, actR = [Res(), Res()], Res()
            wgR = [Res(), Res()]
            wuR = [Res(), Res()]
            wdR = [Res() for _ in range(3)]
            xrR = [Res() for _ in range(4)]
            AR, BR, GR, GcR, tmpR, hbR = Res(), Res(), Res(), Res(), Res(), [[Res(), Res()], [Res(), Res()]]
            sgR = [Res(), Res()]
            tqR = [Res(), Res()]
            stR = [Res(), Res()]
            pdR = [Res() for _ in range(4)]
            ptrR = [Res(), Res()]

            load_bc(tmp[:], tmpR, norm_ffn[l:l + 1, :])
            load_bc(bcA[:], AR, modrow(l, 0, 4))
            make_A(bcA[:], AR, tmp[:], tmpR)
            load_bc(bcB[:], BR, modrow(l, 0, 3))
            load_bc(bcG[:], GR, modrow(l, 0, 5))
            if has_ctx:
                load_bc(xr[3][:], xrR[3], modrow(l, 1, 4))
                make_A(xr[3][:], xrR[3], tmp[:], tmpR)
                load_bc(xr[2][:], xrR[2], modrow(l, 1, 3))
                load_bc(bcGc[:], GcR, modrow(l, 1, 5))
            if final:
                load_bc(bcGc[:], GcR, norm_final)

            wgv = wgb[l].rearrange("(k p) c -> p k c", p=128)
            wuv = wub[l].rearrange("(k p) c -> p k c", p=128)
            wdv = wdb[l].rearrange("(j p) c -> p j c", p=128)
            wdn = [0]
            for t in range(ntile):
                def F_S1(bi, t=t):
                    blk = t * 4 + bi
                    ctxb = has_ctx and blk < 2
                    if t == 0:
                        P.dma("sp", xr[bi][:], x_src(blk), R=[x_srcR], W=[xrR[bi]])
                    A_, AR_ = (xr[3][:], xrR[3]) if ctxb else (bcA[:], AR)
                    B_, BR_ = (xr[2][:], xrR[2]) if ctxb else (bcB[:], BR)
                    return norm_s1(xr[bi][:], xrR[bi], A_, AR_, B_, BR_, tmp[:], tmpR, hb, hbR, junk[:],
                                   st[bi % 2], stR[bi % 2])

                hs = {0: F_S1(0)}
                for bi in range(4):
                    if bi + 1 < 4:
                        hs[bi + 1] = F_S1(bi + 1)
                    norm_s2(hs[bi], hb, hbR, ptr, ptrR,
                            h2T[:, 0:8, bi * 128:(bi + 1) * 128], h2T[:, 8:16, bi * 128:(bi + 1) * 128], h2R)
                for jj in range(FC // 2):
                    s = jj % 2
                    P.dma("sp", wg[s][:], wgv[:, :, jj * 256:(jj + 1) * 256], W=[wgR[s]])
                    P.dma("sp", wu[s][:], wuv[:, :, jj * 256:(jj + 1) * 256], W=[wuR[s]])
                    for jh in range(2):
                        j = 2 * jj + jh
                        pg = (2 * j) % 4
                        pu = (2 * j + 1) % 4
                        for k in range(KC):
                            P.op("pe", lambda e, k=k, s=s, jh=jh, pg=pg: e.matmul(
                                pd[pg][:, :], wg[s][:, k, jh * 128:(jh + 1) * 128], h2T[:, k, :],
                                start=(k == 0), stop=(k == KC - 1)), R=[wgR[s], h2R[0], h2R[1]], W=[pdR[pg]])
                        for k in range(KC):
                            P.op("pe", lambda e, k=k, s=s, jh=jh, pu=pu: e.matmul(
                                pd[pu][:, :], wu[s][:, k, jh * 128:(jh + 1) * 128], h2T[:, k, :],
                                start=(k == 0), stop=(k == KC - 1)), R=[wuR[s], h2R[0], h2R[1]], W=[pdR[pu]])
                        s2 = j % 2
                        P.op("act", lambda e, s2=s2, pg=pg: e.activation(out=sg[s2][:], in_=pd[pg][:, :], func=AF.Silu),
                             R=[pdR[pg]], W=[sgR[s2]])
                        P.op("dve", lambda e, s2=s2, pu=pu, j=j: e.tensor_tensor(out=actT[:, j, :], in0=pd[pu][:, :],
                                                                                 in1=sg[s2][:], op=ALU.mult),
                             R=[pdR[pu], sgR[s2]], W=[actR])
                for nb in range(4):
                    for jg in range(FC // 4):
                        ws = wdn[0] % 3
                        wdn[0] += 1
                        P.dma("sp", wd[ws][:], wdv[:, jg * 4:(jg + 1) * 4, nb * 512:(nb + 1) * 512],
                              W=[wdR[ws]])
                        for bi in range(4):
                            for jl in range(4):
                                j = jg * 4 + jl
                                P.op("pe", lambda e, j=j, jl=jl, bi=bi, ws=ws: e.matmul(
                                    pd[bi][:, :], actT[:, j, bi * 128:(bi + 1) * 128], wd[ws][:, jl, :],
                                    start=(j == 0), stop=(j == FC - 1)), R=[actR, wdR[ws]], W=[pdR[bi]])
                    for bi in range(4):
                        blk = t * 4 + bi
                        ctxb = has_ctx and blk < 2
                        G_, GR_ = (bcGc[:], GcR) if ctxb else (bcG[:], GR)
                        s2 = bi % 2
                        P.op("dve", lambda e, s2=s2, bi=bi, nb=nb, G_=G_: e.tensor_tensor(
                            out=tq[s2][:], in0=pd[bi][:, :], in1=G_[:, nb * 512:(nb + 1) * 512], op=ALU.mult),
                            R=[pdR[bi], GR_], W=[tqR[s2]])
                        P.op("pool", lambda e, s2=s2, bi=bi, nb=nb: e.tensor_tensor(
                            out=xr[bi][:, nb * 512:(nb + 1) * 512], in0=tq[s2][:], in1=xr[bi][:, nb * 512:(nb + 1) * 512],
                            op=ALU.add), R=[tqR[s2]], W=[xrR[bi]])
                        if nb == 3:
                            if final:
                                rms_stats(xr[bi][:], D, st[bi % 2], stR[bi % 2], 4, xrR[bi], junk[:])
                                P.op("dve", lambda e, bi=bi: e.scalar_tensor_tensor(
                                    out=xr[bi][:], in0=xr[bi][:], scalar=st[bi % 2][:, 6:7], in1=bcGc[:],
                                    op0=ALU.mult, op1=ALU.mult), R=[stR[bi % 2], GcR], W=[xrR[bi]])
                            P.dma("sp", x_dst(blk), xr[bi][:], R=[xrR[bi]], W=[Res()])
                    if nb == 3 and t + 1 < ntile:
                        for bi in range(4):
                            P.dma("sp", xr[bi][:], x_src(t * 4 + bi + 4), R=[x_srcR], W=[xrR[bi]])
            P.barrier()

    ffn(0, NQ // 512, lambda blk: x1[blk * 128:(blk + 1) * 128, :], x1R,
        lambda blk: x2[blk * 128:(blk + 1) * 128, :], x2R, True, False)

    with ExitStack() as es:
        hTa = T(es, "hTa", [128, KC, NQ], BF16)
        hTaR = [Res(), Res()]
        with ExitStack() as es2:
            xin = [T(es2, "dxin%d" % i, [128, D], F32) for i in range(3)]
            junk = T(es2, "djunk", [128, D], BF16)
            tmp = T(es2, "dtmp", [128, D], F32)
            hb = [T(es2, "dhb%d" % i, [128, D], BF16) for i in range(2)]
            bcA = [T(es2, "dA%d" % i, [128, D], F32) for i in range(2)]
            bcB = [T(es2, "dB%d" % i, [128, D], F32) for i in range(2)]
            st = [T(es2, "dst%d" % i, [128, 16], F32) for i in range(2)]
            ptr = [PS(es2, "dptr%d" % i, [128, 1024], BF16) for i in range(2)]
            xinR = [Res() for _ in range(3)]
            tmpR, hbR = Res(), [[Res(), Res()], [Res(), Res()]]
            bcAR = [Res(), Res()]
            bcBR = [Res(), Res()]
            stR = [Res(), Res()]
            ptrR = [Res(), Res()]
            load_bc(tmp[:], tmpR, norm_mix[1:2, :])
            for s in range(2):
                load_bc(bcA[s][:], bcAR[s], modrow(1, s, 1))
                load_bc(bcB[s][:], bcBR[s], modrow(1, s, 0))
                make_A(bcA[s][:], bcAR[s], tmp[:], tmpR)
            def D_S1(tb):
                s3 = tb % 3
                sc = 1 if tb < 2 else 0
                P.dma("sp", xin[s3][:], x2[tb * 128:(tb + 1) * 128, :], R=[x2R], W=[xinR[s3]])
                return norm_s1(xin[s3][:], xinR[s3], bcA[sc][:], bcAR[sc], bcB[sc][:], bcBR[sc], tmp[:], tmpR, hb, hbR,
                               junk[:], st[tb % 2], stR[tb % 2])

            hs = {0: D_S1(0)}
            for tb in range(NQ // 128):
                if tb + 1 < NQ // 128:
                    hs[tb + 1] = D_S1(tb + 1)
                norm_s2(hs[tb], hb, hbR, ptr, ptrR,
                        hTa[:, 0:8, tb * 128:(tb + 1) * 128], hTa[:, 8:16, tb * 128:(tb + 1) * 128], hTaR)
            P.barrier()
        with ExitStack() as es2:
            wqb = [T(es2, "wqb%d" % i, [128, KC, 128], BF16) for i in range(2)]
            wvb = [T(es2, "wvb%d" % i, [128, KC, 512], BF16) for i in range(2)]
            qo = [T(es2, "qo%d" % i, [128, 512], BF16) for i in range(3)]
            pp = [PS(es2, "pp%d" % i, [128, 512], F32) for i in range(3)]
            wqbR = [Res(), Res()]
            wvbR = [Res(), Res()]
            qoR = [Res() for _ in range(3)]
            ppR = [Res() for _ in range(3)]
            it = 0
            wn = 0
            wqv = w_qkv.rearrange("(k p) c -> p k c", p=128)
            for part in range(2):
                for c in range(16):
                    ws = wn % 2
                    wn += 1
                    col = part * D + c * 128
                    P.dma("pool", wqb[ws][:], wqv[:, :, col:col + 128], W=[wqbR[ws]])
                    tiles = [(256 + i * 512, i * 512) for i in range(4)] if part == 0 else [(i * 512, i * 512) for i in range(5)]
                    for (src0, dst0) in tiles:
                        s = it % 3
                        it += 1
                        for k in range(KC):
                            P.op("pe", lambda e, k=k, s=s, ws=ws, src0=src0: e.matmul(
                                pp[s][:, :], wqb[ws][:, k, :], hTa[:, k, src0:src0 + 512],
                                start=(k == 0), stop=(k == KC - 1)), R=[wqbR[ws], hTaR[0], hTaR[1]], W=[ppR[s]])
                        if part == 0:
                            P.op("act", lambda e, s=s: e.activation(out=qo[s][:], in_=pp[s][:, :], func=AF.Copy,
                                                                    scale=SC_NA), R=[ppR[s]], W=[qoR[s]])
                            P.dma("sp", q1T[c * 128:(c + 1) * 128, dst0:dst0 + 512], qo[s][:], R=[qoR[s]], W=[Res()])
                        else:
                            P.op("dve", lambda e, s=s: e.tensor_copy(out=qo[s][:], in_=pp[s][:, :]),
                                 R=[ppR[s]], W=[qoR[s]])
                            P.dma("sp", k1T[c * 128:(c + 1) * 128, dst0:dst0 + 512], qo[s][:], R=[qoR[s]], W=[Res()])
            for nb in range(4):
                ws = nb % 2
                P.dma("pool", wvb[ws][:], wqv[:, :, 2 * D + nb * 512:2 * D + (nb + 1) * 512], W=[wvbR[ws]])
                for tb in range(NQ // 128):
                    s = it % 3
                    it += 1
                    for k in range(KC):
                        P.op("pe", lambda e, k=k, s=s, ws=ws, tb=tb: e.matmul(
                            pp[s][:, :], hTa[:, k, tb * 128:(tb + 1) * 128], wvb[ws][:, k, :],
                            start=(k == 0), stop=(k == KC - 1)), R=[wvbR[ws], hTaR[0], hTaR[1]], W=[ppR[s]])
                    if it % 2 == 0:
                        P.op("act", lambda e, s=s: e.activation(out=qo[s][:], in_=pp[s][:, :], func=AF.Copy),
                             R=[ppR[s]], W=[qoR[s]])
                    else:
                        P.op("dve", lambda e, s=s: e.tensor_copy(out=qo[s][:], in_=pp[s][:, :]), R=[ppR[s]], W=[qoR[s]])
                    P.dma("sp", v1[4 * nb:4 * nb + 4, :, tb, 0:128].rearrange("h p d -> p h d"),
                          qo[s][:].rearrange("p (h d) -> p h d", h=4), R=[qoR[s]], W=[Res()])
            P.barrier()

    with ExitStack() as es:
        KTh = [T(es, "KTh%d" % i, [128, NQ], BF16) for i in range(2)]
        Vh = [T(es, "Vh%d" % i, [128, NQ // 128, 144], BF16) for i in range(2)]
        QTh = [T(es, "QTh%d" % i, [128, NOWN], BF16) for i in range(2)]
        tab = [[T(es, "tab%d_%d" % (i, v), [128, 6, 256], BF16) for v in range(3)] for i in range(2)]
        Pt = [T(es, "ePt%d" % i, [128, 512], BF16) for i in range(4)]
        rec = [T(es, "erec%d" % i, [128, 2], F32) for i in range(6)]
        On = [T(es, "eOn%d" % i, [128, 2, 128], BF16) for i in range(6)]
        bank = [PS(es, "ebk%d" % i, [128, 512], F32) for i in range(8)]
        bankR = [Res() for _ in range(8)]
        KThR = [Res(), Res()]
        VhR = [Res(), Res()]
        QThR = [Res(), Res()]
        tabR = [Res(), Res()]
        PtR = [Res() for _ in range(4)]
        recR = [Res() for _ in range(6)]
        OnR = [Res() for _ in range(6)]

        def load_head(h):
            s = h % 2
            P.dma("sp", KTh[s][:], k1T[h * 128:(h + 1) * 128, :], R=[k1R], W=[KThR[s]])
            P.dma("sp", Vh[s][:], v1[h], R=[v1R], W=[VhR[s]])
            P.op("pool", lambda e: e.memset(Vh[s][:, :, 128:129], 1.0), W=[VhR[s]])
            P.dma("sp", QTh[s][:], q1T[h * 128:(h + 1) * 128, :], R=[q1R], W=[QThR[s]])
            for v in range(3):
                P.dma("pool", tab[s][v][:], natab[v, h].rearrange("(c p) q -> p c q", p=128), W=[tabR[s]])

        load_head(0)
        gstep = 0
        tcount = 0
        for h in range(16):
            s = h % 2
            if h + 1 < 16:
                load_head(h + 1)
            for g in range(8):
                tv = 0 if g == 0 else (2 if g == 7 else 1)
                chunks = [(0, None), (128, None)]
                for m in range(6):
                    pos = (4 * g - 4 + 2 * m) % 36
                    chunks.append((256 + pos * 64, m))
                ab = (4, 5) if tcount % 2 == 0 else (6, 7)
                sl = tcount % 6
                tcount += 1
                n = 4
                base = gstep
                gstep += n
                q_ap = QTh[s][:, g * 256:(g + 1) * 256]

                def QK(p, chunks=chunks, base=base, q_ap=q_ap, s=s, tv=tv):
                    sb = (base + p) % 4
                    for hf in range(2):
                        tok0, m = chunks[2 * p + hf]
                        reg = bank[sb][:, hf * 256:(hf + 1) * 256]
                        P.op("pe", lambda e: e.matmul(reg, KTh[s][:, tok0:tok0 + 128], q_ap,
                                                      start=True, stop=(m is None)), R=[KThR[s], QThR[s]], W=[bankR[sb]])
                        if m is not None:
                            P.op("pe", lambda e: e.matmul(reg, identb[:], tab[s][tv][:, m, :],
                                                          start=False, stop=True), R=[RC, tabR[s]], W=[bankR[sb]])

                QK(0)
                QK(1)
                QK(2)
                for p in range(n):
                    sb = (base + p) % 4
                    ps = (base + p) % 4
                    P.op("act", lambda e, sb=sb, ps=ps: e.activation(out=Pt[ps][:], in_=bank[sb][:, :], func=AF.Exp),
                         R=[bankR[sb]], W=[PtR[ps]])
                    for hf in range(2):
                        tok0, m = chunks[2 * p + hf]
                        first = (p == 0 and hf == 0)
                        last = (p == n - 1 and hf == 1)
                        for qb in range(2):
                            P.op("pe", lambda e, tok0=tok0, ps=ps, hf=hf, qb=qb, first=first, last=last: e.matmul(
                                bank[ab[qb]][:, 0:129], Pt[ps][:, hf * 256 + qb * 128:hf * 256 + (qb + 1) * 128],
                                Vh[s][:, tok0 // 128, 0:129], start=first, stop=last),
                                R=[VhR[s], PtR[ps]], W=[bankR[ab[qb]]])
                    if p + 3 < n:
                        QK(p + 3)
                for qb in range(2):
                    bk = ab[qb]
                    P.op("dve", lambda e, sl=sl, bk=bk, qb=qb: e.reciprocal(out=rec[sl][:, qb:qb + 1],
                                                                            in_=bank[bk][:, 128:129]),
                         R=[bankR[bk]], W=[recR[sl]])
                    P.op("dve", lambda e, sl=sl, bk=bk, qb=qb: e.tensor_scalar_mul(
                        out=On[sl][:, qb, :], in0=bank[bk][:, 0:128], scalar1=rec[sl][:, qb:qb + 1]),
                        R=[bankR[bk], recR[sl]], W=[OnR[sl]])
                P.dma("sp", OT[g * 256:(g + 1) * 256, h * 128:(h + 1) * 128].rearrange("(qb p) d -> p qb d", p=128),
                      On[sl][:], R=[OnR[sl]], W=[Res()])
        P.barrier()

    mixer_out(1, w_o1, NOWN // 128, None,
              lambda blk: x2[(blk + 2) * 128:(blk + 3) * 128, :],
              lambda blk: x3[blk * 128:(blk + 1) * 128, :], x3R, x2R, False)

    ffn(1, NOWN // 512, lambda blk: x3[blk * 128:(blk + 1) * 128, :], x3R,
        lambda blk: yout[blk * 128:(blk + 1) * 128, :], Res(), False, True)

    P.barrier()
    top.close()
    return nc, P.nins


def _rope_tables(tok_idx, is_ctx):
    t = np.asarray(tok_idx)
    row = (t // 64).astype(np.float32)
    col = (t % 64).astype(np.float32)
    inv = (np.float32(10000.0) ** (-(np.arange(16, dtype=np.float32)) / np.float32(16))).astype(np.float32)
    ar = row[:, None] * inv[None, :]
    ac = col[:, None] * inv[None, :]
    cr, sr, cc, scn = np.cos(ar), np.sin(ar), np.cos(ac), np.sin(ac)
    cos4 = np.concatenate([cr, cr, cc, cc], axis=1).astype(np.float32)
    sin4 = np.concatenate([-sr, sr, -scn, scn], axis=1).astype(np.float32)
    cos4[is_ctx] = 1.0
    sin4[is_ctx] = 0.0
    return cos4, sin4


def _na_table(rel_bias, half, g):
    own0 = 32 * half
    j = np.arange(12)
    pos = (4 * g - 4 + j) % 36
    krow = np.where(pos < 32, own0 + pos, (32 if half == 0 else 28) + (pos - 32))
    i = np.arange(4)
    r = own0 + 4 * g + i
    rs = np.clip(r - 4, 0, 56)
    vrow = (krow[:, None] >= rs[None, :]) & (krow[:, None] < rs[None, :] + 8)
    dr = np.clip(krow[:, None] - r[None, :] + 7, 0, 14)
    kc = np.arange(64)
    qc = np.arange(64)
    cs = np.clip(qc - 8, 0, 48)
    vcol = (kc[:, None] >= cs[None, :]) & (kc[:, None] < cs[None, :] + 16)
    dc = np.clip(kc[:, None] - qc[None, :] + 15, 0, 30)
    vals = rel_bias[:, dr[:, None, :, None], dc[None, :, None, :]]
    valid = vrow[:, None, :, None] & vcol[None, :, None, :]
    out = np.where(valid[None], vals, np.float32(NEG)).astype(np.float32)
    return out.reshape(16, 768, 256)


_CACHE = {}


def kernel(x, c, ctx, c_ctx, ada_w, ada_b, norm_mix, norm_ffn, norm_final,
           mla_w_dq, mla_q_norm, mla_w_uq, mla_w_dkv, mla_kv_norm, mla_w_ukv, mla_w_o,
           na_w_qkv, na_rel_bias, na_w_o, ffn_w_gate, ffn_w_up, ffn_w_down):
    f = lambda a: np.ascontiguousarray(np.asarray(a, dtype=np.float32))
    x, c, ctx, c_ctx = f(x), f(c), f(ctx), f(c_ctx)
    if "nc" not in _CACHE:
        _CACHE["nc"] = build_program()[0]
    nc = _CACHE["nc"]
    shared = {
        "ident": np.eye(128, dtype=np.float32),
        "ada_w": f(ada_w), "ada_b": f(ada_b), "norm_mix": f(norm_mix), "norm_ffn": f(norm_ffn),
        "norm_final": f(norm_final).reshape(1, D),
        "mla_w_dq": f(mla_w_dq)[0], "mla_q_norm": f(mla_q_norm).reshape(1, 512), "mla_w_uq": f(mla_w_uq)[0],
        "mla_w_dkv": f(mla_w_dkv)[0], "mla_kv_norm": f(mla_kv_norm).reshape(1, 512), "mla_w_ukv": f(mla_w_ukv)[0],
        "mla_w_o": f(mla_w_o)[0], "na_w_qkv": f(na_w_qkv)[0], "na_w_o": f(na_w_o)[0],
        "ffn_w_gate": f(ffn_w_gate), "ffn_w_up": f(ffn_w_up), "ffn_w_down": f(ffn_w_down),
    }
    rb = f(na_rel_bias)[0]
    tabs = {}
    for half in range(2):
        t0 = _na_table(rb, half, 0)
        t1 = _na_table(rb, half, 1)
        t7 = _na_table(rb, half, 7)
        tabs[half] = np.ascontiguousarray(np.stack([t0, t1, t7], axis=0))
    in_maps = []
    orders = []
    for core in range(8):
        b, half = core // 2, core % 2
        if half == 0:
            own = np.arange(0, 2048)
            halo = np.arange(2048, 2304)
            rest = np.arange(2304, 4096)
        else:
            own = np.arange(2048, 4096)
            halo = np.arange(1792, 2048)
            rest = np.arange(0, 1792)
        lat = np.concatenate([own, halo, rest])
        orders.append(own)
        xk = np.ascontiguousarray(np.concatenate([ctx[b], x[b][lat]], axis=0))
        tok = np.concatenate([np.zeros(256, dtype=np.int64), lat])
        is_ctx = np.zeros(NK, dtype=bool)
        is_ctx[:256] = True
        cos4, sin4 = _rope_tables(tok, is_ctx)
        m = dict(shared)
        m["xk"] = xk
        m["cvec"] = np.ascontiguousarray(np.stack([c[b], c_ctx], axis=0))
        m["ropek"] = np.ascontiguousarray(np.concatenate([cos4, sin4], axis=1))
        m["ropeqc"] = np.ascontiguousarray(cos4[:NQ].T)
        m["ropeqs"] = np.ascontiguousarray(sin4[:NQ].T)
        m["natab"] = tabs[half]
        in_maps.append(m)
    res = run_bass_kernel_spmd(nc, in_maps, core_ids=list(range(8)))
    out = np.empty((4, 4096, D), dtype=np.float32)
    for core in range(8):
        b = core // 2
        out[b, orders[core]] = np.asarray(res.results[core]["y"], dtype=np.float32)
    return out
```

```python
import numpy as np
from contextlib import ExitStack
import concourse.bass as bass
import concourse.mybir as mybir
from concourse.bass_utils import run_bass_kernel_spmd

F32 = mybir.dt.float32
BF16 = mybir.dt.bfloat16
AF = mybir.ActivationFunctionType
ALU = mybir.AluOpType

D = 2048
KC = 16
FF = 5632
FC = 44
NQ = 2560
NK = 4352
NOWN = 2048
EPS = 1e-6
SC_MLA = 192 ** -0.5
SC_NA = 128 ** -0.5
NEG = -30000.0
NDS = 24


class Res:
    __slots__ = ("w", "r")

    def __init__(self):
        self.w = None
        self.r = {}


class Prog:
    def __init__(self, nc, es):
        self.nc = nc
        self.E = {"pe": nc.tensor, "act": nc.scalar, "dve": nc.vector, "pool": nc.gpsimd, "sp": nc.sync}
        self.semobj = {}
        self.cnt = {}
        for e in ("pe", "act", "dve", "pool"):
            self.semobj["c_" + e] = es.enter_context(nc.semaphore("c_" + e))
            self.cnt[e] = 0
        self.dq = {}
        for q in ("sp", "pool"):
            keys = []
            for i in range(NDS):
                k = "d_%s%d" % (q, i)
                self.semobj[k] = es.enter_context(nc.semaphore(k))
                keys.append(k)
            self.dq[q] = keys
        self.dcnt = {}
        self.dsrc = {}
        self.drr = {"sp": 0, "pool": 0}
        self.seen = {e: {} for e in self.E}
        self.nins = 0

    def _wait(self, eng, tok):
        key, val, src = tok
        if src == eng and eng == "pe":
            return
        if self.seen[eng].get(key, 0) >= val:
            return
        self.E[eng].wait_ge(self.semobj[key], val)
        self.seen[eng][key] = val
        self.nins += 1

    def _deps(self, eng, R, W):
        for r in R:
            if r.w is not None:
                self._wait(eng, r.w)
        for w in W:
            if w.w is not None:
                self._wait(eng, w.w)
            for t in w.r.values():
                self._wait(eng, t)

    def _commit(self, tok, R, W):
        for r in R:
            old = r.r.get(tok[0])
            if old is None or old[1] < tok[1]:
                r.r[tok[0]] = tok
        for w in W:
            w.w = tok
            w.r = {}

    def op(self, eng, fn, R=(), W=()):
        self._deps(eng, R, W)
        ins = fn(self.E[eng])
        self.cnt[eng] += 1
        ins.then_inc(self.semobj["c_" + eng], 1)
        self._commit(("c_" + eng, self.cnt[eng], eng), R, W)
        self.nins += 1

    def dma(self, q, out, in_, R=(), W=()):
        keys = self.dq[q]
        i = self.drr[q]
        self.drr[q] = (i + 1) % len(keys)
        key = keys[i]
        c = self.dcnt.get(key, 0)
        if c > 0:
            self._wait(q, (key, 16 * c, q))
        self._deps(q, R, W)
        ins = self.E[q].dma_start(out=out, in_=in_)
        self.dcnt[key] = c + 1
        ins.then_inc(self.semobj[key], 16)
        self._commit((key, 16 * (c + 1), q), R, W)
        self.nins += 1

    def barrier(self):
        toks = [("c_" + e, self.cnt[e], e) for e in self.cnt if self.cnt[e] > 0]
        for q in self.dq:
            for k in self.dq[q]:
                if self.dcnt.get(k, 0) > 0:
                    toks.append((k, 16 * self.dcnt[k], q))
        for eng in self.E:
            for t in toks:
                self._wait(eng, t)


def build_program():
    nc = bass.Bass("TRN2", target_bir_lowering=False)

    def din(name, shape):
        return nc.dram_tensor(name, list(shape), F32, kind="ExternalInput").ap()

    def dscr(name, shape, dt):
        return nc.dram_tensor(name, list(shape), dt, kind="Internal").ap()

    xk = din("xk", [NK, D])
    cvec = din("cvec", [2, D])
    ropek = din("ropek", [NK, 128])
    ropeqc = din("ropeqc", [64, NQ])
    ropeqs = din("ropeqs", [64, NQ])
    identd = din("ident", [128, 128])
    ada_w = din("ada_w", [2, D, 6 * D])
    ada_b = din("ada_b", [2, 6 * D])
    norm_mix = din("norm_mix", [2, D])
    norm_ffn = din("norm_ffn", [2, D])
    norm_final = din("norm_final", [1, D])
    w_dq = din("mla_w_dq", [D, 512])
    q_norm = din("mla_q_norm", [1, 512])
    w_uq = din("mla_w_uq", [512, 3072])
    w_dkv = din("mla_w_dkv", [D, 576])
    kv_norm = din("mla_kv_norm", [1, 512])
    w_ukv = din("mla_w_ukv", [512, 4096])
    w_o0 = din("mla_w_o", [D, D])
    w_qkv = din("na_w_qkv", [D, 3 * D])
    natab = din("natab", [3, 16, 768, 256])
    w_o1 = din("na_w_o", [D, D])
    w_gate = din("ffn_w_gate", [2, D, FF])
    w_up = din("ffn_w_up", [2, D, FF])
    w_down = din("ffn_w_down", [2, FF, D])
    yout = nc.dram_tensor("y", [NOWN, D], F32, kind="ExternalOutput").ap()

    mod = dscr("mod", [2, 2, 6 * D], F32)
    OT = dscr("OT", [NQ, D], BF16)
    x1 = dscr("x1", [NQ, D], F32)
    x2 = dscr("x2", [NQ, D], F32)
    q1T = dscr("q1T", [D, NOWN], BF16)
    k1T = dscr("k1T", [D, NQ], BF16)
    v1 = dscr("v1", [16, 128, NQ // 128, 144], BF16)
    x3 = dscr("x3", [NOWN, D], F32)
    wgb = dscr("wgb", [2, D, FF], BF16)
    wub = dscr("wub", [2, D, FF], BF16)
    wdb = dscr("wdb", [2, FF, D], BF16)

    top = ExitStack()
    P = Prog(nc, top)

    uid = [0]

    def T(es, name, shape, dt):
        uid[0] += 1
        return es.enter_context(nc.sbuf_tensor("%s_%d" % (name, uid[0]), list(shape), dt))

    def PS(es, name, shape, dt):
        uid[0] += 1
        return es.enter_context(nc.psum_tensor("%s_%d" % (name, uid[0]), list(shape), dt))

    identb = T(top, "identb", [128, 128], BF16)
    onesb = T(top, "onesb", [128, 128], BF16)
    RC = Res()
    P.dma("pool", identb[:], identd, W=[RC])
    P.op("dve", lambda e: e.memset(onesb[:], 1.0), W=[RC])

    modR = [Res(), Res()]
    OTR = Res()
    x1R = Res()
    x2R = Res()
    q1R = Res()
    k1R = Res()
    v1R = Res()
    x3R = Res()
    wcastR = [[Res() for _ in range(3)] for _ in range(2)]

    def precast_chunk(l, i):
        P.dma("pool", wgb[l, i * 128:(i + 1) * 128, :], w_gate[l, i * 128:(i + 1) * 128, :], W=[Res()])
        P.dma("pool", wub[l, i * 128:(i + 1) * 128, :], w_up[l, i * 128:(i + 1) * 128, :], W=[Res()])
        P.dma("pool", wdb[l, i * 352:(i + 1) * 352, :], w_down[l, i * 352:(i + 1) * 352, :], W=[Res()])

    def rms_stats(src, n, st, stR, c, srcR, junk):
        P.op("act", lambda e: e.activation(out=junk, in_=src, func=AF.Square, accum_out=st[:, c:c + 1]),
             R=[srcR], W=[stR])
        P.op("act", lambda e: e.activation(out=st[:, c + 1:c + 2], in_=st[:, c:c + 1], func=AF.Sqrt,
                                           scale=1.0 / n, bias=EPS), R=[stR], W=[stR])
        P.op("dve", lambda e: e.reciprocal(out=st[:, c + 2:c + 3], in_=st[:, c + 1:c + 2]), R=[stR], W=[stR])

    def load_bc(dst, dstR, src_row):
        P.dma("sp", dst, src_row.partition_broadcast(128), W=[dstR])

    def modrow(l, s, i):
        return mod[l, s:s + 1, i * D:(i + 1) * D]

    def make_A(A, AR, gw, gwR):
        P.op("dve", lambda e: e.scalar_tensor_tensor(out=A, in0=A, scalar=1.0, in1=gw, op0=ALU.add, op1=ALU.mult),
             R=[gwR], W=[AR])

    sT = T(top, "sT", [128, 32], BF16)
    sTR = Res()
    adak = [T(top, "adak%d" % i, [2, 512], F32) for i in range(2)]
    msk = [T(top, "msk%d" % i, [2, 512], F32) for i in range(2)]
    adakR = [Res(), Res()]
    mskR = [Res(), Res()]
    adan = [0]

    def ada_load(l, nb, wb, wbR):
        s = adan[0] % len(wb)
        adan[0] += 1
        P.dma("pool", wb[s][:], ada_w[l, :, nb * 512:(nb + 1) * 512].rearrange("(k p) c -> p k c", p=128),
              W=[wbR[s]])
        return s

    def ada_compute(l, nb, s, wb, wbR, pm, pmR):
        a = s % 2
        P.dma("sp", adak[a][:], ada_b[l:l + 1, nb * 512:(nb + 1) * 512].partition_broadcast(2), W=[adakR[a]])
        for k in range(KC):
            P.op("pe", lambda e, k=k: e.matmul(pm[0:2, :], sT[:, 2 * k:2 * k + 2], wb[s][:, k, :],
                                               start=(k == 0), stop=(k == KC - 1)), R=[sTR, wbR[s]], W=[pmR])
        P.op("dve", lambda e: e.tensor_tensor(out=msk[a][:], in0=pm[0:2, :], in1=adak[a][:], op=ALU.add),
             R=[pmR, adakR[a]], W=[mskR[a]])
        P.dma("sp", mod[l, :, nb * 512:(nb + 1) * 512], msk[a][:], R=[mskR[a]], W=[Res()])

    def ada_block(l, nb, wb, wbR, pm, pmR):
        s = ada_load(l, nb, wb, wbR)
        ada_compute(l, nb, s, wb, wbR, pm, pmR)

    es0 = ExitStack()
    ckvT = T(es0, "ckvT", [128, 4, NK], BF16)
    kpeT = T(es0, "kpeT", [128, NK], BF16)
    cqT = T(es0, "cqT", [128, 4, NQ], BF16)
    ckvR, kpeR, cqR = Res(), Res(), Res()
    P.op("dve", lambda e: e.memset(kpeT[:], 0.0), W=[kpeR])
    esW = ExitStack()
    wdkv = T(esW, "wdkv", [128, KC, 576], BF16)
    wdq = T(esW, "wdq", [128, KC, 512], BF16)
    wdkvR, wdqR = Res(), Res()

    with ExitStack() as es:
        cv = T(es, "cv", [2, D], F32)
        cs = T(es, "cs", [2, D], BF16)
        wb = [T(es, "adaw%d" % i, [128, KC, 512], BF16) for i in range(2)]
        pm = [PS(es, "pm%d" % i, [128, 512], F32) for i in range(2)]
        pt = PS(es, "pt", [128, 32], BF16)
        cvR, csR, ptR = Res(), Res(), Res()
        wbR = [Res(), Res()]
        pmR = [Res(), Res()]
        P.dma("sp", cv[:], cvec, W=[cvR])
        P.op("act", lambda e: e.activation(out=cs[:], in_=cv[:], func=AF.Silu), R=[cvR], W=[csR])
        for k in range(KC):
            P.op("pe", lambda e, k=k: e.transpose(pt[:, 2 * k:2 * k + 2], cs[0:2, k * 128:(k + 1) * 128],
                                                   identb[0:2, 0:2]), R=[csR, RC], W=[ptR])
        P.op("dve", lambda e: e.tensor_copy(out=sT[:], in_=pt[:]), R=[ptR], W=[sTR])
        for nb in range(8):
            ada_block(0, nb, wb, wbR, pm[nb % 2], pmR[nb % 2])
        P.dma("pool", wdkv[:], w_dkv.rearrange("(k p) c -> p k c", p=128), W=[wdkvR])
        P.dma("pool", wdq[:], w_dq.rearrange("(k p) c -> p k c", p=128), W=[wdqR])
        P.barrier()
    ada_todo = [(0, nb) for nb in range(8, 24)] + [(1, nb) for nb in range(24)]


    nrm_n = [0]
    NSPL = 896

    def norm_s1(xt, xR, A, AR, B, BR, tmp, tmpR, hbs, hbRs, junk, st, stR):
        i = nrm_n[0] % 2
        nrm_n[0] += 1
        hb = hbs[i]
        rms_stats(xt, D, st, stR, 0, xR, junk)
        P.op("dve", lambda e: e.scalar_tensor_tensor(out=tmp, in0=xt, scalar=st[:, 2:3], in1=A,
                                                     op0=ALU.mult, op1=ALU.mult), R=[xR, stR, AR], W=[tmpR])
        P.op("pool", lambda e: e.tensor_tensor(out=hb[:, 0:NSPL], in0=tmp[:, 0:NSPL], in1=B[:, 0:NSPL], op=ALU.add),
             R=[tmpR, BR], W=[hbRs[i][0]])
        P.op("dve", lambda e: e.tensor_tensor(out=hb[:, NSPL:D], in0=tmp[:, NSPL:D], in1=B[:, NSPL:D], op=ALU.add),
             R=[tmpR, BR], W=[hbRs[i][1]])
        return i

    def norm_s2(i, hbs, hbRs, ptr, ptrR, hT_lo, hT_hi, hTR):
        hb = hbs[i]
        for k in range(KC):
            pi = k // 8
            P.op("pe", lambda e, k=k, pi=pi: e.transpose(ptr[pi][:, (k % 8) * 128:(k % 8 + 1) * 128],
                                                          hb[:, k * 128:(k + 1) * 128], identb[:]),
                 R=[hbRs[i][0], hbRs[i][1], RC], W=[ptrR[pi]])
        P.op("act", lambda e: e.activation(out=hT_lo, in_=ptr[0][:].rearrange("p (k t) -> p k t", k=8), func=AF.Copy),
             R=[ptrR[0]], W=[hTR[0]])
        P.op("act", lambda e: e.activation(out=hT_hi, in_=ptr[1][:].rearrange("p (k t) -> p k t", k=8), func=AF.Copy),
             R=[ptrR[1]], W=[hTR[1]])

    with ExitStack() as es:
        xin = [T(es, "xin%d" % i, [128, D], F32) for i in range(3)]
        junk = T(es, "junk", [128, D], BF16)
        tmp = T(es, "tmp", [128, D], F32)
        hb = [T(es, "hb%d" % i, [128, D], BF16) for i in range(2)]
        hT = [T(es, "hT%d" % i, [128, KC, 128], BF16) for i in range(2)]
        bcA = [T(es, "bcA%d" % i, [128, D], F32) for i in range(2)]
        bcB = [T(es, "bcB%d" % i, [128, D], F32) for i in range(2)]
        kvn = T(es, "kvn", [128, 512], F32)
        qn = T(es, "qn", [128, 512], F32)
        rk = [T(es, "rk%d" % i, [128, 128], F32) for i in range(2)]
        st = [T(es, "st%d" % i, [128, 16], F32) for i in range(2)]
        ckb = T(es, "ckb", [128, 512], BF16)
        cqb = T(es, "cqb", [128, 512], BF16)
        kpb = T(es, "kpb", [128, 64], BF16)
        t1 = T(es, "t1", [128, 64], F32)
        t2 = T(es, "t2", [128, 64], F32)
        ptr = [PS(es, "ptr%d" % i, [128, 1024], BF16) for i in range(2)]
        pkv1 = PS(es, "pkv1", [128, 512], F32)
        pkv2 = PS(es, "pkv2", [128, 512], F32)
        pq = PS(es, "pq", [128, 512], F32)
        ptc = PS(es, "ptc", [128, 1024], BF16)
        ptc2 = PS(es, "ptc2", [128, 1024], BF16)
        xinR = [Res() for _ in range(3)]
        tmpR, hbR = Res(), [[Res(), Res()], [Res(), Res()]]
        hTR = [[Res(), Res()], [Res(), Res()]]
        bcAR = [Res(), Res()]
        bcBR = [Res(), Res()]
        kvnR, qnR = Res(), Res()
        rkR = [Res(), Res()]
        stR = [Res(), Res()]
        ckbR, cqbR, kpbR, t1R, t2R = Res(), Res(), Res(), Res(), Res()
        ptrR = [Res(), Res()]
        pkv1R, pkv2R, pqR, ptcR, ptc2R = Res(), Res(), Res(), Res(), Res()

        load_bc(kvn[:], kvnR, kv_norm)
        load_bc(qn[:], qnR, q_norm)
        load_bc(tmp[:], tmpR, norm_mix[0:1, :])
        for s in range(2):
            load_bc(bcA[s][:], bcAR[s], modrow(0, s, 1))
            load_bc(bcB[s][:], bcBR[s], modrow(0, s, 0))
            make_A(bcA[s][:], bcAR[s], tmp[:], tmpR)

        NB_A = NK // 128
        rk3 = [rk[0], rk[1], T(es, "rk2", [128, 128], F32)]
        rk3R = [rkR[0], rkR[1], Res()]
        stB = [T(es, "stB%d" % i, [128, 16], F32) for i in range(2)]
        stBR = [Res(), Res()]

        def A_S1(tb):
            s3 = tb % 3
            sc = 1 if tb < 2 else 0
            P.dma("sp", xin[s3][:], xk[tb * 128:(tb + 1) * 128, :], W=[xinR[s3]])
            P.dma("sp", rk3[s3][:], ropek[tb * 128:(tb + 1) * 128, :], W=[rk3R[s3]])
            return norm_s1(xin[s3][:], xinR[s3], bcA[sc][:], bcAR[sc], bcB[sc][:], bcBR[sc], tmp[:], tmpR, hb, hbR,
                           junk[:], st[tb % 2], stR[tb % 2])

        def A_S2a(tb, hi):
            s2 = tb % 2
            norm_s2(hi, hb, hbR, ptr, ptrR, hT[s2][:, 0:8, :], hT[s2][:, 8:16, :], hTR[s2])

        def A_S2b(tb):
            s2 = tb % 2
            for k in range(KC):
                P.op("pe", lambda e, k=k: e.matmul(pkv1[:, :], hT[s2][:, k, :], wdkv[:, k, 0:512],
                                                   start=(k == 0), stop=(k == KC - 1)),
                     R=[hTR[s2][0], hTR[s2][1], wdkvR], W=[pkv1R])
            for k in range(KC):
                P.op("pe", lambda e, k=k: e.matmul(pkv2[:, 0:64], hT[s2][:, k, :], wdkv[:, k, 512:576],
                                                   start=(k == 0), stop=(k == KC - 1)),
                     R=[hTR[s2][0], hTR[s2][1], wdkvR], W=[pkv2R])
            if tb < NQ // 128:
                for k in range(KC):
                    P.op("pe", lambda e, k=k: e.matmul(pq[:, :], hT[s2][:, k, :], wdq[:, k, :],
                                                       start=(k == 0), stop=(k == KC - 1)),
                         R=[hTR[s2][0], hTR[s2][1], wdqR], W=[pqR])

        def A_S3(tb):
            s2 = tb % 2
            s3 = tb % 3
            sB = stB[s2]
            sBR = stBR[s2]
            rkt = rk3[s3]
            rktR = rk3R[s3]
            rms_stats(pkv1[:, :], 512, sB, sBR, 3, pkv1R, junk[:, 0:512])
            P.op("dve", lambda e: e.scalar_tensor_tensor(out=ckb[:], in0=pkv1[:, :], scalar=sB[:, 5:6],
                                                         in1=kvn[:], op0=ALU.mult, op1=ALU.mult),
                 R=[pkv1R, sBR, kvnR], W=[ckbR])
            for c in range(4):
                P.op("pe", lambda e, c=c: e.transpose(ptc[:, c * 128:(c + 1) * 128], ckb[:, c * 128:(c + 1) * 128],
                                                      identb[:]), R=[ckbR, RC], W=[ptcR])
            kp4 = pkv2[:, 0:64].rearrange("p (g h f) -> p g h f", g=2, h=2)
            sn4 = rkt[:, 64:128].rearrange("p (g h f) -> p g h f", g=2, h=2)
            t24 = t2[:].rearrange("p (g h f) -> p g h f", g=2, h=2)
            P.op("dve", lambda e: e.tensor_tensor(out=t1[:], in0=pkv2[:, 0:64], in1=rkt[:, 0:64], op=ALU.mult),
                 R=[pkv2R, rktR], W=[t1R])
            P.op("dve", lambda e: e.tensor_tensor(out=t24[:, :, 0, :], in0=kp4[:, :, 1, :], in1=sn4[:, :, 0, :],
                                                  op=ALU.mult), R=[pkv2R, rktR], W=[t2R])
            P.op("dve", lambda e: e.tensor_tensor(out=t24[:, :, 1, :], in0=kp4[:, :, 0, :], in1=sn4[:, :, 1, :],
                                                  op=ALU.mult), R=[pkv2R, rktR], W=[t2R])
            P.op("dve", lambda e: e.tensor_tensor(out=kpb[:], in0=t1[:], in1=t2[:], op=ALU.add),
                 R=[t1R, t2R], W=[kpbR])
            P.op("pe", lambda e: e.transpose(ptc[0:64, 512:640], kpb[:, :], identb[:]), R=[kpbR, RC], W=[ptcR])
            P.op("act", lambda e: e.activation(out=ckvT[:, :, tb * 128:(tb + 1) * 128],
                                               in_=ptc[:, 0:512].rearrange("p (c t) -> p c t", c=4),
                                               func=AF.Copy), R=[ptcR], W=[ckvR])
            P.op("act", lambda e: e.activation(out=kpeT[0:64, tb * 128:(tb + 1) * 128], in_=ptc[0:64, 512:640],
                                               func=AF.Copy), R=[ptcR], W=[kpeR])
            if tb < NQ // 128:
                rms_stats(pq[:, :], 512, sB, sBR, 6, pqR, junk[:, 512:1024])
                P.op("dve", lambda e: e.scalar_tensor_tensor(out=cqb[:], in0=pq[:, :], scalar=sB[:, 8:9],
                                                             in1=qn[:], op0=ALU.mult, op1=ALU.mult),
                     R=[pqR, sBR, qnR], W=[cqbR])
                for c in range(4):
                    P.op("pe", lambda e, c=c: e.transpose(ptc2[:, c * 128:(c + 1) * 128],
                                                          cqb[:, c * 128:(c + 1) * 128], identb[:]),
                         R=[cqbR, RC], W=[ptc2R])
                P.op("dve", lambda e: e.tensor_copy(out=cqT[:, :, tb * 128:(tb + 1) * 128],
                                                    in_=ptc2[:, 0:512].rearrange("p (c t) -> p c t", c=4)),
                     R=[ptc2R], W=[cqR])

        hslot = {0: A_S1(0)}
        for tb in range(NB_A):
            if tb + 1 < NB_A:
                hslot[tb + 1] = A_S1(tb + 1)
            A_S2a(tb, hslot[tb])
            if tb > 0:
                A_S3(tb - 1)
            A_S2b(tb)
        A_S3(NB_A - 1)
        P.barrier()

    esW.close()

    with ExitStack() as es:
        KT = T(es, "KT", [128, NK], BF16)
        V = T(es, "V", [128, NK // 128, 129], BF16)
        qT = T(es, "qT", [128, NQ], BF16)
        qpeT = T(es, "qpeT", [128, NQ], BF16)
        rqc = T(es, "rqc", [64, NQ], F32)
        rqs = T(es, "rqs", [64, NQ], F32)
        wq = [T(es, "wq%d" % i, [128, 4, 192], BF16) for i in range(2)]
        wqs = [T(es, "wqs%d" % i, [128, 4, 64], BF16) for i in range(2)]
        wkv = [T(es, "wkv%d" % i, [128, 4, 256], BF16) for i in range(2)]
        Pt = [T(es, "Pt%d" % i, [128, 512], BF16) for i in range(4)]
        rec = [T(es, "rec%d" % i, [128, 4], F32) for i in range(2)]
        On = [T(es, "On%d" % i, [128, 4, 128], BF16) for i in range(2)]
        r1 = T(es, "r1", [64, 512], F32)
        r2 = T(es, "r2", [64, 512], F32)
        bank = [PS(es, "bk%d" % i, [128, 512], F32) for i in range(8)]
        bankR = [Res() for _ in range(8)]
        KTR, VR, qTR, qpeR, rqR = Res(), Res(), Res(), Res(), Res()
        wqR = [Res(), Res()]
        wqsR = [Res(), Res()]
        wkvR = [Res(), Res()]
        PtR = [Res() for _ in range(4)]
        recR = [Res(), Res()]
        OnR = [Res(), Res()]
        r1R, r2R = Res(), Res()
        Sb = [0, 1, 2, 3]
        accb = [(4, 5), (6, 7)]
        pA, pB, pC = 3, 0, 1

        P.op("dve", lambda e: e.memset(qpeT[:], 0.0), W=[qpeR])
        P.op("dve", lambda e: e.memset(V[:, :, 128:129], 1.0), W=[VR])
        P.dma("sp", rqc[:], ropeqc, W=[rqR])
        P.dma("sp", rqs[:], ropeqs, W=[rqR])

        def load_head_w(h):
            s = h % 2
            P.dma("pool", wq[s][:], w_uq[:, h * 192:(h + 1) * 192].rearrange("(k p) c -> p k c", p=128), W=[wqR[s]])
            P.dma("pool", wkv[s][:], w_ukv[:, h * 256:(h + 1) * 256].rearrange("(k p) c -> p k c", p=128), W=[wkvR[s]])

        load_head_w(0)
        evn = [0]
        adawb = [T(es, "adawB%d" % i, [128, KC, 512], BF16) for i in range(3)]
        adawbR = [Res(), Res(), Res()]

        def evac(out, in_, R, W):
            evn[0] += 1
            if evn[0] % 2 == 0:
                P.op("act", lambda e: e.activation(out=out, in_=in_, func=AF.Copy), R=R, W=W)
            else:
                P.op("dve", lambda e: e.tensor_copy(out=out, in_=in_), R=R, W=W)

        gstep = [0]
        tcount = [0]
        for h in range(16):
            s = h % 2
            if h + 1 < 16:
                load_head_w(h + 1)
            ada_now = [(al, anb, ada_load(al, anb, adawb, adawbR)) for (al, anb) in ada_todo[h * 40 // 16:(h + 1) * 40 // 16]]
            precast_chunk(0, h)
            precast_chunk(1, h)
            for (d0, s0) in ((0, 144), (16, 128), (32, 176), (48, 160)):
                P.op("act", lambda e, d0=d0, s0=s0, s=s: e.activation(out=wqs[s][:, :, d0:d0 + 16],
                                                                      in_=wq[s][:, :, s0:s0 + 16], func=AF.Copy),
                     R=[wqR[s]], W=[wqsR[s]])
            for nt in range(9):
                w = 512 if nt < 8 else 256
                b = pA if nt % 2 == 0 else pB
                for k in range(4):
                    P.op("pe", lambda e, k=k, b=b, w=w, nt=nt, s=s: e.matmul(bank[b][:, 0:w], wkv[s][:, k, 0:128],
                                                                             ckvT[:, k, nt * 512:nt * 512 + w],
                                                                             start=(k == 0), stop=(k == 3)),
                         R=[wkvR[s], ckvR], W=[bankR[b]])
                evac(KT[:, nt * 512:nt * 512 + w], bank[b][:, 0:w], [bankR[b]], [KTR])
            for vg in range(9):
                n = 4 if vg < 8 else 2
                b = pA if vg % 2 == 1 else pB
                for ci in range(n):
                    kc = vg * 4 + ci
                    for k in range(4):
                        P.op("pe", lambda e, k=k, b=b, ci=ci, kc=kc, s=s: e.matmul(
                            bank[b][:, ci * 128:(ci + 1) * 128], ckvT[:, k, kc * 128:(kc + 1) * 128],
                            wkv[s][:, k, 128:256], start=(k == 0), stop=(k == 3)),
                            R=[wkvR[s], ckvR], W=[bankR[b]])
                evac(V[:, vg * 4:vg * 4 + n, 0:128], bank[b][:, 0:n * 128].rearrange("p (c d) -> p c d", c=n),
                     [bankR[b]], [VR])
            for qt in range(5):
                c0 = qt * 512
                for k in range(4):
                    P.op("pe", lambda e, k=k, c0=c0, s=s: e.matmul(bank[pA][:, :], wq[s][:, k, 0:128],
                                                                   cqT[:, k, c0:c0 + 512], start=(k == 0), stop=(k == 3)),
                         R=[wqR[s], cqR], W=[bankR[pA]])
                evac(qT[:, c0:c0 + 512], bank[pA][:, :], [bankR[pA]], [qTR])
                for k in range(4):
                    P.op("pe", lambda e, k=k, c0=c0, s=s: e.matmul(bank[pB][0:64, :], wq[s][:, k, 128:192],
                                                                   cqT[:, k, c0:c0 + 512], start=(k == 0), stop=(k == 3)),
                         R=[wqR[s], cqR], W=[bankR[pB]])
                for k in range(4):
                    P.op("pe", lambda e, k=k, c0=c0, s=s: e.matmul(bank[pC][0:64, :], wqs[s][:, k, :],
                                                                   cqT[:, k, c0:c0 + 512], start=(k == 0), stop=(k == 3)),
                         R=[wqsR[s], cqR], W=[bankR[pC]])
                P.op("dve", lambda e, c0=c0: e.tensor_tensor(out=r1[:], in0=bank[pB][0:64, :], in1=rqc[:, c0:c0 + 512],
                                                             op=ALU.mult), R=[bankR[pB], rqR], W=[r1R])
                P.op("dve", lambda e, c0=c0: e.tensor_tensor(out=r2[:], in0=bank[pC][0:64, :], in1=rqs[:, c0:c0 + 512],
                                                             op=ALU.mult), R=[bankR[pC], rqR], W=[r2R])
                P.op("pool", lambda e, c0=c0: e.tensor_tensor(out=qpeT[0:64, c0:c0 + 512], in0=r1[:], in1=r2[:], op=ALU.add),
                     R=[r1R, r2R], W=[qpeR])
            qtiles = [(0, 256, [0, 1])] + [(256 + i * 512, 512, list(range(34))) for i in range(4)] + \
                     [(2304, 256, list(range(34)))]
            for (q0, w, kcs) in qtiles:
                ti = tcount[0]
                tcount[0] += 1
                aO, aS = accb[ti % 2]
                n = len(kcs)
                base = gstep[0]
                gstep[0] += n

                def QK(i, q0=q0, w=w, kcs=kcs, base=base):
                    kc = kcs[i]
                    sb = Sb[(base + i) % 4]
                    P.op("pe", lambda e: e.matmul(bank[sb][:, 0:w], KT[:, kc * 128:(kc + 1) * 128], qT[:, q0:q0 + w],
                                                  start=True, stop=False), R=[KTR, qTR], W=[bankR[sb]])
                    P.op("pe", lambda e: e.matmul(bank[sb][:, 0:w], kpeT[:, kc * 128:(kc + 1) * 128],
                                                  qpeT[:, q0:q0 + w], start=False, stop=True),
                         R=[kpeR, qpeR], W=[bankR[sb]])

                for i0 in range(min(3, n)):
                    QK(i0)
                for i in range(n):
                    kc = kcs[i]
                    sb = Sb[(base + i) % 4]
                    ps = (base + i) % 4
                    P.op("act", lambda e, sb=sb, ps=ps, w=w: e.activation(out=Pt[ps][:, 0:w], in_=bank[sb][:, 0:w],
                                                                         func=AF.Exp, scale=SC_MLA),
                         R=[bankR[sb]], W=[PtR[ps]])
                    for qb in range(w // 128):
                        bk = 4 + qb
                        off = 0
                        P.op("pe", lambda e, kc=kc, ps=ps, i=i, n=n, bk=bk, off=off, qb=qb: e.matmul(
                            bank[bk][:, off:off + 129], Pt[ps][:, qb * 128:(qb + 1) * 128], V[:, kc, :],
                            start=(i == 0), stop=(i == n - 1)), R=[VR, PtR[ps]], W=[bankR[bk]])
                    if i + 3 < n:
                        QK(i + 3)
                sl = ti % 2
                nqb = w // 128
                for qb in range(nqb):
                    bk = 4 + qb
                    off = 0
                    P.op("dve", lambda e, sl=sl, bk=bk, off=off, qb=qb: e.reciprocal(
                        out=rec[sl][:, qb:qb + 1], in_=bank[bk][:, off + 128:off + 129]), R=[bankR[bk]], W=[recR[sl]])
                    P.op("dve", lambda e, sl=sl, bk=bk, off=off, qb=qb: e.tensor_scalar_mul(
                        out=On[sl][:, qb, :], in0=bank[bk][:, off:off + 128], scalar1=rec[sl][:, qb:qb + 1]),
                        R=[bankR[bk], recR[sl]], W=[OnR[sl]])
                P.dma("sp", OT[q0:q0 + w, h * 128:(h + 1) * 128].rearrange("(qb p) d -> p qb d", p=128),
                      On[sl][:, 0:nqb, :], R=[OnR[sl]], W=[Res()])
            for (al, anb, asl) in ada_now:
                ada_compute(al, anb, asl, adawb, adawbR, bank[pA], bankR[pA])
        P.barrier()
    es0.close()

    def mixer_out(l, w_o, nblk, ot_src, x_src, x_dst, x_dstR, x_srcR, has_ctx):
        with ExitStack() as es:
            wo = T(es, "wo", [128, KC, D], BF16)
            xb = [T(es, "xb%d" % i, [128, D], F32) for i in range(3)]
            xo = [T(es, "xo%d" % i, [128, D], F32) for i in range(2)]
            tp = [T(es, "tp%d" % i, [128, 512], F32) for i in range(2)]
            G = [T(es, "G%d" % i, [128, D], F32) for i in range(2)]
            py = [PS(es, "py%d" % i, [128, 512], F32) for i in range(4)]
            woR = [Res() for _ in range(4)]
            xbR = [Res() for _ in range(3)]
            xoR = [[Res() for _ in range(4)] for _ in range(2)]
            tpR = [Res(), Res()]
            GR = [Res(), Res()]
            pyR = [Res() for _ in range(4)]
            if ot_src is None:
                ob = [T(es, "ob%d" % i, [128, D], BF16) for i in range(3)]
                ot = [T(es, "ot%d" % i, [128, KC, 128], BF16) for i in range(2)]
                ptr = [PS(es, "mptr%d" % i, [128, 1024], BF16) for i in range(2)]
                obR = [Res() for _ in range(3)]
                otR = [[Res(), Res()], [Res(), Res()]]
                ptrR = [Res(), Res()]
            load_bc(G[0][:], GR[0], modrow(l, 0, 2))
            if has_ctx:
                load_bc(G[1][:], GR[1], modrow(l, 1, 2))
            for nb in range(4):
                P.dma("pool", wo[:, :, nb * 512:(nb + 1) * 512],
                      w_o[:, nb * 512:(nb + 1) * 512].rearrange("(k p) c -> p k c", p=128), W=[woR[nb]])

            def loads(blk):
                if ot_src is None:
                    P.dma("sp", ob[blk % 3][:], OT[blk * 128:(blk + 1) * 128, :], R=[OTR], W=[obR[blk % 3]])
                P.dma("sp", xb[blk % 3][:], x_src(blk), R=[x_srcR] if x_srcR is not None else [], W=[xbR[blk % 3]])

            def transp(blk):
                o_ = ob[blk % 3]
                s2 = blk % 2
                for k in range(KC):
                    pi = k // 8
                    P.op("pe", lambda e, k=k, pi=pi: e.transpose(ptr[pi][:, (k % 8) * 128:(k % 8 + 1) * 128],
                                                                  o_[:, k * 128:(k + 1) * 128], identb[:]),
                         R=[obR[blk % 3], RC], W=[ptrR[pi]])
                P.op("act", lambda e: e.activation(out=ot[s2][:, 0:8, :], in_=ptr[0][:].rearrange("p (k t) -> p k t", k=8),
                                                   func=AF.Copy), R=[ptrR[0]], W=[otR[s2][0]])
                P.op("dve", lambda e: e.tensor_copy(out=ot[s2][:, 8:16, :], in_=ptr[1][:].rearrange("p (k t) -> p k t", k=8)),
                     R=[ptrR[1]], W=[otR[s2][1]])

            loads(0)
            if nblk > 1:
                loads(1)
            if ot_src is None:
                transp(0)
            it = 0
            for blk in range(nblk):
                if blk + 2 < nblk:
                    loads(blk + 2)
                if ot_src is None and blk + 1 < nblk:
                    transp(blk + 1)
                s3 = blk % 3
                s2 = blk % 2
                sc = 1 if (has_ctx and blk < 2) else 0
                for nb in range(4):
                    pb = it % 4
                    t2 = it % 2
                    it += 1
                    for k in range(KC):
                        if ot_src is None:
                            lhs = ot[s2][:, k, :]
                            lR = otR[s2]
                        else:
                            lhs = ot_src[0][:, k, blk * 128:(blk + 1) * 128]
                            lR = [ot_src[1]]
                        P.op("pe", lambda e, k=k, lhs=lhs: e.matmul(py[pb][:, :], lhs, wo[:, k, nb * 512:(nb + 1) * 512],
                                                                    start=(k == 0), stop=(k == KC - 1)),
                             R=lR + [woR[nb]], W=[pyR[pb]])
                    P.op("dve", lambda e: e.tensor_tensor(out=tp[t2][:], in0=py[pb][:, :],
                                                          in1=G[sc][:, nb * 512:(nb + 1) * 512], op=ALU.mult),
                         R=[pyR[pb], GR[sc]], W=[tpR[t2]])
                    P.op("pool", lambda e: e.tensor_tensor(out=xo[s2][:, nb * 512:(nb + 1) * 512], in0=tp[t2][:],
                                                           in1=xb[s3][:, nb * 512:(nb + 1) * 512], op=ALU.add),
                         R=[tpR[t2], xbR[s3]], W=[xoR[s2][nb]])
                P.dma("sp", x_dst(blk), xo[s2][:], R=xoR[s2], W=[Res()])
            P.barrier()

    mixer_out(0, w_o0, NQ // 128, None,
              lambda blk: xk[blk * 128:(blk + 1) * 128, :],
              lambda blk: x1[blk * 128:(blk + 1) * 128, :], x1R, None, True)

    def ffn(l, ntile, x_src, x_srcR, x_dst, x_dstR, has_ctx, final):
        with ExitStack() as es:
            h2T = T(es, "h2T", [128, KC, 512], BF16)
            actT = T(es, "actT", [128, FC, 512], BF16)
            wg = [T(es, "wg%d" % i, [128, KC, 256], BF16) for i in range(2)]
            wu = [T(es, "wu%d" % i, [128, KC, 256], BF16) for i in range(2)]
            wd = [T(es, "wd%d" % i, [128, 4, 512], BF16) for i in range(3)]
            xr = [T(es, "xr%d" % i, [128, D], F32) for i in range(4)]
            bcA = T(es, "fA", [128, D], F32)
            bcB = T(es, "fB", [128, D], F32)
            bcG = T(es, "fG", [128, D], F32)
            bcGc = T(es, "fGc", [128, D], F32) if (has_ctx or final) else None
            tmp = T(es, "ftmp", [128, D], F32)
            hb = [T(es, "fhb%d" % i, [128, D], BF16) for i in range(2)]
            junk = T(es, "fjunk", [128, D], BF16)
            sg = [T(es, "sg%d" % i, [128, 512], F32) for i in range(2)]
            tq = [T(es, "tq%d" % i, [128, 512], F32) for i in range(2)]
            st = [T(es, "fst%d" % i, [128, 16], F32) for i in range(2)]
            stF = [T(es, "fstF%d" % i, [128, 16], F32) for i in range(4)]
            stFR = [Res() for _ in range(4)]
            pd = [PS(es, "pd%d" % i, [128, 512], F32) for i in range(4)]
            ptr = [PS(es, "fptr%d" % i, [128, 1024], BF16) for i in range(2)]
            h2R, actR = [Res(), Res()], Res()
            wgR = [Res(), Res()]
            wuR = [Res(), Res()]
            wdR = [Res() for _ in range(3)]
            xrR = [Res() for _ in range(4)]
            AR, BR, GR, GcR, tmpR, hbR = Res(), Res(), Res(), Res(), Res(), [[Res(), Res()], [Res(), Res()]]
            sgR = [Res(), Res()]
            tqR = [Res(), Res()]
            stR = [Res(), Res()]
            pdR = [Res() for _ in range(4)]
            ptrR = [Res(), Res()]

            load_bc(tmp[:], tmpR, norm_ffn[l:l + 1, :])
            load_bc(bcA[:], AR, modrow(l, 0, 4))
            make_A(bcA[:], AR, tmp[:], tmpR)
            load_bc(bcB[:], BR, modrow(l, 0, 3))
            load_bc(bcG[:], GR, modrow(l, 0, 5))
            if has_ctx:
                load_bc(xr[3][:], xrR[3], modrow(l, 1, 4))
                make_A(xr[3][:], xrR[3], tmp[:], tmpR)
                load_bc(xr[2][:], xrR[2], modrow(l, 1, 3))
                load_bc(bcGc[:], GcR, modrow(l, 1, 5))
            if final:
                load_bc(bcGc[:], GcR, norm_final)

            wgv = wgb[l].rearrange("(k p) c -> p k c", p=128)
            wuv = wub[l].rearrange("(k p) c -> p k c", p=128)
            wdv = wdb[l].rearrange("(j p) c -> p j c", p=128)
            wdn = [0]
            for t in range(ntile):
                def F_S1(bi, t=t):
                    blk = t * 4 + bi
                    ctxb = has_ctx and blk < 2
                    if t == 0:
                        P.dma("sp", xr[bi][:], x_src(blk), R=[x_srcR], W=[xrR[bi]])
                    A_, AR_ = (xr[3][:], xrR[3]) if ctxb else (bcA[:], AR)
                    B_, BR_ = (xr[2][:], xrR[2]) if ctxb else (bcB[:], BR)
                    return norm_s1(xr[bi][:], xrR[bi], A_, AR_, B_, BR_, tmp[:], tmpR, hb, hbR, junk[:],
                                   st[bi % 2], stR[bi % 2])

                hs = {0: F_S1(0)}
                for bi in range(4):
                    if bi + 1 < 4:
                        hs[bi + 1] = F_S1(bi + 1)
                    norm_s2(hs[bi], hb, hbR, ptr, ptrR,
                            h2T[:, 0:8, bi * 128:(bi + 1) * 128], h2T[:, 8:16, bi * 128:(bi + 1) * 128], h2R)
                for jj in range(FC // 2):
                    s = jj % 2
                    P.dma("sp", wg[s][:], wgv[:, :, jj * 256:(jj + 1) * 256], W=[wgR[s]])
                    P.dma("sp", wu[s][:], wuv[:, :, jj * 256:(jj + 1) * 256], W=[wuR[s]])
                    for jh in range(2):
                        j = 2 * jj + jh
                        pg = (2 * j) % 4
                        pu = (2 * j + 1) % 4
                        for k in range(KC):
                            P.op("pe", lambda e, k=k, s=s, jh=jh, pg=pg: e.matmul(
                                pd[pg][:, :], wg[s][:, k, jh * 128:(jh + 1) * 128], h2T[:, k, :],
                                start=(k == 0), stop=(k == KC - 1)), R=[wgR[s], h2R[0], h2R[1]], W=[pdR[pg]])
                        for k in range(KC):
                            P.op("pe", lambda e, k=k, s=s, jh=jh, pu=pu: e.matmul(
                                pd[pu][:, :], wu[s][:, k, jh * 128:(jh + 1) * 128], h2T[:, k, :],
                                start=(k == 0), stop=(k == KC - 1)), R=[wuR[s], h2R[0], h2R[1]], W=[pdR[pu]])
                        s2 = j % 2
                        P.op("act", lambda e, s2=s2, pg=pg: e.activation(out=sg[s2][:], in_=pd[pg][:, :], func=AF.Silu),
                             R=[pdR[pg]], W=[sgR[s2]])
                        P.op("dve", lambda e, s2=s2, pu=pu, j=j: e.tensor_tensor(out=actT[:, j, :], in0=pd[pu][:, :],
                                                                                 in1=sg[s2][:], op=ALU.mult),
                             R=[pdR[pu], sgR[s2]], W=[actR])
                for nb in range(4):
                    for jg in range(FC // 4):
                        ws = wdn[0] % 3
                        wdn[0] += 1
                        P.dma("sp", wd[ws][:], wdv[:, jg * 4:(jg + 1) * 4, nb * 512:(nb + 1) * 512],
                              W=[wdR[ws]])
                        for bi in range(4):
                            for jl in range(4):
                                j = jg * 4 + jl
                                P.op("pe", lambda e, j=j, jl=jl, bi=bi, ws=ws: e.matmul(
                                    pd[bi][:, :], actT[:, j, bi * 128:(bi + 1) * 128], wd[ws][:, jl, :],
                                    start=(j == 0), stop=(j == FC - 1)), R=[actR, wdR[ws]], W=[pdR[bi]])
                    for bi in range(4):
                        blk = t * 4 + bi
                        ctxb = has_ctx and blk < 2
                        G_, GR_ = (bcGc[:], GcR) if ctxb else (bcG[:], GR)
                        s2 = bi % 2
                        P.op("dve", lambda e, s2=s2, bi=bi, nb=nb, G_=G_: e.tensor_tensor(
                            out=tq[s2][:], in0=pd[bi][:, :], in1=G_[:, nb * 512:(nb + 1) * 512], op=ALU.mult),
                            R=[pdR[bi], GR_], W=[tqR[s2]])
                        P.op("pool", lambda e, s2=s2, bi=bi, nb=nb: e.tensor_tensor(
                            out=xr[bi][:, nb * 512:(nb + 1) * 512], in0=tq[s2][:], in1=xr[bi][:, nb * 512:(nb + 1) * 512],
                            op=ALU.add), R=[tqR[s2]], W=[xrR[bi]])
                        if nb == 3 and not final:
                            P.dma("sp", x_dst(blk), xr[bi][:], R=[xrR[bi]], W=[Res()])
                    if nb == 3 and final:
                        for bi in range(4):
                            rms_stats(xr[bi][:], D, stF[bi], stFR[bi], 4, xrR[bi], junk[:])
                        for bi in range(4):
                            P.op("dve", lambda e, bi=bi: e.scalar_tensor_tensor(
                                out=xr[bi][:], in0=xr[bi][:], scalar=stF[bi][:, 6:7], in1=bcGc[:],
                                op0=ALU.mult, op1=ALU.mult), R=[stFR[bi], GcR], W=[xrR[bi]])
                            P.dma("sp", x_dst(t * 4 + bi), xr[bi][:], R=[xrR[bi]], W=[Res()])
                    if nb == 3 and t + 1 < ntile:
                        for bi in range(4):
                            P.dma("sp", xr[bi][:], x_src(t * 4 + bi + 4), R=[x_srcR], W=[xrR[bi]])
            P.barrier()

    ffn(0, NQ // 512, lambda blk: x1[blk * 128:(blk + 1) * 128, :], x1R,
        lambda blk: x2[blk * 128:(blk + 1) * 128, :], x2R, True, False)

    with ExitStack() as es:
        hTa = T(es, "hTa", [128, KC, NQ], BF16)
        hTaR = [Res(), Res()]
        with ExitStack() as es2:
            xin = [T(es2, "dxin%d" % i, [128, D], F32) for i in range(3)]
            junk = T(es2, "djunk", [128, D], BF16)
            tmp = T(es2, "dtmp", [128, D], F32)
            hb = [T(es2, "dhb%d" % i, [128, D], BF16) for i in range(2)]
            bcA = [T(es2, "dA%d" % i, [128, D], F32) for i in range(2)]
            bcB = [T(es2, "dB%d" % i, [128, D], F32) for i in range(2)]
            st = [T(es2, "dst%d" % i, [128, 16], F32) for i in range(2)]
            ptr = [PS(es2, "dptr%d" % i, [128, 1024], BF16) for i in range(2)]
            xinR = [Res() for _ in range(3)]
            tmpR, hbR = Res(), [[Res(), Res()], [Res(), Res()]]
            bcAR = [Res(), Res()]
            bcBR = [Res(), Res()]
            stR = [Res(), Res()]
            ptrR = [Res(), Res()]
            load_bc(tmp[:], tmpR, norm_mix[1:2, :])
            for s in range(2):
                load_bc(bcA[s][:], bcAR[s], modrow(1, s, 1))
                load_bc(bcB[s][:], bcBR[s], modrow(1, s, 0))
                make_A(bcA[s][:], bcAR[s], tmp[:], tmpR)
            def D_S1(tb):
                s3 = tb % 3
                sc = 1 if tb < 2 else 0
                P.dma("sp", xin[s3][:], x2[tb * 128:(tb + 1) * 128, :], R=[x2R], W=[xinR[s3]])
                return norm_s1(xin[s3][:], xinR[s3], bcA[sc][:], bcAR[sc], bcB[sc][:], bcBR[sc], tmp[:], tmpR, hb, hbR,
                               junk[:], st[tb % 2], stR[tb % 2])

            hs = {0: D_S1(0)}
            for tb in range(NQ // 128):
                if tb + 1 < NQ // 128:
                    hs[tb + 1] = D_S1(tb + 1)
                norm_s2(hs[tb], hb, hbR, ptr, ptrR,
                        hTa[:, 0:8, tb * 128:(tb + 1) * 128], hTa[:, 8:16, tb * 128:(tb + 1) * 128], hTaR)
            P.barrier()
        with ExitStack() as es2:
            wqb = [T(es2, "wqb%d" % i, [128, KC, 128], BF16) for i in range(2)]
            wvb = [T(es2, "wvb%d" % i, [128, KC, 512], BF16) for i in range(2)]
            qo = [T(es2, "qo%d" % i, [128, 512], BF16) for i in range(3)]
            pp = [PS(es2, "pp%d" % i, [128, 512], F32) for i in range(3)]
            wqbR = [Res(), Res()]
            wvbR = [Res(), Res()]
            qoR = [Res() for _ in range(3)]
            ppR = [Res() for _ in range(3)]
            it = 0
            wn = 0
            wqv = w_qkv.rearrange("(k p) c -> p k c", p=128)
            for part in range(2):
                for c in range(16):
                    ws = wn % 2
                    wn += 1
                    col = part * D + c * 128
                    P.dma("pool", wqb[ws][:], wqv[:, :, col:col + 128], W=[wqbR[ws]])
                    tiles = [(256 + i * 512, i * 512) for i in range(4)] if part == 0 else [(i * 512, i * 512) for i in range(5)]
                    for (src0, dst0) in tiles:
                        s = it % 3
                        it += 1
                        for k in range(KC):
                            P.op("pe", lambda e, k=k, s=s, ws=ws, src0=src0: e.matmul(
                                pp[s][:, :], wqb[ws][:, k, :], hTa[:, k, src0:src0 + 512],
                                start=(k == 0), stop=(k == KC - 1)), R=[wqbR[ws], hTaR[0], hTaR[1]], W=[ppR[s]])
                        if part == 0:
                            P.op("act", lambda e, s=s: e.activation(out=qo[s][:], in_=pp[s][:, :], func=AF.Copy,
                                                                    scale=SC_NA), R=[ppR[s]], W=[qoR[s]])
                            P.dma("sp", q1T[c * 128:(c + 1) * 128, dst0:dst0 + 512], qo[s][:], R=[qoR[s]], W=[Res()])
                        else:
                            P.op("dve", lambda e, s=s: e.tensor_copy(out=qo[s][:], in_=pp[s][:, :]),
                                 R=[ppR[s]], W=[qoR[s]])
                            P.dma("sp", k1T[c * 128:(c + 1) * 128, dst0:dst0 + 512], qo[s][:], R=[qoR[s]], W=[Res()])
            for nb in range(4):
                ws = nb % 2
                P.dma("pool", wvb[ws][:], wqv[:, :, 2 * D + nb * 512:2 * D + (nb + 1) * 512], W=[wvbR[ws]])
                for tb in range(NQ // 128):
                    s = it % 3
                    it += 1
                    for k in range(KC):
                        P.op("pe", lambda e, k=k, s=s, ws=ws, tb=tb: e.matmul(
                            pp[s][:, :], hTa[:, k, tb * 128:(tb + 1) * 128], wvb[ws][:, k, :],
                            start=(k == 0), stop=(k == KC - 1)), R=[wvbR[ws], hTaR[0], hTaR[1]], W=[ppR[s]])
                    if it % 2 == 0:
                        P.op("act", lambda e, s=s: e.activation(out=qo[s][:], in_=pp[s][:, :], func=AF.Copy),
                             R=[ppR[s]], W=[qoR[s]])
                    else:
                        P.op("dve", lambda e, s=s: e.tensor_copy(out=qo[s][:], in_=pp[s][:, :]), R=[ppR[s]], W=[qoR[s]])
                    P.dma("sp", v1[4 * nb:4 * nb + 4, :, tb, 0:128].rearrange("h p d -> p h d"),
                          qo[s][:].rearrange("p (h d) -> p h d", h=4), R=[qoR[s]], W=[Res()])
            P.barrier()

    with ExitStack() as es:
        KTh = [T(es, "KTh%d" % i, [128, NQ], BF16) for i in range(2)]
        Vh = [T(es, "Vh%d" % i, [128, NQ // 128, 144], BF16) for i in range(2)]
        QTh = [T(es, "QTh%d" % i, [128, NOWN], BF16) for i in range(2)]
        tab = [[T(es, "tab%d_%d" % (i, v), [128, 6, 256], BF16) for v in range(3)] for i in range(2)]
        Pt = [T(es, "ePt%d" % i, [128, 512], BF16) for i in range(4)]
        rec = [T(es, "erec%d" % i, [128, 2], F32) for i in range(6)]
        On = [T(es, "eOn%d" % i, [128, 2, 128], BF16) for i in range(6)]
        bank = [PS(es, "ebk%d" % i, [128, 512], F32) for i in range(8)]
        bankR = [Res() for _ in range(8)]
        KThR = [Res(), Res()]
        VhR = [Res(), Res()]
        QThR = [Res(), Res()]
        tabR = [Res(), Res()]
        PtR = [Res() for _ in range(4)]
        recR = [Res() for _ in range(6)]
        OnR = [Res() for _ in range(6)]

        def load_head(h):
            s = h % 2
            P.dma("sp", KTh[s][:], k1T[h * 128:(h + 1) * 128, :], R=[k1R], W=[KThR[s]])
            P.dma("sp", Vh[s][:], v1[h], R=[v1R], W=[VhR[s]])
            P.op("pool", lambda e: e.memset(Vh[s][:, :, 128:129], 1.0), W=[VhR[s]])
            P.dma("sp", QTh[s][:], q1T[h * 128:(h + 1) * 128, :], R=[q1R], W=[QThR[s]])
            for v in range(3):
                P.dma("pool", tab[s][v][:], natab[v, h].rearrange("(c p) q -> p c q", p=128), W=[tabR[s]])

        load_head(0)
        gstep = 0
        tcount = 0
        for h in range(16):
            s = h % 2
            if h + 1 < 16:
                load_head(h + 1)
            for g in range(8):
                tv = 0 if g == 0 else (2 if g == 7 else 1)
                chunks = [(0, None), (128, None)]
                for m in range(6):
                    pos = (4 * g - 4 + 2 * m) % 36
                    chunks.append((256 + pos * 64, m))
                ab = (4, 5) if tcount % 2 == 0 else (6, 7)
                sl = tcount % 6
                tcount += 1
                n = 4
                base = gstep
                gstep += n
                q_ap = QTh[s][:, g * 256:(g + 1) * 256]

                def QK(p, chunks=chunks, base=base, q_ap=q_ap, s=s, tv=tv):
                    sb = (base + p) % 4
                    for hf in range(2):
                        tok0, m = chunks[2 * p + hf]
                        reg = bank[sb][:, hf * 256:(hf + 1) * 256]
                        P.op("pe", lambda e: e.matmul(reg, KTh[s][:, tok0:tok0 + 128], q_ap,
                                                      start=True, stop=(m is None)), R=[KThR[s], QThR[s]], W=[bankR[sb]])
                        if m is not None:
                            P.op("pe", lambda e: e.matmul(reg, identb[:], tab[s][tv][:, m, :],
                                                          start=False, stop=True), R=[RC, tabR[s]], W=[bankR[sb]])

                QK(0)
                QK(1)
                QK(2)
                for p in range(n):
                    sb = (base + p) % 4
                    ps = (base + p) % 4
                    P.op("act", lambda e, sb=sb, ps=ps: e.activation(out=Pt[ps][:], in_=bank[sb][:, :], func=AF.Exp),
                         R=[bankR[sb]], W=[PtR[ps]])
                    for hf in range(2):
                        tok0, m = chunks[2 * p + hf]
                        first = (p == 0 and hf == 0)
                        last = (p == n - 1 and hf == 1)
                        for qb in range(2):
                            P.op("pe", lambda e, tok0=tok0, ps=ps, hf=hf, qb=qb, first=first, last=last: e.matmul(
                                bank[ab[qb]][:, 0:129], Pt[ps][:, hf * 256 + qb * 128:hf * 256 + (qb + 1) * 128],
                                Vh[s][:, tok0 // 128, 0:129], start=first, stop=last),
                                R=[VhR[s], PtR[ps]], W=[bankR[ab[qb]]])
                    if p + 3 < n:
                        QK(p + 3)
                for qb in range(2):
                    bk = ab[qb]
                    P.op("dve", lambda e, sl=sl, bk=bk, qb=qb: e.reciprocal(out=rec[sl][:, qb:qb + 1],
                                                                            in_=bank[bk][:, 128:129]),
                         R=[bankR[bk]], W=[recR[sl]])
                    P.op("dve", lambda e, sl=sl, bk=bk, qb=qb: e.tensor_scalar_mul(
                        out=On[sl][:, qb, :], in0=bank[bk][:, 0:128], scalar1=rec[sl][:, qb:qb + 1]),
                        R=[bankR[bk], recR[sl]], W=[OnR[sl]])
                P.dma("sp", OT[g * 256:(g + 1) * 256, h * 128:(h + 1) * 128].rearrange("(qb p) d -> p qb d", p=128),
                      On[sl][:], R=[OnR[sl]], W=[Res()])
        P.barrier()

    mixer_out(1, w_o1, NOWN // 128, None,
              lambda blk: x2[(blk + 2) * 128:(blk + 3) * 128, :],
              lambda blk: x3[blk * 128:(blk + 1) * 128, :], x3R, x2R, False)

    ffn(1, NOWN // 512, lambda blk: x3[blk * 128:(blk + 1) * 128, :], x3R,
        lambda blk: yout[blk * 128:(blk + 1) * 128, :], Res(), False, True)

    P.barrier()
    top.close()
    return nc, P.nins


def _rope_tables(tok_idx, is_ctx):
    t = np.asarray(tok_idx)
    row = (t // 64).astype(np.float32)
    col = (t % 64).astype(np.float32)
    inv = (np.float32(10000.0) ** (-(np.arange(16, dtype=np.float32)) / np.float32(16))).astype(np.float32)
    ar = row[:, None] * inv[None, :]
    ac = col[:, None] * inv[None, :]
    cr, sr, cc, scn = np.cos(ar), np.sin(ar), np.cos(ac), np.sin(ac)
    cos4 = np.concatenate([cr, cr, cc, cc], axis=1).astype(np.float32)
    sin4 = np.concatenate([-sr, sr, -scn, scn], axis=1).astype(np.float32)
    cos4[is_ctx] = 1.0
    sin4[is_ctx] = 0.0
    return cos4, sin4


def _na_table(rel_bias, half, g):
    own0 = 32 * half
    j = np.arange(12)
    pos = (4 * g - 4 + j) % 36
    krow = np.where(pos < 32, own0 + pos, (32 if half == 0 else 28) + (pos - 32))
    i = np.arange(4)
    r = own0 + 4 * g + i
    rs = np.clip(r - 4, 0, 56)
    vrow = (krow[:, None] >= rs[None, :]) & (krow[:, None] < rs[None, :] + 8)
    dr = np.clip(krow[:, None] - r[None, :] + 7, 0, 14)
    kc = np.arange(64)
    qc = np.arange(64)
    cs = np.clip(qc - 8, 0, 48)
    vcol = (kc[:, None] >= cs[None, :]) & (kc[:, None] < cs[None, :] + 16)
    dc = np.clip(kc[:, None] - qc[None, :] + 15, 0, 30)
    vals = rel_bias[:, dr[:, None, :, None], dc[None, :, None, :]]
    valid = vrow[:, None, :, None] & vcol[None, :, None, :]
    out = np.where(valid[None], vals, np.float32(NEG)).astype(np.float32)
    return out.reshape(16, 768, 256)


_CACHE = {}


def kernel(x, c, ctx, c_ctx, ada_w, ada_b, norm_mix, norm_ffn, norm_final,
           mla_w_dq, mla_q_norm, mla_w_uq, mla_w_dkv, mla_kv_norm, mla_w_ukv, mla_w_o,
           na_w_qkv, na_rel_bias, na_w_o, ffn_w_gate, ffn_w_up, ffn_w_down):
    f = lambda a: np.ascontiguousarray(np.asarray(a, dtype=np.float32))
    x, c, ctx, c_ctx = f(x), f(c), f(ctx), f(c_ctx)
    if "nc" not in _CACHE:
        _CACHE["nc"] = build_program()[0]
    nc = _CACHE["nc"]
    shared = {
        "ident": np.eye(128, dtype=np.float32),
        "ada_w": f(ada_w), "ada_b": f(ada_b), "norm_mix": f(norm_mix), "norm_ffn": f(norm_ffn),
        "norm_final": f(norm_final).reshape(1, D),
        "mla_w_dq": f(mla_w_dq)[0], "mla_q_norm": f(mla_q_norm).reshape(1, 512), "mla_w_uq": f(mla_w_uq)[0],
        "mla_w_dkv": f(mla_w_dkv)[0], "mla_kv_norm": f(mla_kv_norm).reshape(1, 512), "mla_w_ukv": f(mla_w_ukv)[0],
        "mla_w_o": f(mla_w_o)[0], "na_w_qkv": f(na_w_qkv)[0], "na_w_o": f(na_w_o)[0],
        "ffn_w_gate": f(ffn_w_gate), "ffn_w_up": f(ffn_w_up), "ffn_w_down": f(ffn_w_down),
    }
    rb = f(na_rel_bias)[0]
    tabs = {}
    for half in range(2):
        t0 = _na_table(rb, half, 0)
        t1 = _na_table(rb, half, 1)
        t7 = _na_table(rb, half, 7)
        tabs[half] = np.ascontiguousarray(np.stack([t0, t1, t7], axis=0))
    in_maps = []
    orders = []
    for core in range(8):
        b, half = core // 2, core % 2
        if half == 0:
            own = np.arange(0, 2048)
            halo = np.arange(2048, 2304)
            rest = np.arange(2304, 4096)
        else:
            own = np.arange(2048, 4096)
            halo = np.arange(1792, 2048)
            rest = np.arange(0, 1792)
        lat = np.concatenate([own, halo, rest])
        orders.append(own)
        xk = np.ascontiguousarray(np.concatenate([ctx[b], x[b][lat]], axis=0))
        tok = np.concatenate([np.zeros(256, dtype=np.int64), lat])
        is_ctx = np.zeros(NK, dtype=bool)
        is_ctx[:256] = True
        cos4, sin4 = _rope_tables(tok, is_ctx)
        m = dict(shared)
        m["xk"] = xk
        m["cvec"] = np.ascontiguousarray(np.stack([c[b], c_ctx], axis=0))
        m["ropek"] = np.ascontiguousarray(np.concatenate([cos4, sin4], axis=1))
        m["ropeqc"] = np.ascontiguousarray(cos4[:NQ].T)
        m["ropeqs"] = np.ascontiguousarray(sin4[:NQ].T)
        m["natab"] = tabs[half]
        in_maps.append(m)
    res = run_bass_kernel_spmd(nc, in_maps, core_ids=list(range(8)))
    out = np.empty((4, 4096, D), dtype=np.float32)
    for core in range(8):
        b = core // 2
        out[b, orders[core]] = np.asarray(res.results[core]["y"], dtype=np.float32)
    return out
```

```python
import numpy as np
from contextlib import ExitStack
import concourse.bass as bass
import concourse.mybir as mybir
from concourse.bass_utils import run_bass_kernel_spmd

F32 = mybir.dt.float32
BF16 = mybir.dt.bfloat16
AF = mybir.ActivationFunctionType
ALU = mybir.AluOpType

D = 2048
KC = 16
FF = 5632
FC = 44
NQ = 2560
NK = 4352
NOWN = 2048
EPS = 1e-6
SC_MLA = 192 ** -0.5
SC_NA = 128 ** -0.5
NEG = -30000.0
NDS = 24


class Res:
    __slots__ = ("w", "r")

    def __init__(self):
        self.w = None
        self.r = {}


class Prog:
    def __init__(self, nc, es):
        self.nc = nc
        self.E = {"pe": nc.tensor, "act": nc.scalar, "dve": nc.vector, "pool": nc.gpsimd, "sp": nc.sync}
        self.semobj = {}
        self.cnt = {}
        for e in ("pe", "act", "dve", "pool"):
            self.semobj["c_" + e] = es.enter_context(nc.semaphore("c_" + e))
            self.cnt[e] = 0
        self.dq = {}
        for q in ("sp", "pool"):
            keys = []
            for i in range(NDS):
                k = "d_%s%d" % (q, i)
                self.semobj[k] = es.enter_context(nc.semaphore(k))
                keys.append(k)
            self.dq[q] = keys
        self.dcnt = {}
        self.dsrc = {}
        self.drr = {"sp": 0, "pool": 0}
        self.seen = {e: {} for e in self.E}
        self.nins = 0

    def _wait(self, eng, tok):
        key, val, src = tok
        if src == eng and eng == "pe":
            return
        if self.seen[eng].get(key, 0) >= val:
            return
        self.E[eng].wait_ge(self.semobj[key], val)
        self.seen[eng][key] = val
        self.nins += 1

    def _deps(self, eng, R, W):
        for r in R:
            if r.w is not None:
                self._wait(eng, r.w)
        for w in W:
            if w.w is not None:
                self._wait(eng, w.w)
            for t in w.r.values():
                self._wait(eng, t)

    def _commit(self, tok, R, W):
        for r in R:
            old = r.r.get(tok[0])
            if old is None or old[1] < tok[1]:
                r.r[tok[0]] = tok
        for w in W:
            w.w = tok
            w.r = {}

    def op(self, eng, fn, R=(), W=()):
        self._deps(eng, R, W)
        ins = fn(self.E[eng])
        self.cnt[eng] += 1
        ins.then_inc(self.semobj["c_" + eng], 1)
        self._commit(("c_" + eng, self.cnt[eng], eng), R, W)
        self.nins += 1

    def dma(self, q, out, in_, R=(), W=()):
        keys = self.dq[q]
        i = self.drr[q]
        self.drr[q] = (i + 1) % len(keys)
        key = keys[i]
        c = self.dcnt.get(key, 0)
        if c > 0:
            self._wait(q, (key, 16 * c, q))
        self._deps(q, R, W)
        ins = self.E[q].dma_start(out=out, in_=in_)
        self.dcnt[key] = c + 1
        ins.then_inc(self.semobj[key], 16)
        self._commit((key, 16 * (c + 1), q), R, W)
        self.nins += 1

    def barrier(self):
        toks = [("c_" + e, self.cnt[e], e) for e in self.cnt if self.cnt[e] > 0]
        for q in self.dq:
            for k in self.dq[q]:
                if self.dcnt.get(k, 0) > 0:
                    toks.append((k, 16 * self.dcnt[k], q))
        for eng in self.E:
            for t in toks:
                self._wait(eng, t)


def build_program():
    nc = bass.Bass("TRN2", target_bir_lowering=False)

    def din(name, shape):
        return nc.dram_tensor(name, list(shape), F32, kind="ExternalInput").ap()

    def dscr(name, shape, dt):
        return nc.dram_tensor(name, list(shape), dt, kind="Internal").ap()

    xk = din("xk", [NK, D])
    cvec = din("cvec", [2, D])
    ropek = din("ropek", [NK, 128])
    ropeqc = din("ropeqc", [64, NQ])
    ropeqs = din("ropeqs", [64, NQ])
    identd = din("ident", [128, 128])
    ada_w = din("ada_w", [2, D, 6 * D])
    ada_b = din("ada_b", [2, 6 * D])
    norm_mix = din("norm_mix", [2, D])
    norm_ffn = din("norm_ffn", [2, D])
    norm_final = din("norm_final", [1, D])
    w_dq = din("mla_w_dq", [D, 512])
    q_norm = din("mla_q_norm", [1, 512])
    w_uq = din("mla_w_uq", [512, 3072])
    w_dkv = din("mla_w_dkv", [D, 576])
    kv_norm = din("mla_kv_norm", [1, 512])
    w_ukv = din("mla_w_ukv", [512, 4096])
    w_o0 = din("mla_w_o", [D, D])
    w_qkv = din("na_w_qkv", [D, 3 * D])
    natab = din("natab", [3, 16, 768, 256])
    w_o1 = din("na_w_o", [D, D])
    w_gate = din("ffn_w_gate", [2, D, FF])
    w_up = din("ffn_w_up", [2, D, FF])
    w_down = din("ffn_w_down", [2, FF, D])
    yout = nc.dram_tensor("y", [NOWN, D], F32, kind="ExternalOutput").ap()

    mod = dscr("mod", [2, 2, 6 * D], F32)
    OT = dscr("OT", [NQ, D], BF16)
    x1 = dscr("x1", [NQ, D], F32)
    x2 = dscr("x2", [NQ, D], F32)
    q1T = dscr("q1T", [D, NOWN], BF16)
    k1T = dscr("k1T", [D, NQ], BF16)
    v1 = dscr("v1", [16, 128, NQ // 128, 144], BF16)
    x3 = dscr("x3", [NOWN, D], F32)
    wgb = dscr("wgb", [2, D, FF], BF16)
    wub = dscr("wub", [2, D, FF], BF16)
    wdb = dscr("wdb", [2, FF, D], BF16)

    top = ExitStack()
    P = Prog(nc, top)

    uid = [0]

    def T(es, name, shape, dt):
        uid[0] += 1
        return es.enter_context(nc.sbuf_tensor("%s_%d" % (name, uid[0]), list(shape), dt))

    def PS(es, name, shape, dt):
        uid[0] += 1
        return es.enter_context(nc.psum_tensor("%s_%d" % (name, uid[0]), list(shape), dt))

    identb = T(top, "identb", [128, 128], BF16)
    onesb = T(top, "onesb", [128, 128], BF16)
    RC = Res()
    P.dma("pool", identb[:], identd, W=[RC])
    P.op("dve", lambda e: e.memset(onesb[:], 1.0), W=[RC])

    modR = [Res(), Res()]
    OTR = Res()
    x1R = Res()
    x2R = Res()
    q1R = Res()
    k1R = Res()
    v1R = Res()
    x3R = Res()
    wcastR = [[Res() for _ in range(3)] for _ in range(2)]

    def precast_chunk(l, i):
        P.dma("pool", wgb[l, i * 128:(i + 1) * 128, :], w_gate[l, i * 128:(i + 1) * 128, :], W=[Res()])
        P.dma("pool", wub[l, i * 128:(i + 1) * 128, :], w_up[l, i * 128:(i + 1) * 128, :], W=[Res()])
        P.dma("pool", wdb[l, i * 352:(i + 1) * 352, :], w_down[l, i * 352:(i + 1) * 352, :], W=[Res()])

    def rms_stats(src, n, st, stR, c, srcR, junk):
        P.op("act", lambda e: e.activation(out=junk, in_=src, func=AF.Square, accum_out=st[:, c:c + 1]),
             R=[srcR], W=[stR])
        P.op("act", lambda e: e.activation(out=st[:, c + 1:c + 2], in_=st[:, c:c + 1], func=AF.Sqrt,
                                           scale=1.0 / n, bias=EPS), R=[stR], W=[stR])
        P.op("dve", lambda e: e.reciprocal(out=st[:, c + 2:c + 3], in_=st[:, c + 1:c + 2]), R=[stR], W=[stR])

    def load_bc(dst, dstR, src_row):
        P.dma("sp", dst, src_row.partition_broadcast(128), W=[dstR])

    def modrow(l, s, i):
        return mod[l, s:s + 1, i * D:(i + 1) * D]

    def make_A(A, AR, gw, gwR):
        P.op("dve", lambda e: e.scalar_tensor_tensor(out=A, in0=A, scalar=1.0, in1=gw, op0=ALU.add, op1=ALU.mult),
             R=[gwR], W=[AR])

    sT = T(top, "sT", [128, 32], BF16)
    sTR = Res()
    adak = [T(top, "adak%d" % i, [2, 512], F32) for i in range(2)]
    msk = [T(top, "msk%d" % i, [2, 512], F32) for i in range(2)]
    adakR = [Res(), Res()]
    mskR = [Res(), Res()]
    adan = [0]

    def ada_load(l, nb, wb, wbR):
        s = adan[0] % len(wb)
        adan[0] += 1
        P.dma("pool", wb[s][:], ada_w[l, :, nb * 512:(nb + 1) * 512].rearrange("(k p) c -> p k c", p=128),
              W=[wbR[s]])
        return s

    def ada_compute(l, nb, s, wb, wbR, pm, pmR):
        a = s % 2
        P.dma("sp", adak[a][:], ada_b[l:l + 1, nb * 512:(nb + 1) * 512].partition_broadcast(2), W=[adakR[a]])
        for k in range(KC):
            P.op("pe", lambda e, k=k: e.matmul(pm[0:2, :], sT[:, 2 * k:2 * k + 2], wb[s][:, k, :],
                                               start=(k == 0), stop=(k == KC - 1)), R=[sTR, wbR[s]], W=[pmR])
        P.op("dve", lambda e: e.tensor_tensor(out=msk[a][:], in0=pm[0:2, :], in1=adak[a][:], op=ALU.add),
             R=[pmR, adakR[a]], W=[mskR[a]])
        P.dma("sp", mod[l, :, nb * 512:(nb + 1) * 512], msk[a][:], R=[mskR[a]], W=[Res()])

    def ada_block(l, nb, wb, wbR, pm, pmR):
        s = ada_load(l, nb, wb, wbR)
        ada_compute(l, nb, s, wb, wbR, pm, pmR)

    es0 = ExitStack()
    ckvT = T(es0, "ckvT", [128, 4, NK], BF16)
    kpeT = T(es0, "kpeT", [128, NK], BF16)
    cqT = T(es0, "cqT", [128, 4, NQ], BF16)
    ckvR, kpeR, cqR = Res(), Res(), Res()
    P.op("dve", lambda e: e.memset(kpeT[:], 0.0), W=[kpeR])
    esW = ExitStack()
    wdkv = T(esW, "wdkv", [128, KC, 576], BF16)
    wdq = T(esW, "wdq", [128, KC, 512], BF16)
    wdkvR, wdqR = Res(), Res()

    with ExitStack() as es:
        cv = T(es, "cv", [2, D], F32)
        cs = T(es, "cs", [2, D], BF16)
        wb = [T(es, "adaw%d" % i, [128, KC, 512], BF16) for i in range(2)]
        pm = [PS(es, "pm%d" % i, [128, 512], F32) for i in range(2)]
        pt = PS(es, "pt", [128, 32], BF16)
        cvR, csR, ptR = Res(), Res(), Res()
        wbR = [Res(), Res()]
        pmR = [Res(), Res()]
        P.dma("sp", cv[:], cvec, W=[cvR])
        P.op("act", lambda e: e.activation(out=cs[:], in_=cv[:], func=AF.Silu), R=[cvR], W=[csR])
        for k in range(KC):
            P.op("pe", lambda e, k=k: e.transpose(pt[:, 2 * k:2 * k + 2], cs[0:2, k * 128:(k + 1) * 128],
                                                   identb[0:2, 0:2]), R=[csR, RC], W=[ptR])
        P.op("dve", lambda e: e.tensor_copy(out=sT[:], in_=pt[:]), R=[ptR], W=[sTR])
        for nb in range(8):
            ada_block(0, nb, wb, wbR, pm[nb % 2], pmR[nb % 2])
        P.dma("pool", wdkv[:], w_dkv.rearrange("(k p) c -> p k c", p=128), W=[wdkvR])
        P.dma("pool", wdq[:], w_dq.rearrange("(k p) c -> p k c", p=128), W=[wdqR])
        P.barrier()
    ada_todo = [(0, nb) for nb in range(8, 24)] + [(1, nb) for nb in range(24)]


    nrm_n = [0]
    NSPL = 896

    def norm_s1(xt, xR, A, AR, B, BR, tmp, tmpR, hbs, hbRs, junk, st, stR):
        i = nrm_n[0] % 2
        nrm_n[0] += 1
        hb = hbs[i]
        rms_stats(xt, D, st, stR, 0, xR, junk)
        P.op("dve", lambda e: e.scalar_tensor_tensor(out=tmp, in0=xt, scalar=st[:, 2:3], in1=A,
                                                     op0=ALU.mult, op1=ALU.mult), R=[xR, stR, AR], W=[tmpR])
        P.op("pool", lambda e: e.tensor_tensor(out=hb[:, 0:NSPL], in0=tmp[:, 0:NSPL], in1=B[:, 0:NSPL], op=ALU.add),
             R=[tmpR, BR], W=[hbRs[i][0]])
        P.op("dve", lambda e: e.tensor_tensor(out=hb[:, NSPL:D], in0=tmp[:, NSPL:D], in1=B[:, NSPL:D], op=ALU.add),
             R=[tmpR, BR], W=[hbRs[i][1]])
        return i

    def norm_s2(i, hbs, hbRs, ptr, ptrR, hT_lo, hT_hi, hTR):
        hb = hbs[i]
        for k in range(KC):
            pi = k // 8
            P.op("pe", lambda e, k=k, pi=pi: e.transpose(ptr[pi][:, (k % 8) * 128:(k % 8 + 1) * 128],
                                                          hb[:, k * 128:(k + 1) * 128], identb[:]),
                 R=[hbRs[i][0], hbRs[i][1], RC], W=[ptrR[pi]])
        P.op("act", lambda e: e.activation(out=hT_lo, in_=ptr[0][:].rearrange("p (k t) -> p k t", k=8), func=AF.Copy),
             R=[ptrR[0]], W=[hTR[0]])
        P.op("act", lambda e: e.activation(out=hT_hi, in_=ptr[1][:].rearrange("p (k t) -> p k t", k=8), func=AF.Copy),
             R=[ptrR[1]], W=[hTR[1]])

    with ExitStack() as es:
        xin = [T(es, "xin%d" % i, [128, D], F32) for i in range(3)]
        junk = T(es, "junk", [128, D], BF16)
        tmp = T(es, "tmp", [128, D], F32)
        hb = [T(es, "hb%d" % i, [128, D], BF16) for i in range(2)]
        hT = [T(es, "hT%d" % i, [128, KC, 128], BF16) for i in range(2)]
        bcA = [T(es, "bcA%d" % i, [128, D], F32) for i in range(2)]
        bcB = [T(es, "bcB%d" % i, [128, D], F32) for i in range(2)]
        kvn = T(es, "kvn", [128, 512], F32)
        qn = T(es, "qn", [128, 512], F32)
        rk = [T(es, "rk%d" % i, [128, 128], F32) for i in range(2)]
        st = [T(es, "st%d" % i, [128, 16], F32) for i in range(2)]
        ckb = T(es, "ckb", [128, 512], BF16)
        cqb = T(es, "cqb", [128, 512], BF16)
        kpb = T(es, "kpb", [128, 64], BF16)
        t1 = T(es, "t1", [128, 64], F32)
        t2 = T(es, "t2", [128, 64], F32)
        ptr = [PS(es, "ptr%d" % i, [128, 1024], BF16) for i in range(2)]
        pkv1 = PS(es, "pkv1", [128, 512], F32)
        pkv2 = PS(es, "pkv2", [128, 512], F32)
        pq = PS(es, "pq", [128, 512], F32)
        ptc = PS(es, "ptc", [128, 1024], BF16)
        ptc2 = PS(es, "ptc2", [128, 1024], BF16)
        xinR = [Res() for _ in range(3)]
        tmpR, hbR = Res(), [[Res(), Res()], [Res(), Res()]]
        hTR = [[Res(), Res()], [Res(), Res()]]
        bcAR = [Res(), Res()]
        bcBR = [Res(), Res()]
        kvnR, qnR = Res(), Res()
        rkR = [Res(), Res()]
        stR = [Res(), Res()]
        ckbR, cqbR, kpbR, t1R, t2R = Res(), Res(), Res(), Res(), Res()
        ptrR = [Res(), Res()]
        pkv1R, pkv2R, pqR, ptcR, ptc2R = Res(), Res(), Res(), Res(), Res()

        load_bc(kvn[:], kvnR, kv_norm)
        load_bc(qn[:], qnR, q_norm)
        load_bc(tmp[:], tmpR, norm_mix[0:1, :])
        for s in range(2):
            load_bc(bcA[s][:], bcAR[s], modrow(0, s, 1))
            load_bc(bcB[s][:], bcBR[s], modrow(0, s, 0))
            make_A(bcA[s][:], bcAR[s], tmp[:], tmpR)

        NB_A = NK // 128
        rk3 = [rk[0], rk[1], T(es, "rk2", [128, 128], F32)]
        rk3R = [rkR[0], rkR[1], Res()]
        stB = [T(es, "stB%d" % i, [128, 16], F32) for i in range(2)]
        stBR = [Res(), Res()]

        def A_S1(tb):
            s3 = tb % 3
            sc = 1 if tb < 2 else 0
            P.dma("sp", xin[s3][:], xk[tb * 128:(tb + 1) * 128, :], W=[xinR[s3]])
            P.dma("sp", rk3[s3][:], ropek[tb * 128:(tb + 1) * 128, :], W=[rk3R[s3]])
            return norm_s1(xin[s3][:], xinR[s3], bcA[sc][:], bcAR[sc], bcB[sc][:], bcBR[sc], tmp[:], tmpR, hb, hbR,
                           junk[:], st[tb % 2], stR[tb % 2])

        def A_S2a(tb, hi):
            s2 = tb % 2
            norm_s2(hi, hb, hbR, ptr, ptrR, hT[s2][:, 0:8, :], hT[s2][:, 8:16, :], hTR[s2])

        def A_S2b(tb):
            s2 = tb % 2
            for k in range(KC):
                P.op("pe", lambda e, k=k: e.matmul(pkv1[:, :], hT[s2][:, k, :], wdkv[:, k, 0:512],
                                                   start=(k == 0), stop=(k == KC - 1)),
                     R=[hTR[s2][0], hTR[s2][1], wdkvR], W=[pkv1R])
            for k in range(KC):
                P.op("pe", lambda e, k=k: e.matmul(pkv2[:, 0:64], hT[s2][:, k, :], wdkv[:, k, 512:576],
                                                   start=(k == 0), stop=(k == KC - 1)),
                     R=[hTR[s2][0], hTR[s2][1], wdkvR], W=[pkv2R])
            if tb < NQ // 128:
                for k in range(KC):
                    P.op("pe", lambda e, k=k: e.matmul(pq[:, :], hT[s2][:, k, :], wdq[:, k, :],
                                                       start=(k == 0), stop=(k == KC - 1)),
                         R=[hTR[s2][0], hTR[s2][1], wdqR], W=[pqR])

        def A_S3(tb):
            s2 = tb % 2
            s3 = tb % 3
            sB = stB[s2]
            sBR = stBR[s2]
            rkt = rk3[s3]
            rktR = rk3R[s3]
            rms_stats(pkv1[:, :], 512, sB, sBR, 3, pkv1R, junk[:, 0:512])
            P.op("dve", lambda e: e.scalar_tensor_tensor(out=ckb[:], in0=pkv1[:, :], scalar=sB[:, 5:6],
                                                         in1=kvn[:], op0=ALU.mult, op1=ALU.mult),
                 R=[pkv1R, sBR, kvnR], W=[ckbR])
            for c in range(4):
                P.op("pe", lambda e, c=c: e.transpose(ptc[:, c * 128:(c + 1) * 128], ckb[:, c * 128:(c + 1) * 128],
                                                      identb[:]), R=[ckbR, RC], W=[ptcR])
            kp4 = pkv2[:, 0:64].rearrange("p (g h f) -> p g h f", g=2, h=2)
            sn4 = rkt[:, 64:128].rearrange("p (g h f) -> p g h f", g=2, h=2)
            t24 = t2[:].rearrange("p (g h f) -> p g h f", g=2, h=2)
            P.op("dve", lambda e: e.tensor_tensor(out=t1[:], in0=pkv2[:, 0:64], in1=rkt[:, 0:64], op=ALU.mult),
                 R=[pkv2R, rktR], W=[t1R])
            P.op("dve", lambda e: e.tensor_tensor(out=t24[:, :, 0, :], in0=kp4[:, :, 1, :], in1=sn4[:, :, 0, :],
                                                  op=ALU.mult), R=[pkv2R, rktR], W=[t2R])
            P.op("dve", lambda e: e.tensor_tensor(out=t24[:, :, 1, :], in0=kp4[:, :, 0, :], in1=sn4[:, :, 1, :],
                                                  op=ALU.mult), R=[pkv2R, rktR], W=[t2R])
            P.op("dve", lambda e: e.tensor_tensor(out=kpb[:], in0=t1[:], in1=t2[:], op=ALU.add),
                 R=[t1R, t2R], W=[kpbR])
            P.op("pe", lambda e: e.transpose(ptc[0:64, 512:640], kpb[:, :], identb[:]), R=[kpbR, RC], W=[ptcR])
            P.op("act", lambda e: e.activation(out=ckvT[:, :, tb * 128:(tb + 1) * 128],
                                               in_=ptc[:, 0:512].rearrange("p (c t) -> p c t", c=4),
                                               func=AF.Copy), R=[ptcR], W=[ckvR])
            P.op("act", lambda e: e.activation(out=kpeT[0:64, tb * 128:(tb + 1) * 128], in_=ptc[0:64, 512:640],
                                               func=AF.Copy), R=[ptcR], W=[kpeR])
            if tb < NQ // 128:
                rms_stats(pq[:, :], 512, sB, sBR, 6, pqR, junk[:, 512:1024])
                P.op("dve", lambda e: e.scalar_tensor_tensor(out=cqb[:], in0=pq[:, :], scalar=sB[:, 8:9],
                                                             in1=qn[:], op0=ALU.mult, op1=ALU.mult),
                     R=[pqR, sBR, qnR], W=[cqbR])
                for c in range(4):
                    P.op("pe", lambda e, c=c: e.transpose(ptc2[:, c * 128:(c + 1) * 128],
                                                          cqb[:, c * 128:(c + 1) * 128], identb[:]),
                         R=[cqbR, RC], W=[ptc2R])
                P.op("dve", lambda e: e.tensor_copy(out=cqT[:, :, tb * 128:(tb + 1) * 128],
                                                    in_=ptc2[:, 0:512].rearrange("p (c t) -> p c t", c=4)),
                     R=[ptc2R], W=[cqR])

        hslot = {0: A_S1(0)}
        for tb in range(NB_A):
            if tb + 1 < NB_A:
                hslot[tb + 1] = A_S1(tb + 1)
            A_S2a(tb, hslot[tb])
            if tb > 0:
                A_S3(tb - 1)
            A_S2b(tb)
        A_S3(NB_A - 1)
        P.barrier()

    esW.close()

    with ExitStack() as es:
        KT = T(es, "KT", [128, NK], BF16)
        V = T(es, "V", [128, NK // 128, 129], BF16)
        qT = T(es, "qT", [128, NQ], BF16)
        qpeT = T(es, "qpeT", [128, NQ], BF16)
        rqc = T(es, "rqc", [64, NQ], F32)
        rqs = T(es, "rqs", [64, NQ], F32)
        wq = [T(es, "wq%d" % i, [128, 4, 192], BF16) for i in range(2)]
        wqs = [T(es, "wqs%d" % i, [128, 4, 64], BF16) for i in range(2)]
        wkv = [T(es, "wkv%d" % i, [128, 4, 256], BF16) for i in range(2)]
        Pt = [T(es, "Pt%d" % i, [128, 512], BF16) for i in range(4)]
        rec = [T(es, "rec%d" % i, [128, 4], F32) for i in range(2)]
        On = [T(es, "On%d" % i, [128, 4, 128], BF16) for i in range(2)]
        r1 = T(es, "r1", [64, 512], F32)
        r2 = T(es, "r2", [64, 512], F32)
        bank = [PS(es, "bk%d" % i, [128, 512], F32) for i in range(8)]
        bankR = [Res() for _ in range(8)]
        KTR, VR, qTR, qpeR, rqR = Res(), Res(), Res(), Res(), Res()
        wqR = [Res(), Res()]
        wqsR = [Res(), Res()]
        wkvR = [Res(), Res()]
        PtR = [Res() for _ in range(4)]
        recR = [Res(), Res()]
        OnR = [Res(), Res()]
        r1R, r2R = Res(), Res()
        Sb = [0, 1, 2, 3]
        accb = [(4, 5), (6, 7)]
        pA, pB, pC = 3, 0, 1

        P.op("dve", lambda e: e.memset(qpeT[:], 0.0), W=[qpeR])
        P.op("dve", lambda e: e.memset(V[:, :, 128:129], 1.0), W=[VR])
        P.dma("sp", rqc[:], ropeqc, W=[rqR])
        P.dma("sp", rqs[:], ropeqs, W=[rqR])

        def load_head_w(h):
            s = h % 2
            P.dma("pool", wq[s][:], w_uq[:, h * 192:(h + 1) * 192].rearrange("(k p) c -> p k c", p=128), W=[wqR[s]])
            P.dma("pool", wkv[s][:], w_ukv[:, h * 256:(h + 1) * 256].rearrange("(k p) c -> p k c", p=128), W=[wkvR[s]])

        load_head_w(0)
        evn = [0]
        adawb = [T(es, "adawB%d" % i, [128, KC, 512], BF16) for i in range(3)]
        adawbR = [Res(), Res(), Res()]

        def evac(out, in_, R, W):
            evn[0] += 1
            if evn[0] % 2 == 0:
                P.op("act", lambda e: e.activation(out=out, in_=in_, func=AF.Copy), R=R, W=W)
            else:
                P.op("dve", lambda e: e.tensor_copy(out=out, in_=in_), R=R, W=W)

        gstep = [0]
        tcount = [0]
        for h in range(16):
            s = h % 2
            if h + 1 < 16:
                load_head_w(h + 1)
            ada_now = [(al, anb, ada_load(al, anb, adawb, adawbR)) for (al, anb) in ada_todo[h * 40 // 16:(h + 1) * 40 // 16]]
            precast_chunk(0, h)
            precast_chunk(1, h)
            for (d0, s0) in ((0, 144), (16, 128), (32, 176), (48, 160)):
                P.op("act", lambda e, d0=d0, s0=s0, s=s: e.activation(out=wqs[s][:, :, d0:d0 + 16],
                                                                      in_=wq[s][:, :, s0:s0 + 16], func=AF.Copy),
                     R=[wqR[s]], W=[wqsR[s]])
            for nt in range(9):
                w = 512 if nt < 8 else 256
                b = pA if nt % 2 == 0 else pB
                for k in range(4):
                    P.op("pe", lambda e, k=k, b=b, w=w, nt=nt, s=s: e.matmul(bank[b][:, 0:w], wkv[s][:, k, 0:128],
                                                                             ckvT[:, k, nt * 512:nt * 512 + w],
                                                                             start=(k == 0), stop=(k == 3)),
                         R=[wkvR[s], ckvR], W=[bankR[b]])
                evac(KT[:, nt * 512:nt * 512 + w], bank[b][:, 0:w], [bankR[b]], [KTR])
            for vg in range(9):
                n = 4 if vg < 8 else 2
                b = pA if vg % 2 == 1 else pB
                for ci in range(n):
                    kc = vg * 4 + ci
                    for k in range(4):
                        P.op("pe", lambda e, k=k, b=b, ci=ci, kc=kc, s=s: e.matmul(
                            bank[b][:, ci * 128:(ci + 1) * 128], ckvT[:, k, kc * 128:(kc + 1) * 128],
                            wkv[s][:, k, 128:256], start=(k == 0), stop=(k == 3)),
                            R=[wkvR[s], ckvR], W=[bankR[b]])
                evac(V[:, vg * 4:vg * 4 + n, 0:128], bank[b][:, 0:n * 128].rearrange("p (c d) -> p c d", c=n),
                     [bankR[b]], [VR])
            for qt in range(5):
                c0 = qt * 512
                for k in range(4):
                    P.op("pe", lambda e, k=k, c0=c0, s=s: e.matmul(bank[pA][:, :], wq[s][:, k, 0:128],
                                                                   cqT[:, k, c0:c0 + 512], start=(k == 0), stop=(k == 3)),
                         R=[wqR[s], cqR], W=[bankR[pA]])
                evac(qT[:, c0:c0 + 512], bank[pA][:, :], [bankR[pA]], [qTR])
                for k in range(4):
                    P.op("pe", lambda e, k=k, c0=c0, s=s: e.matmul(bank[pB][0:64, :], wq[s][:, k, 128:192],
                                                                   cqT[:, k, c0:c0 + 512], start=(k == 0), stop=(k == 3)),
                         R=[wqR[s], cqR], W=[bankR[pB]])
                for k in range(4):
                    P.op("pe", lambda e, k=k, c0=c0, s=s: e.matmul(bank[pC][0:64, :], wqs[s][:, k, :],
                                                                   cqT[:, k, c0:c0 + 512], start=(k == 0), stop=(k == 3)),
                         R=[wqsR[s], cqR], W=[bankR[pC]])
                P.op("dve", lambda e, c0=c0: e.tensor_tensor(out=r1[:], in0=bank[pB][0:64, :], in1=rqc[:, c0:c0 + 512],
                                                             op=ALU.mult), R=[bankR[pB], rqR], W=[r1R])
                P.op("dve", lambda e, c0=c0: e.tensor_tensor(out=r2[:], in0=bank[pC][0:64, :], in1=rqs[:, c0:c0 + 512],
                                                             op=ALU.mult), R=[bankR[pC], rqR], W=[r2R])
                P.op("pool", lambda e, c0=c0: e.tensor_tensor(out=qpeT[0:64, c0:c0 + 512], in0=r1[:], in1=r2[:], op=ALU.add),
                     R=[r1R, r2R], W=[qpeR])
            qtiles = [(0, 256, [0, 1])] + [(256 + i * 512, 512, list(range(34))) for i in range(4)] + \
                     [(2304, 256, list(range(34)))]
            for (q0, w, kcs) in qtiles:
                ti = tcount[0]
                tcount[0] += 1
                aO, aS = accb[ti % 2]
                n = len(kcs)
                base = gstep[0]
                gstep[0] += n

                def QK(i, q0=q0, w=w, kcs=kcs, base=base):
                    kc = kcs[i]
                    sb = Sb[(base + i) % 4]
                    P.op("pe", lambda e: e.matmul(bank[sb][:, 0:w], KT[:, kc * 128:(kc + 1) * 128], qT[:, q0:q0 + w],
                                                  start=True, stop=False), R=[KTR, qTR], W=[bankR[sb]])
                    P.op("pe", lambda e: e.matmul(bank[sb][:, 0:w], kpeT[:, kc * 128:(kc + 1) * 128],
                                                  qpeT[:, q0:q0 + w], start=False, stop=True),
                         R=[kpeR, qpeR], W=[bankR[sb]])

                for i0 in range(min(3, n)):
                    QK(i0)
                for i in range(n):
                    kc = kcs[i]
                    sb = Sb[(base + i) % 4]
                    ps = (base + i) % 4
                    P.op("act", lambda e, sb=sb, ps=ps, w=w: e.activation(out=Pt[ps][:, 0:w], in_=bank[sb][:, 0:w],
                                                                         func=AF.Exp, scale=SC_MLA),
                         R=[bankR[sb]], W=[PtR[ps]])
                    for qb in range(w // 128):
                        bk = 4 + qb
                        off = 0
                        P.op("pe", lambda e, kc=kc, ps=ps, i=i, n=n, bk=bk, off=off, qb=qb: e.matmul(
                            bank[bk][:, off:off + 129], Pt[ps][:, qb * 128:(qb + 1) * 128], V[:, kc, :],
                            start=(i == 0), stop=(i == n - 1)), R=[VR, PtR[ps]], W=[bankR[bk]])
                    if i + 3 < n:
                        QK(i + 3)
                sl = ti % 2
                nqb = w // 128
                for qb in range(nqb):
                    bk = 4 + qb
                    off = 0
                    P.op("dve", lambda e, sl=sl, bk=bk, off=off, qb=qb: e.reciprocal(
                        out=rec[sl][:, qb:qb + 1], in_=bank[bk][:, off + 128:off + 129]), R=[bankR[bk]], W=[recR[sl]])
                    P.op("dve", lambda e, sl=sl, bk=bk, off=off, qb=qb: e.tensor_scalar_mul(
                        out=On[sl][:, qb, :], in0=bank[bk][:, off:off + 128], scalar1=rec[sl][:, qb:qb + 1]),
                        R=[bankR[bk], recR[sl]], W=[OnR[sl]])
                P.dma("sp", OT[q0:q0 + w, h * 128:(h + 1) * 128].rearrange("(qb p) d -> p qb d", p=128),
                      On[sl][:, 0:nqb, :], R=[OnR[sl]], W=[Res()])
            for (al, anb, asl) in ada_now:
                ada_compute(al, anb, asl, adawb, adawbR, bank[pA], bankR[pA])
        P.barrier()
    es0.close()

    def mixer_out(l, w_o, nblk, ot_src, x_src, x_dst, x_dstR, x_srcR, has_ctx):
        with ExitStack() as es:
            wo = T(es, "wo", [128, KC, D], BF16)
            xb = [T(es, "xb%d" % i, [128, D], F32) for i in range(3)]
            xo = [T(es, "xo%d" % i, [128, D], F32) for i in range(2)]
            tp = [T(es, "tp%d" % i, [128, 512], F32) for i in range(2)]
            G = [T(es, "G%d" % i, [128, D], F32) for i in range(2)]
            py = [PS(es, "py%d" % i, [128, 512], F32) for i in range(4)]
            woR = [Res() for _ in range(4)]
            xbR = [Res() for _ in range(3)]
            xoR = [[Res() for _ in range(4)] for _ in range(2)]
            tpR = [Res(), Res()]
            GR = [Res(), Res()]
            pyR = [Res() for _ in range(4)]
            if ot_src is None:
                ob = [T(es, "ob%d" % i, [128, D], BF16) for i in range(3)]
                ot = [T(es, "ot%d" % i, [128, KC, 128], BF16) for i in range(2)]
                ptr = [PS(es, "mptr%d" % i, [128, 1024], BF16) for i in range(2)]
                obR = [Res() for _ in range(3)]
                otR = [[Res(), Res()], [Res(), Res()]]
                ptrR = [Res(), Res()]
            load_bc(G[0][:], GR[0], modrow(l, 0, 2))
            if has_ctx:
                load_bc(G[1][:], GR[1], modrow(l, 1, 2))
            for nb in range(4):
                P.dma("pool", wo[:, :, nb * 512:(nb + 1) * 512],
                      w_o[:, nb * 512:(nb + 1) * 512].rearrange("(k p) c -> p k c", p=128), W=[woR[nb]])

            def loads(blk):
                if ot_src is None:
                    P.dma("sp", ob[blk % 3][:], OT[blk * 128:(blk + 1) * 128, :], R=[OTR], W=[obR[blk % 3]])
                P.dma("sp", xb[blk % 3][:], x_src(blk), R=[x_srcR] if x_srcR is not None else [], W=[xbR[blk % 3]])

            def transp(blk):
                o_ = ob[blk % 3]
                s2 = blk % 2
                for k in range(KC):
                    pi = k // 8
                    P.op("pe", lambda e, k=k, pi=pi: e.transpose(ptr[pi][:, (k % 8) * 128:(k % 8 + 1) * 128],
                                                                  o_[:, k * 128:(k + 1) * 128], identb[:]),
                         R=[obR[blk % 3], RC], W=[ptrR[pi]])
                P.op("act", lambda e: e.activation(out=ot[s2][:, 0:8, :], in_=ptr[0][:].rearrange("p (k t) -> p k t", k=8),
                                                   func=AF.Copy), R=[ptrR[0]], W=[otR[s2][0]])
                P.op("dve", lambda e: e.tensor_copy(out=ot[s2][:, 8:16, :], in_=ptr[1][:].rearrange("p (k t) -> p k t", k=8)),
                     R=[ptrR[1]], W=[otR[s2][1]])

            loads(0)
            if nblk > 1:
                loads(1)
            if ot_src is None:
                transp(0)
            it = 0
            for blk in range(nblk):
                if blk + 2 < nblk:
                    loads(blk + 2)
                if ot_src is None and blk + 1 < nblk:
                    transp(blk + 1)
                s3 = blk % 3
                s2 = blk % 2
                sc = 1 if (has_ctx and blk < 2) else 0
                for nb in range(4):
                    pb = it % 4
                    t2 = it % 2
                    it += 1
                    for k in range(KC):
                        if ot_src is None:
                            lhs = ot[s2][:, k, :]
                            lR = otR[s2]
                        else:
                            lhs = ot_src[0][:, k, blk * 128:(blk + 1) * 128]
                            lR = [ot_src[1]]
                        P.op("pe", lambda e, k=k, lhs=lhs: e.matmul(py[pb][:, :], lhs, wo[:, k, nb * 512:(nb + 1) * 512],
                                                                    start=(k == 0), stop=(k == KC - 1)),
                             R=lR + [woR[nb]], W=[pyR[pb]])
                    P.op("dve", lambda e: e.tensor_tensor(out=tp[t2][:], in0=py[pb][:, :],
                                                          in1=G[sc][:, nb * 512:(nb + 1) * 512], op=ALU.mult),
                         R=[pyR[pb], GR[sc]], W=[tpR[t2]])
                    P.op("pool", lambda e: e.tensor_tensor(out=xo[s2][:, nb * 512:(nb + 1) * 512], in0=tp[t2][:],
                                                           in1=xb[s3][:, nb * 512:(nb + 1) * 512], op=ALU.add),
                         R=[tpR[t2], xbR[s3]], W=[xoR[s2][nb]])
                P.dma("sp", x_dst(blk), xo[s2][:], R=xoR[s2], W=[Res()])
            P.barrier()

    mixer_out(0, w_o0, NQ // 128, None,
              lambda blk: xk[blk * 128:(blk + 1) * 128, :],
              lambda blk: x1[blk * 128:(blk + 1) * 128, :], x1R, None, True)

    def ffn(l, ntile, x_src, x_srcR, x_dst, x_dstR, has_ctx, final):
        with ExitStack() as es:
            h2T = T(es, "h2T", [128, KC, 512], BF16)
            actT = T(es, "actT", [128, FC, 512], BF16)
            wg = [T(es, "wg%d" % i, [128, KC, 256], BF16) for i in range(2)]
            wu = [T(es, "wu%d" % i, [128, KC, 256], BF16) for i in range(2)]
            wd = [T(es, "wd%d" % i, [128, 4, 512], BF16) for i in range(3)]
            xr = [T(es, "xr%d" % i, [128, D], F32) for i in range(4)]
            bcA = T(es, "fA", [128, D], F32)
            bcB = T(es, "fB", [128, D], F32)
            bcG = T(es, "fG", [128, D], F32)
            bcGc = T(es, "fGc", [128, D], F32) if (has_ctx or final) else None
            tmp = T(es, "ftmp", [128, D], F32)
            hb = [T(es, "fhb%d" % i, [128, D], BF16) for i in range(2)]
            junk = T(es, "fjunk", [128, D], BF16)
            sg = [T(es, "sg%d" % i, [128, 512], F32) for i in range(2)]
            tq = [T(es, "tq%d" % i, [128, 512], F32) for i in range(2)]
            st = [T(es, "fst%d" % i, [128, 16], F32) for i in range(2)]
            stF = [T(es, "fstF%d" % i, [128, 16], F32) for i in range(4)]
            stFR = [Res() for _ in range(4)]
            pd = [PS(es, "pd%d" % i, [128, 512], F32) for i in range(4)]
            ptr = [PS(es, "fptr%d" % i, [128, 1024], BF16) for i in range(2)]
            h2R, actR = [Res(), Res()], Res()
            wgR = [Res(), Res()]
            wuR = [Res(), Res()]
            wdR = [Res() for _ in range(3)]
            xrR = [Res() for _ in range(4)]
            AR, BR, GR, GcR, tmpR, hbR = Res(), Res(), Res(), Res(), Res(), [[Res(), Res()], [Res(), Res()]]
            sgR = [Res(), Res()]
            tqR = [Res(), Res()]
            stR = [Res(), Res()]
            pdR = [Res() for _ in range(4)]
            ptrR = [Res(), Res()]

            load_bc(tmp[:], tmpR, norm_ffn[l:l + 1, :])
            load_bc(bcA[:], AR, modrow(l, 0, 4))
            make_A(bcA[:], AR, tmp[:], tmpR)
            load_bc(bcB[:], BR, modrow(l, 0, 3))
            load_bc(bcG[:], GR, modrow(l, 0, 5))
            if has_ctx:
                load_bc(xr[3][:], xrR[3], modrow(l, 1, 4))
                make_A(xr[3][:], xrR[3], tmp[:], tmpR)
                load_bc(xr[2][:], xrR[2], modrow(l, 1, 3))
                load_bc(bcGc[:], GcR, modrow(l, 1, 5))
            if final:
                load_bc(bcGc[:], GcR, norm_final)

            wgv = wgb[l].rearrange("(k p) c -> p k c", p=128)
            wuv = wub[l].rearrange("(k p) c -> p k c", p=128)
            wdv = wdb[l].rearrange("(j p) c -> p j c", p=128)
            wdn = [0]
            for t in range(ntile):
                def F_S1(bi, t=t):
                    blk = t * 4 + bi
                    ctxb = has_ctx and blk < 2
                    if t == 0:
                        P.dma("sp", xr[bi][:], x_src(blk), R=[x_srcR], W=[xrR[bi]])
                    A_, AR_ = (xr[3][:], xrR[3]) if ctxb else (bcA[:], AR)
                    B_, BR_ = (xr[2][:], xrR[2]) if ctxb else (bcB[:], BR)
                    return norm_s1(xr[bi][:], xrR[bi], A_, AR_, B_, BR_, tmp[:], tmpR, hb, hbR, junk[:],
                                   st[bi % 2], stR[bi % 2])

                hs = {0: F_S1(0)}
                for bi in range(4):
                    if bi + 1 < 4:
                        hs[bi + 1] = F_S1(bi + 1)
                    norm_s2(hs[bi], hb, hbR, ptr, ptrR,
                            h2T[:, 0:8, bi * 128:(bi + 1) * 128], h2T[:, 8:16, bi * 128:(bi + 1) * 128], h2R)
                for jj in range(FC // 2):
                    s = jj % 2
                    P.dma("sp", wg[s][:], wgv[:, :, jj * 256:(jj + 1) * 256], W=[wgR[s]])
                    P.dma("sp", wu[s][:], wuv[:, :, jj * 256:(jj + 1) * 256], W=[wuR[s]])
                    for jh in range(2):
                        j = 2 * jj + jh
                        pg = (2 * j) % 4
                        pu = (2 * j + 1) % 4
                        for k in range(KC):
                            P.op("pe", lambda e, k=k, s=s, jh=jh, pg=pg: e.matmul(
                                pd[pg][:, :], wg[s][:, k, jh * 128:(jh + 1) * 128], h2T[:, k, :],
                                start=(k == 0), stop=(k == KC - 1)), R=[wgR[s], h2R[0], h2R[1]], W=[pdR[pg]])
                        for k in range(KC):
                            P.op("pe", lambda e, k=k, s=s, jh=jh, pu=pu: e.matmul(
                                pd[pu][:, :], wu[s][:, k, jh * 128:(jh + 1) * 128], h2T[:, k, :],
                                start=(k == 0), stop=(k == KC - 1)), R=[wuR[s], h2R[0], h2R[1]], W=[pdR[pu]])
                        s2 = j % 2
                        P.op("act", lambda e, s2=s2, pg=pg: e.activation(out=sg[s2][:], in_=pd[pg][:, :], func=AF.Silu),
                             R=[pdR[pg]], W=[sgR[s2]])
                        P.op("dve", lambda e, s2=s2, pu=pu, j=j: e.tensor_tensor(out=actT[:, j, :], in0=pd[pu][:, :],
                                                                                 in1=sg[s2][:], op=ALU.mult),
                             R=[pdR[pu], sgR[s2]], W=[actR])
                for nb in range(4):
                    for jg in range(FC // 4):
                        ws = wdn[0] % 3
                        wdn[0] += 1
                        P.dma("sp", wd[ws][:], wdv[:, jg * 4:(jg + 1) * 4, nb * 512:(nb + 1) * 512],
                              W=[wdR[ws]])
                        for bi in range(4):
                            for jl in range(4):
                                j = jg * 4 + jl
                                P.op("pe", lambda e, j=j, jl=jl, bi=bi, ws=ws: e.matmul(
                                    pd[bi][:, :], actT[:, j, bi * 128:(bi + 1) * 128], wd[ws][:, jl, :],
                                    start=(j == 0), stop=(j == FC - 1)), R=[actR, wdR[ws]], W=[pdR[bi]])
                    for bi in range(4):
                        blk = t * 4 + bi
                        ctxb = has_ctx and blk < 2
                        G_, GR_ = (bcGc[:], GcR) if ctxb else (bcG[:], GR)
                        s2 = bi % 2
                        P.op("dve", lambda e, s2=s2, bi=bi, nb=nb, G_=G_: e.tensor_tensor(
                            out=tq[s2][:], in0=pd[bi][:, :], in1=G_[:, nb * 512:(nb + 1) * 512], op=ALU.mult),
                            R=[pdR[bi], GR_], W=[tqR[s2]])
                        P.op("pool", lambda e, s2=s2, bi=bi, nb=nb: e.tensor_tensor(
                            out=xr[bi][:, nb * 512:(nb + 1) * 512], in0=tq[s2][:], in1=xr[bi][:, nb * 512:(nb + 1) * 512],
                            op=ALU.add), R=[tqR[s2]], W=[xrR[bi]])
                        if nb == 3 and not final:
                            P.dma("sp", x_dst(blk), xr[bi][:], R=[xrR[bi]], W=[Res()])
                    if nb == 3 and final:
                        for bi in range(4):
                            rms_stats(xr[bi][:], D, stF[bi], stFR[bi], 4, xrR[bi], junk[:])
                        for bi in range(4):
                            P.op("dve", lambda e, bi=bi: e.scalar_tensor_tensor(
                                out=xr[bi][:], in0=xr[bi][:], scalar=stF[bi][:, 6:7], in1=bcGc[:],
                                op0=ALU.mult, op1=ALU.mult), R=[stFR[bi], GcR], W=[xrR[bi]])
                            P.dma("sp", x_dst(t * 4 + bi), xr[bi][:], R=[xrR[bi]], W=[Res()])
                    if nb == 3 and t + 1 < ntile:
                        for bi in range(4):
                            P.dma("sp", xr[bi][:], x_src(t * 4 + bi + 4), R=[x_srcR], W=[xrR[bi]])
            P.barrier()

    ffn(0, NQ // 512, lambda blk: x1[blk * 128:(blk + 1) * 128, :], x1R,
        lambda blk: x2[blk * 128:(blk + 1) * 128, :], x2R, True, False)

    with ExitStack() as es:
        hTa = T(es, "hTa", [128, KC, NQ], BF16)
        hTaR = [Res(), Res()]
        with ExitStack() as es2:
            xin = [T(es2, "dxin%d" % i, [128, D], F32) for i in range(3)]
            junk = T(es2, "djunk", [128, D], BF16)
            tmp = T(es2, "dtmp", [128, D], F32)
            hb = [T(es2, "dhb%d" % i, [128, D], BF16) for i in range(2)]
            bcA = [T(es2, "dA%d" % i, [128, D], F32) for i in range(2)]
            bcB = [T(es2, "dB%d" % i, [128, D], F32) for i in range(2)]
            st = [T(es2, "dst%d" % i, [128, 16], F32) for i in range(2)]
            ptr = [PS(es2, "dptr%d" % i, [128, 1024], BF16) for i in range(2)]
            xinR = [Res() for _ in range(3)]
            tmpR, hbR = Res(), [[Res(), Res()], [Res(), Res()]]
            bcAR = [Res(), Res()]
            bcBR = [Res(), Res()]
            stR = [Res(), Res()]
            ptrR = [Res(), Res()]
            load_bc(tmp[:], tmpR, norm_mix[1:2, :])
            for s in range(2):
                load_bc(bcA[s][:], bcAR[s], modrow(1, s, 1))
                load_bc(bcB[s][:], bcBR[s], modrow(1, s, 0))
                make_A(bcA[s][:], bcAR[s], tmp[:], tmpR)
            def D_S1(tb):
                s3 = tb % 3
                sc = 1 if tb < 2 else 0
                P.dma("sp", xin[s3][:], x2[tb * 128:(tb + 1) * 128, :], R=[x2R], W=[xinR[s3]])
                return norm_s1(xin[s3][:], xinR[s3], bcA[sc][:], bcAR[sc], bcB[sc][:], bcBR[sc], tmp[:], tmpR, hb, hbR,
                               junk[:], st[tb % 2], stR[tb % 2])

            hs = {0: D_S1(0)}
            for tb in range(NQ // 128):
                if tb + 1 < NQ // 128:
                    hs[tb + 1] = D_S1(tb + 1)
                norm_s2(hs[tb], hb, hbR, ptr, ptrR,
                        hTa[:, 0:8, tb * 128:(tb + 1) * 128], hTa[:, 8:16, tb * 128:(tb + 1) * 128], hTaR)
            P.barrier()
        with ExitStack() as es2:
            wqb = [T(es2, "wqb%d" % i, [128, KC, 128], BF16) for i in range(2)]
            wvb = [T(es2, "wvb%d" % i, [128, KC, 512], BF16) for i in range(2)]
            qo = [T(es2, "qo%d" % i, [128, 512], BF16) for i in range(3)]
            qv = [T(es2, "qv%d" % i, [128, 4, 144], BF16) for i in range(3)]
            pp = [PS(es2, "pp%d" % i, [128, 512], F32) for i in range(3)]
            wqbR = [Res(), Res()]
            wvbR = [Res(), Res()]
            qoR = [Res() for _ in range(3)]
            ppR = [Res() for _ in range(3)]
            for i in range(3):
                P.op("pool", lambda e, i=i: e.memset(qv[i][:], 1.0), W=[qoR[i]])
            it = 0
            wn = 0
            wqv = w_qkv.rearrange("(k p) c -> p k c", p=128)
            for part in range(2):
                for c in range(16):
                    ws = wn % 2
                    wn += 1
                    col = part * D + c * 128
                    P.dma("pool", wqb[ws][:], wqv[:, :, col:col + 128], W=[wqbR[ws]])
                    tiles = [(256 + i * 512, i * 512) for i in range(4)] if part == 0 else [(i * 512, i * 512) for i in range(5)]
                    for (src0, dst0) in tiles:
                        s = it % 3
                        it += 1
                        for k in range(KC):
                            P.op("pe", lambda e, k=k, s=s, ws=ws, src0=src0: e.matmul(
                                pp[s][:, :], wqb[ws][:, k, :], hTa[:, k, src0:src0 + 512],
                                start=(k == 0), stop=(k == KC - 1)), R=[wqbR[ws], hTaR[0], hTaR[1]], W=[ppR[s]])
                        if part == 0:
                            P.op("act", lambda e, s=s: e.activation(out=qo[s][:], in_=pp[s][:, :], func=AF.Copy,
                                                                    scale=SC_NA), R=[ppR[s]], W=[qoR[s]])
                            P.dma("sp", q1T[c * 128:(c + 1) * 128, dst0:dst0 + 512], qo[s][:], R=[qoR[s]], W=[Res()])
                        else:
                            P.op("dve", lambda e, s=s: e.tensor_copy(out=qo[s][:], in_=pp[s][:, :]),
                                 R=[ppR[s]], W=[qoR[s]])
                            P.dma("sp", k1T[c * 128:(c + 1) * 128, dst0:dst0 + 512], qo[s][:], R=[qoR[s]], W=[Res()])
            for nb in range(4):
                ws = nb % 2
                P.dma("pool", wvb[ws][:], wqv[:, :, 2 * D + nb * 512:2 * D + (nb + 1) * 512], W=[wvbR[ws]])
                for tb in range(NQ // 128):
                    s = it % 3
                    it += 1
                    for k in range(KC):
                        P.op("pe", lambda e, k=k, s=s, ws=ws, tb=tb: e.matmul(
                            pp[s][:, :], hTa[:, k, tb * 128:(tb + 1) * 128], wvb[ws][:, k, :],
                            start=(k == 0), stop=(k == KC - 1)), R=[wvbR[ws], hTaR[0], hTaR[1]], W=[ppR[s]])
                    ppv = pp[s][:, :].rearrange("p (h d) -> p h d", h=4)
                    if it % 2 == 0:
                        P.op("act", lambda e, s=s, ppv=ppv: e.activation(out=qv[s][:, :, 0:128], in_=ppv, func=AF.Copy),
                             R=[ppR[s]], W=[qoR[s]])
                    else:
                        P.op("dve", lambda e, s=s, ppv=ppv: e.tensor_copy(out=qv[s][:, :, 0:128], in_=ppv),
                             R=[ppR[s]], W=[qoR[s]])
                    P.dma("sp", v1[4 * nb:4 * nb + 4, :, tb, :].rearrange("h p d -> p h d"),
                          qv[s][:], R=[qoR[s]], W=[Res()])
            P.barrier()

    with ExitStack() as es:
        KTh = [T(es, "KTh%d" % i, [128, NQ], BF16) for i in range(2)]
        Vh = [T(es, "Vh%d" % i, [128, NQ // 128, 144], BF16) for i in range(2)]
        QTh = [T(es, "QTh%d" % i, [128, NOWN], BF16) for i in range(2)]
        tab = [[T(es, "tab%d_%d" % (i, v), [128, 6, 256], BF16) for v in range(3)] for i in range(2)]
        Pt = [T(es, "ePt%d" % i, [128, 512], BF16) for i in range(4)]
        rec = [T(es, "erec%d" % i, [128, 2], F32) for i in range(6)]
        On = [T(es, "eOn%d" % i, [128, 2, 128], BF16) for i in range(6)]
        bank = [PS(es, "ebk%d" % i, [128, 512], F32) for i in range(8)]
        bankR = [Res() for _ in range(8)]
        KThR = [Res(), Res()]
        VhR = [Res(), Res()]
        QThR = [Res(), Res()]
        tabR = [Res(), Res()]
        PtR = [Res() for _ in range(4)]
        recR = [Res() for _ in range(6)]
        OnR = [Res() for _ in range(6)]

        def load_head(h):
            s = h % 2
            P.dma("sp", KTh[s][:], k1T[h * 128:(h + 1) * 128, :], R=[k1R], W=[KThR[s]])
            P.dma("sp", Vh[s][:], v1[h], R=[v1R], W=[VhR[s]])
            P.op("pool", lambda e: e.memset(Vh[s][:, :, 128:129], 1.0), W=[VhR[s]])
            P.dma("sp", QTh[s][:], q1T[h * 128:(h + 1) * 128, :], R=[q1R], W=[QThR[s]])
            for v in range(3):
                P.dma("pool", tab[s][v][:], natab[v, h].rearrange("(c p) q -> p c q", p=128), W=[tabR[s]])

        load_head(0)
        gstep = 0
        tcount = 0
        for h in range(16):
            s = h % 2
            if h + 1 < 16:
                load_head(h + 1)
            for g in range(8):
                tv = 0 if g == 0 else (2 if g == 7 else 1)
                chunks = [(0, None), (128, None)]
                for m in range(6):
                    pos = (4 * g - 4 + 2 * m) % 36
                    chunks.append((256 + pos * 64, m))
                ab = (4, 5) if tcount % 2 == 0 else (6, 7)
                sl = tcount % 6
                tcount += 1
                n = 4
                base = gstep
                gstep += n
                q_ap = QTh[s][:, g * 256:(g + 1) * 256]

                def QK(p, chunks=chunks, base=base, q_ap=q_ap, s=s, tv=tv):
                    sb = (base + p) % 4
                    for hf in range(2):
                        tok0, m = chunks[2 * p + hf]
                        reg = bank[sb][:, hf * 256:(hf + 1) * 256]
                        P.op("pe", lambda e: e.matmul(reg, KTh[s][:, tok0:tok0 + 128], q_ap,
                                                      start=True, stop=(m is None)), R=[KThR[s], QThR[s]], W=[bankR[sb]])
                        if m is not None:
                            P.op("pe", lambda e: e.matmul(reg, identb[:], tab[s][tv][:, m, :],
                                                          start=False, stop=True), R=[RC, tabR[s]], W=[bankR[sb]])

                QK(0)
                QK(1)
                QK(2)
                for p in range(n):
                    sb = (base + p) % 4
                    ps = (base + p) % 4
                    P.op("act", lambda e, sb=sb, ps=ps: e.activation(out=Pt[ps][:], in_=bank[sb][:, :], func=AF.Exp),
                         R=[bankR[sb]], W=[PtR[ps]])
                    for hf in range(2):
                        tok0, m = chunks[2 * p + hf]
                        first = (p == 0 and hf == 0)
                        last = (p == n - 1 and hf == 1)
                        for qb in range(2):
                            P.op("pe", lambda e, tok0=tok0, ps=ps, hf=hf, qb=qb, first=first, last=last: e.matmul(
                                bank[ab[qb]][:, 0:129], Pt[ps][:, hf * 256 + qb * 128:hf * 256 + (qb + 1) * 128],
                                Vh[s][:, tok0 // 128, 0:129], start=first, stop=last),
                                R=[VhR[s], PtR[ps]], W=[bankR[ab[qb]]])
                    if p + 3 < n:
                        QK(p + 3)
                for qb in range(2):
                    bk = ab[qb]
                    P.op("dve", lambda e, sl=sl, bk=bk, qb=qb: e.reciprocal(out=rec[sl][:, qb:qb + 1],
                                                                            in_=bank[bk][:, 128:129]),
                         R=[bankR[bk]], W=[recR[sl]])
                    P.op("dve", lambda e, sl=sl, bk=bk, qb=qb: e.tensor_scalar_mul(
                        out=On[sl][:, qb, :], in0=bank[bk][:, 0:128], scalar1=rec[sl][:, qb:qb + 1]),
                        R=[bankR[bk], recR[sl]], W=[OnR[sl]])
                P.dma("sp", OT[g * 256:(g + 1) * 256, h * 128:(h + 1) * 128].rearrange("(qb p) d -> p qb d", p=128),
                      On[sl][:], R=[OnR[sl]], W=[Res()])
        P.barrier()

    mixer_out(1, w_o1, NOWN // 128, None,
              lambda blk: x2[(blk + 2) * 128:(blk + 3) * 128, :],
              lambda blk: x3[blk * 128:(blk + 1) * 128, :], x3R, x2R, False)

    ffn(1, NOWN // 512, lambda blk: x3[blk * 128:(blk + 1) * 128, :], x3R,
        lambda blk: yout[blk * 128:(blk + 1) * 128, :], Res(), False, True)

    P.barrier()
    top.close()
    return nc, P.nins


def _rope_tables(tok_idx, is_ctx):
    t = np.asarray(tok_idx)
    row = (t // 64).astype(np.float32)
    col = (t % 64).astype(np.float32)
    inv = (np.float32(10000.0) ** (-(np.arange(16, dtype=np.float32)) / np.float32(16))).astype(np.float32)
    ar = row[:, None] * inv[None, :]
    ac = col[:, None] * inv[None, :]
    cr, sr, cc, scn = np.cos(ar), np.sin(ar), np.cos(ac), np.sin(ac)
    cos4 = np.concatenate([cr, cr, cc, cc], axis=1).astype(np.float32)
    sin4 = np.concatenate([-sr, sr, -scn, scn], axis=1).astype(np.float32)
    cos4[is_ctx] = 1.0
    sin4[is_ctx] = 0.0
    return cos4, sin4


def _na_table(rel_bias, half, g):
    own0 = 32 * half
    j = np.arange(12)
    pos = (4 * g - 4 + j) % 36
    krow = np.where(pos < 32, own0 + pos, (32 if half == 0 else 28) + (pos - 32))
    i = np.arange(4)
    r = own0 + 4 * g + i
    rs = np.clip(r - 4, 0, 56)
    vrow = (krow[:, None] >= rs[None, :]) & (krow[:, None] < rs[None, :] + 8)
    dr = np.clip(krow[:, None] - r[None, :] + 7, 0, 14)
    kc = np.arange(64)
    qc = np.arange(64)
    cs = np.clip(qc - 8, 0, 48)
    vcol = (kc[:, None] >= cs[None, :]) & (kc[:, None] < cs[None, :] + 16)
    dc = np.clip(kc[:, None] - qc[None, :] + 15, 0, 30)
    vals = rel_bias[:, dr[:, None, :, None], dc[None, :, None, :]]
    valid = vrow[:, None, :, None] & vcol[None, :, None, :]
    out = np.where(valid[None], vals, np.float32(NEG)).astype(np.float32)
    return out.reshape(16, 768, 256)


_CACHE = {}


def kernel(x, c, ctx, c_ctx, ada_w, ada_b, norm_mix, norm_ffn, norm_final,
           mla_w_dq, mla_q_norm, mla_w_uq, mla_w_dkv, mla_kv_norm, mla_w_ukv, mla_w_o,
           na_w_qkv, na_rel_bias, na_w_o, ffn_w_gate, ffn_w_up, ffn_w_down):
    f = lambda a: np.ascontiguousarray(np.asarray(a, dtype=np.float32))
    x, c, ctx, c_ctx = f(x), f(c), f(ctx), f(c_ctx)
    if "nc" not in _CACHE:
        _CACHE["nc"] = build_program()[0]
    nc = _CACHE["nc"]
    shared = {
        "ident": np.eye(128, dtype=np.float32),
        "ada_w": f(ada_w), "ada_b": f(ada_b), "norm_mix": f(norm_mix), "norm_ffn": f(norm_ffn),
        "norm_final": f(norm_final).reshape(1, D),
        "mla_w_dq": f(mla_w_dq)[0], "mla_q_norm": f(mla_q_norm).reshape(1, 512), "mla_w_uq": f(mla_w_uq)[0],
        "mla_w_dkv": f(mla_w_dkv)[0], "mla_kv_norm": f(mla_kv_norm).reshape(1, 512), "mla_w_ukv": f(mla_w_ukv)[0],
        "mla_w_o": f(mla_w_o)[0], "na_w_qkv": f(na_w_qkv)[0], "na_w_o": f(na_w_o)[0],
        "ffn_w_gate": f(ffn_w_gate), "ffn_w_up": f(ffn_w_up), "ffn_w_down": f(ffn_w_down),
    }
    rb = f(na_rel_bias)[0]
    tabs = {}
    for half in range(2):
        t0 = _na_table(rb, half, 0)
        t1 = _na_table(rb, half, 1)
        t7 = _na_table(rb, half, 7)
        tabs[half] = np.ascontiguousarray(np.stack([t0, t1, t7], axis=0))
    in_maps = []
    orders = []
    for core in range(8):
        b, half = core // 2, core % 2
        if half == 0:
            own = np.arange(0, 2048)
            halo = np.arange(2048, 2304)
            rest = np.arange(2304, 4096)
        else:
            own = np.arange(2048, 4096)
            halo = np.arange(1792, 2048)
            rest = np.arange(0, 1792)
        lat = np.concatenate([own, halo, rest])
        orders.append(own)
        xk = np.ascontiguousarray(np.concatenate([ctx[b], x[b][lat]], axis=0))
        tok = np.concatenate([np.zeros(256, dtype=np.int64), lat])
        is_ctx = np.zeros(NK, dtype=bool)
        is_ctx[:256] = True
        cos4, sin4 = _rope_tables(tok, is_ctx)
        m = dict(shared)
        m["xk"] = xk
        m["cvec"] = np.ascontiguousarray(np.stack([c[b], c_ctx], axis=0))
        m["ropek"] = np.ascontiguousarray(np.concatenate([cos4, sin4], axis=1))
        m["ropeqc"] = np.ascontiguousarray(cos4[:NQ].T)
        m["ropeqs"] = np.ascontiguousarray(sin4[:NQ].T)
        m["natab"] = tabs[half]
        in_maps.append(m)
    res = run_bass_kernel_spmd(nc, in_maps, core_ids=list(range(8)))
    out = np.empty((4, 4096, D), dtype=np.float32)
    for core in range(8):
        b = core // 2
        out[b, orders[core]] = np.asarray(res.results[core]["y"], dtype=np.float32)
    return out
```
